# Optimizing a Trainium2 kernel written in Bass

```python
import math
import jax, jax.numpy as jnp
from jax import lax
import numpy as np

D_MODEL = 1024
BATCH = 8
SEQ = 2048
DEPTH = 2

CTX_LEN = 256
GRID_W = 64
D_MIX = D_MODEL
EPS = 1e-6

A_HEADS = 4
A_DK = 128
A_DV = 128
A_QK_WIDTH = A_HEADS * A_DK
A_WIDTH = A_HEADS * A_DV
A_CONV = 5
A_CHUNK = 64

B_HEADS = 4
B_NOPE = 64
B_ROPE = 32
B_V = 64
B_Q_LORA = 192
B_KV_LORA = 128
B_WIDTH = B_HEADS * B_V
B_QBLOCK = 128
ROPE_THETA = 10000.0

C_WIDTH = D_MIX - A_WIDTH - B_WIDTH
C_CONV = 3

IN_SIZES = (A_QK_WIDTH, A_QK_WIDTH, A_WIDTH, A_WIDTH, 2 * A_HEADS, 2 * A_HEADS,
            B_Q_LORA, B_KV_LORA + B_ROPE, B_WIDTH,
            C_WIDTH, C_WIDTH, C_WIDTH, C_WIDTH)
N_IN = sum(IN_SIZES)

kernel_name = "hybrid_parallel_heads_dit_block"

F32 = jnp.float32


def rmsnorm(x, g):
    xf = x.astype(F32)
    y = xf * lax.rsqrt(jnp.mean(xf * xf, axis=-1, keepdims=True) + EPS)
    return (y * g.astype(F32)).astype(x.dtype)


def l2norm(x):
    return x * lax.rsqrt(jnp.sum(x * x, axis=-1, keepdims=True) + EPS)


def dwconv_centred(x, w):
    pad = w.shape[0] // 2
    return lax.conv_general_dilated(
        x, w[:, None, :].astype(x.dtype), window_strides=(1,), padding=[(pad, pad)],
        dimension_numbers=("NWC", "WIO", "NWC"), feature_group_count=x.shape[-1])


def split_in(u):
    idx = [int(i) for i in np.cumsum(IN_SIZES)[:-1]]
    return jnp.split(u, idx, axis=-1)


def gdn_prepare(q, k, v, b_raw, a_raw, conv_w, a_log, dt_bias):
    bn, t, _ = q.shape
    qkv = jax.nn.silu(dwconv_centred(jnp.concatenate([q, k, v], axis=-1), conv_w)).astype(F32)
    q, k, v = jnp.split(qkv, [A_QK_WIDTH, 2 * A_QK_WIDTH], axis=-1)
    q = l2norm(q.reshape(bn, t, A_HEADS, A_DK)) * (A_DK ** -0.5)
    k = l2norm(k.reshape(bn, t, A_HEADS, A_DK))
    v = v.reshape(bn, t, A_HEADS, A_DV)
    beta = jax.nn.sigmoid(b_raw.astype(F32).reshape(bn, t, 2, A_HEADS))
    g = -jnp.exp(a_log.astype(F32)) * jax.nn.softplus(
        a_raw.astype(F32).reshape(bn, t, 2, A_HEADS) + dt_bias.astype(F32))
    return q, k, v, g, beta


def gated_delta_chunked(q, k, v, g, beta, s0):
    bn, t, h, _ = q.shape
    n = t // A_CHUNK

    def to_chunks(a):
        a = jnp.moveaxis(a, 2, 1)
        return a.reshape(bn, h, n, A_CHUNK, *a.shape[3:])

    q, k, v, g, beta = (to_chunks(a) for a in (q, k, v, g, beta))
    gc = jnp.cumsum(g, axis=-1)
    causal = jnp.tril(jnp.ones((A_CHUNK, A_CHUNK), bool))
    strict = jnp.tril(jnp.ones((A_CHUNK, A_CHUNK), F32), -1)
    diff = gc[..., :, None] - gc[..., None, :]
    decay = jnp.where(causal, jnp.exp(jnp.where(causal, diff, 0.0)), 0.0)
    kb = k * beta[..., None]
    lower = jnp.einsum("bhnik,bhnjk->bhnij", kb, k) * decay * strict
    a_mat = jnp.eye(A_CHUNK, dtype=F32) + lower
    u = lax.linalg.triangular_solve(a_mat, v * beta[..., None], left_side=True, lower=True, unit_diagonal=True)
    w = lax.linalg.triangular_solve(a_mat, kb * jnp.exp(gc)[..., None], left_side=True, lower=True,
                                    unit_diagonal=True)
    qk = jnp.einsum("bhnik,bhnjk->bhnij", q, k) * decay
    q_dec = q * jnp.exp(gc)[..., None]
    k_dec = k * jnp.exp(gc[..., -1:] - gc)[..., None]
    g_tot = jnp.exp(gc[..., -1])

    def step(s, xs):
        w_n, u_n, qk_n, qd_n, kd_n, gt_n = xs
        v_new = u_n - jnp.einsum("bhck,bhkv->bhcv", w_n, s)
        o_n = jnp.einsum("bhck,bhkv->bhcv", qd_n, s) + jnp.einsum("bhij,bhjv->bhiv", qk_n, v_new)
        s = s * gt_n[..., None, None] + jnp.einsum("bhck,bhcv->bhkv", kd_n, v_new)
        return s, o_n

    xs = tuple(jnp.moveaxis(a, 2, 0) for a in (w, u, qk, q_dec, k_dec, g_tot))
    s_fin, o = lax.scan(step, s0, xs)
    o = jnp.transpose(o, (1, 0, 3, 2, 4)).reshape(bn, t, h, A_DV)
    return o, s_fin


def gdn_bidir(q, k, v, g, beta, s0_f, s0_b):
    o_f, s_f = gated_delta_chunked(q, k, v, g[:, :, 0], beta[:, :, 0], s0_f)
    rev = lambda a: jnp.flip(a, axis=1)
    o_b, s_b = gated_delta_chunked(rev(q), rev(k), rev(v), rev(g[:, :, 1]), rev(beta[:, :, 1]), s0_b)
    return o_f + rev(o_b), s_f, s_b


def gdn_output(o, z, norm_g):
    bn, t = z.shape[:2]
    o = rmsnorm(o, norm_g).reshape(bn, t, A_WIDTH).astype(z.dtype)
    return o * jax.nn.silu(z)


def axial_rope_tables(rows):
    r = jnp.repeat(jnp.arange(rows), GRID_W)
    col = jnp.tile(jnp.arange(GRID_W), rows)
    n_freq = B_ROPE // 4
    inv_freq = ROPE_THETA ** (-jnp.arange(n_freq, dtype=F32) / n_freq)
    ang = jnp.concatenate([r[:, None] * inv_freq, col[:, None] * inv_freq], axis=-1)
    return jnp.cos(ang)[:, None, :], jnp.sin(ang)[:, None, :]


def rope_2d(x, cos, sin):
    x2 = x.astype(F32).reshape(*x.shape[:-1], B_ROPE // 2, 2)
    x0, x1 = x2[..., 0], x2[..., 1]
    out = jnp.stack([x0 * cos - x1 * sin, x0 * sin + x1 * cos], axis=-1)
    return out.reshape(x.shape).astype(x.dtype)


def mla_queries(q_a, norm_g, w_qb, cos, sin):
    bn, t, _ = q_a.shape
    q = (rmsnorm(q_a, norm_g) @ w_qb).reshape(bn, t, B_HEADS, B_NOPE + B_ROPE)
    if cos is not None:
        q = jnp.concatenate([q[..., :B_NOPE], rope_2d(q[..., B_NOPE:], cos, sin)], axis=-1)
    return q


def mla_keys_values(kv_a, norm_g, w_kvb, cos, sin):
    bn, t, _ = kv_a.shape
    c_kv, k_pe = kv_a[..., :B_KV_LORA], kv_a[..., B_KV_LORA:]
    kv = (rmsnorm(c_kv, norm_g) @ w_kvb).reshape(bn, t, B_HEADS, B_NOPE + B_V)
    k_nope, v = kv[..., :B_NOPE], kv[..., B_NOPE:]
    k_pe = k_pe[:, :, None, :]
    if cos is not None:
        k_pe = rope_2d(k_pe, cos, sin)
    k = jnp.concatenate([k_nope, jnp.broadcast_to(k_pe, (bn, t, B_HEADS, B_ROPE))], axis=-1)
    return k, v


def softmax_attention(q, k, v):
    bn, t, h, dq = q.shape
    nb = t // B_QBLOCK
    scale = dq ** -0.5
    qb = jnp.moveaxis(q.reshape(bn, nb, B_QBLOCK, h, dq), 1, 0)

    def block(q_blk):
        s = jnp.einsum("bqhd,bkhd->bhqk", q_blk, k).astype(F32) * scale
        p = jax.nn.softmax(s, axis=-1).astype(v.dtype)
        return jnp.einsum("bhqk,bkhd->bqhd", p, v)

    o = lax.map(block, qb)
    return jnp.moveaxis(o, 0, 1).reshape(bn, t, h * v.shape[-1])


def conv_branch(h, b_gate, c_gate, z, conv_w):
    return b_gate * dwconv_centred(c_gate * h, conv_w) * jax.nn.silu(z)


def modulate(x, g, shift, scale):
    return rmsnorm(x, g) * (1.0 + scale) + shift


def hybrid_layer(x, ctx, mod_x, mod_c, norm_g, w_in, gdn_conv, gdn_a_log, gdn_dt_bias, gdn_norm_g,
                 mla_q_norm_g, mla_w_qb, mla_kv_norm_g, mla_w_kvb, conv_w, w_out, cos, sin, update_ctx):
    bn = x.shape[0]
    shift_x, scale_x, gate_x = jnp.split(mod_x, 3, axis=-1)
    shift_c, scale_c, gate_c = jnp.split(mod_c, 3, axis=-1)
    hx = modulate(x, norm_g, shift_x, scale_x)
    hc = modulate(ctx, norm_g, shift_c, scale_c)
    (xq, xk, xv, xz_a, xb, xa, xq_a, xkv_a, xz_b, xh, xbg, xcg, xz_c) = split_in(hx @ w_in)
    (cq, ck, cv, cz_a, cb, ca, cq_a, ckv_a, cz_b, ch, cbg, ccg, cz_c) = split_in(hc @ w_in)

    zeros = jnp.zeros((bn, A_HEADS, A_DK, A_DV), F32)
    o_ac, s_f, s_b = gdn_bidir(*gdn_prepare(cq, ck, cv, cb, ca, gdn_conv, gdn_a_log, gdn_dt_bias), zeros, zeros)
    o_ax, _, _ = gdn_bidir(*gdn_prepare(xq, xk, xv, xb, xa, gdn_conv, gdn_a_log, gdn_dt_bias), s_f, s_b)
    a_x = gdn_output(o_ax, xz_a, gdn_norm_g)

    k_c, v_c = mla_keys_values(ckv_a, mla_kv_norm_g, mla_w_kvb, None, None)
    k_x, v_x = mla_keys_values(xkv_a, mla_kv_norm_g, mla_w_kvb, cos, sin)
    q_x = mla_queries(xq_a, mla_q_norm_g, mla_w_qb, cos, sin)
    b_x = softmax_attention(q_x, jnp.concatenate([k_x, k_c], axis=1),
                            jnp.concatenate([v_x, v_c], axis=1)) * jax.nn.silu(xz_b)

    c_x = conv_branch(xh, xbg, xcg, xz_c, conv_w)

    x = x + gate_x * (jnp.concatenate([a_x, b_x, c_x], axis=-1) @ w_out)

    if update_ctx:
        a_c = gdn_output(o_ac, cz_a, gdn_norm_g)
        q_c = mla_queries(cq_a, mla_q_norm_g, mla_w_qb, None, None)
        b_c = softmax_attention(q_c, k_c, v_c) * jax.nn.silu(cz_b)
        c_c = conv_branch(ch, cbg, ccg, cz_c, conv_w)
        ctx = ctx + gate_c * (jnp.concatenate([a_c, b_c, c_c], axis=-1) @ w_out)
    return x, ctx


def setup_inputs(seed: int = 0) -> dict:
    key = jax.random.key(seed)
    ks = jax.random.split(key, 20)
    nrm = lambda k, shape, s: jax.random.normal(k, shape, F32) * s
    x = nrm(ks[0], (BATCH, SEQ, D_MODEL), 1.0)
    c = nrm(ks[1], (BATCH, D_MODEL), 1.0)
    ctx = nrm(ks[2], (BATCH, CTX_LEN, D_MODEL), 1.0)
    c_ctx = nrm(ks[3], (D_MODEL,), 1.0)
    w_ada = nrm(ks[4], (DEPTH, D_MODEL, 3 * D_MODEL), 0.5 * D_MODEL ** -0.5)
    b_ada = nrm(ks[5], (DEPTH, 3 * D_MODEL), 0.02)
    norm_g = 1.0 + nrm(ks[6], (DEPTH, D_MODEL), 0.1)
    w_in = nrm(ks[7], (DEPTH, D_MODEL, N_IN), D_MODEL ** -0.5)
    gdn_conv = nrm(ks[8], (DEPTH, A_CONV, 2 * A_QK_WIDTH + A_WIDTH), A_CONV ** -0.5)
    gdn_a_log = jnp.log(jax.random.uniform(ks[9], (DEPTH, 2, A_HEADS), F32, 1.0, 16.0))
    dt = jnp.exp(jax.random.uniform(ks[10], (DEPTH, 2, A_HEADS), F32, math.log(1e-3), math.log(1e-1)))
    gdn_dt_bias = dt + jnp.log(-jnp.expm1(-dt))
    gdn_norm_g = 1.0 + nrm(ks[11], (DEPTH, A_DV), 0.1)
    mla_q_norm_g = 1.0 + nrm(ks[12], (DEPTH, B_Q_LORA), 0.1)
    mla_w_qb = nrm(ks[13], (DEPTH, B_Q_LORA, B_HEADS * (B_NOPE + B_ROPE)), B_Q_LORA ** -0.5)
    mla_kv_norm_g = 1.0 + nrm(ks[14], (DEPTH, B_KV_LORA), 0.1)
    mla_w_kvb = nrm(ks[15], (DEPTH, B_KV_LORA, B_HEADS * (B_NOPE + B_V)), B_KV_LORA ** -0.5)
    conv_w = nrm(ks[16], (DEPTH, C_CONV, C_WIDTH), C_CONV ** -0.5)
    w_out = nrm(ks[17], (DEPTH, D_MIX, D_MODEL), D_MIX ** -0.5)
    final_norm_g = 1.0 + nrm(ks[18], (D_MODEL,), 0.1)
    return {"x": x, "c": c, "ctx": ctx, "c_ctx": c_ctx, "w_ada": w_ada, "b_ada": b_ada,
            "norm_g": norm_g, "w_in": w_in, "gdn_conv": gdn_conv, "gdn_a_log": gdn_a_log,
            "gdn_dt_bias": gdn_dt_bias, "gdn_norm_g": gdn_norm_g, "mla_q_norm_g": mla_q_norm_g,
            "mla_w_qb": mla_w_qb, "mla_kv_norm_g": mla_kv_norm_g, "mla_w_kvb": mla_w_kvb,
            "conv_w": conv_w, "w_out": w_out, "final_norm_g": final_norm_g}


def reference(x, c, ctx, c_ctx, w_ada, b_ada, norm_g, w_in, gdn_conv, gdn_a_log, gdn_dt_bias, gdn_norm_g,
              mla_q_norm_g, mla_w_qb, mla_kv_norm_g, mla_w_kvb, conv_w, w_out, final_norm_g):
    ROWS = x.shape[1] // GRID_W
    cos, sin = axial_rope_tables(ROWS)
    for l in range(DEPTH):
        mod_x = (jax.nn.silu(c) @ w_ada[l] + b_ada[l])[:, None, :]
        mod_c = jax.nn.silu(c_ctx) @ w_ada[l] + b_ada[l]
        x, ctx = hybrid_layer(x, ctx, mod_x, mod_c, norm_g[l], w_in[l], gdn_conv[l], gdn_a_log[l],
                              gdn_dt_bias[l], gdn_norm_g[l], mla_q_norm_g[l], mla_w_qb[l],
                              mla_kv_norm_g[l], mla_w_kvb[l], conv_w[l], w_out[l], cos, sin,
                              l < DEPTH - 1)
    return rmsnorm(x, final_norm_g)
```

```python
from contextlib import ExitStack
import numpy as np
import concourse.bass as bass
import concourse.mybir as mybir
from concourse.bass_utils import run_bass_kernel_spmd

F32 = mybir.dt.float32
BF16 = mybir.dt.bfloat16
AF = mybir.ActivationFunctionType
ALU = mybir.AluOpType
AX = mybir.AxisListType

EPOCH = 1000
STRICT_SAME_ENGINE = False
DMA_ROT = 12


class Buf:
    __slots__ = ("name", "last_w", "readers", "multi", "writers")

    def __init__(self, name, multi=False):
        self.name = name
        self.last_w = None
        self.readers = []
        self.multi = multi
        self.writers = []


class V:
    __slots__ = ("ap", "bufs")

    def __init__(self, ap, bufs):
        self.ap = ap
        self.bufs = bufs if isinstance(bufs, (list, tuple)) else [bufs]


class Op:
    __slots__ = ("eng", "fn", "reads", "writes", "waits", "signal", "count", "is_dma", "dsem", "dval", "prewait")

    def __init__(self, eng, fn, reads, writes, is_dma=False):
        self.eng = eng; self.fn = fn; self.reads = reads; self.writes = writes
        self.waits = []; self.signal = False; self.count = None
        self.is_dma = is_dma; self.dsem = None; self.dval = None; self.prewait = None


ENGS = ("pe", "act", "dve", "pool", "sp")


class Prog:
    def __init__(self, nc):
        self.nc = nc
        self.ops = []
        self.per_eng = {e: [] for e in ENGS}
        self.nbuf = 0
        self.dma_n = {e: 0 for e in ENGS}
        self.dma_last = {}

    def init_bar(self, stack):
        self._bar_sb = stack.enter_context(self.nc.sbuf_tensor("bar_sb", [128, 8], F32))
        self._bar_ps = stack.enter_context(self.nc.psum_tensor("bar_ps", [128, 512], F32))
        self._bar_tok = {e: self.buf("bartok_" + e) for e in ("act", "dve", "pool")}
        self._bar_init = self.buf("barinit")
        self.add("dve", lambda e: e.memset(self._bar_sb[:], 0.0), [], [V(None, [self._bar_init] + list(self._bar_tok.values()))])

    def buf(self, name=None):
        self.nbuf += 1
        return Buf(name or f"b{self.nbuf}")

    def mbuf(self, name=None):
        self.nbuf += 1
        return Buf(name or f"m{self.nbuf}", multi=True)

    def sb(self, name, shape, dtype):
        t = self.nc.alloc_sbuf_tensor(name, list(shape), dtype)
        return t

    def ps(self, name, shape, dtype=F32):
        return self.nc.alloc_psum_tensor(name, list(shape), dtype)

    def add(self, eng, fn, reads, writes, is_dma=False):
        rb = [b for v in reads for b in v.bufs]
        wb = [b for v in writes for b in v.bufs]
        op = Op(eng, fn, rb, wb, is_dma)
        deps = []
        for b in rb:
            if b.multi:
                for w_ in b.writers:
                    deps.append((w_, "raw"))
            elif b.last_w is not None:
                deps.append((b.last_w, "raw"))
        for b in wb:
            if b.multi and is_dma:
                pass
            elif b.multi:
                for w_ in b.writers:
                    deps.append((w_, "waw"))
            elif b.last_w is not None:
                deps.append((b.last_w, "waw"))
            for r in b.readers:
                deps.append((r, "war"))
        seen = set()
        for d, kind in deps:
            if d is op or id(d) in seen:
                continue
            if d.eng == eng and not d.is_dma:
                if eng == "pe" or (kind != "raw" and not STRICT_SAME_ENGINE):
                    continue
            seen.add(id(d))
            d.signal = True
            op.waits.append(d)
        for b in rb:
            b.readers.append(op)
        for b in wb:
            if b.multi:
                if b.readers or not is_dma:
                    b.writers = []
                b.writers.append(op)
            b.last_w = op
            b.readers = []
        self.ops.append(op)
        self.per_eng[eng].append(op)
        return op

    def emit(self):
        nc = self.nc
        nsig = {e: 0 for e in ENGS}
        dma_ops = {e: [] for e in ENGS}
        for op in self.ops:
            if op.is_dma:
                dma_ops[op.eng].append(op)
            elif op.signal:
                nsig[op.eng] += 1
                op.count = nsig[op.eng]
        self.nsig = nsig
        sems = {}
        for e in ENGS:
            n = (nsig[e] + EPOCH - 1) // EPOCH
            sems[e] = [nc.alloc_semaphore(f"s_{e}_{i}") for i in range(max(n, 1))]
        dsems = {}
        for e in ENGS:
            if dma_ops[e]:
                dsems[e] = [nc.alloc_semaphore(f"d_{e}_{i}") for i in range(DMA_ROT)]
                cnt = [0] * DMA_ROT
                for i, op in enumerate(dma_ops[e]):
                    k = i % DMA_ROT
                    if cnt[k] > 0:
                        op.prewait = (dsems[e][k], 16 * cnt[k])
                    cnt[k] += 1
                    op.dsem = dsems[e][k]; op.dval = 16 * cnt[k]
        handles = {"pe": nc.tensor, "act": nc.scalar, "dve": nc.vector, "pool": nc.gpsimd, "sp": nc.sync}

        def emit_engine(ename, eng):
            known = {}
            for op in self.per_eng[ename]:
                if op.prewait is not None:
                    eng.wait_ge(op.prewait[0], op.prewait[1])
                for d in op.waits:
                    if d.is_dma:
                        key = ("d", id(d.dsem)); val = d.dval; sem = d.dsem
                    else:
                        ep = (d.count - 1) // EPOCH
                        key = (d.eng, ep); val = d.count - ep * EPOCH; sem = sems[d.eng][ep]
                    if known.get(key, 0) >= val:
                        continue
                    known[key] = val
                    eng.wait_ge(sem, val)
                if op.fn is None:
                    continue
                ins = op.fn(eng)
                if op.is_dma:
                    ins.then_inc(op.dsem, 16)
                elif op.signal:
                    ep = (op.count - 1) // EPOCH
                    ins.then_inc(sems[ename][ep], 1)
            if ename in dsems:
                last = {}
                for op in dma_ops[ename]:
                    last[id(op.dsem)] = (op.dsem, op.dval)
                for sem, val in last.values():
                    eng.wait_ge(sem, val)

        with nc.Block() as block:
            @block.tensor
            def _(e):
                emit_engine("pe", e)

            @block.scalar
            def _(e):
                emit_engine("act", e)

            @block.vector
            def _(e):
                emit_engine("dve", e)

            @block.gpsimd
            def _(e):
                emit_engine("pool", e)

            @block.sync
            def _(e):
                emit_engine("sp", e)

    def barrier(self):
        toks = []
        if not hasattr(self, "_bar_tok"):
            self._bar_tok = {e: self.buf("bartok_" + e) for e in ("act", "dve", "pool")}
        for e in ("pe", "act", "dve", "pool"):
            b = self.buf("bar_" + e)
            if e == "pe":
                self.add("pe", lambda en: en.matmul(self._bar_ps[0:1, 0:1], self._bar_sb[0:1, 0:1], self._bar_sb[0:1, 0:1], start=True, stop=True), [V(None, self._bar_init)], [V(None, b)])
            else:
                i = ("act", "dve", "pool").index(e) + 1
                tk_ = self._bar_tok[e]
                self.add(e, (lambda i, e_: (lambda en: en.memzero(self._bar_sb[0:1, i:i + 1]) if e_ == 'act' else en.memset(self._bar_sb[0:1, i:i + 1], 0.0)))(i, e),
                         [V(None, tk_)], [V(None, [b, tk_])])
            toks.append(b)
        dm = []
        for q in ENGS:
            dm += [op for op in self.per_eng[q] if op.is_dma][-DMA_ROT:]
        for e in ENGS:
            op = self.add(e, None, [V(None, toks)], [])
            op.eng = e
            for d in dm:
                if d not in op.waits:
                    op.waits.append(d)

    def dma(self, out, in_, eng="sp", **kw):
        return self.add(eng, lambda e: e.dma_start(out=out.ap, in_=in_.ap, **kw), [in_], [out], is_dma=True)

    def mm(self, out, lhsT, rhs, start=True, stop=True, extra_reads=()):
        reads = [lhsT, rhs] + list(extra_reads)
        return self.add("pe", lambda e: e.matmul(out.ap, lhsT.ap, rhs.ap, start=start, stop=stop), reads, [out])

    def transpose(self, out, in_, ident):
        return self.add("pe", lambda e: e.transpose(out.ap, in_.ap, ident.ap), [in_, ident], [out])

    def act(self, out, in_, func, bias=None, scale=None, accum=None, eng="sp"):
        reads = [in_]
        kw = {}
        if bias is not None:
            if isinstance(bias, V):
                reads.append(bias); kw["bias"] = bias.ap
            else:
                kw["bias"] = bias
        if scale is not None:
            if isinstance(scale, V):
                reads.append(scale); kw["scale"] = scale.ap
            else:
                kw["scale"] = scale
        writes = [out]
        if accum is not None:
            writes.append(accum); kw["accum_out"] = accum.ap
        return self.add("act", lambda e: e.activation(out.ap, in_.ap, func, **kw), reads, writes)

    def tt(self, eng, out, a, b, op):
        return self.add(eng, lambda e: e.tensor_tensor(out.ap, a.ap, b.ap, op), [a, b], [out])

    def ts(self, eng, out, a, s1, op0, s2=None, op1=None, accum=None):
        reads = [a]
        s1a = s1.ap if isinstance(s1, V) else s1
        s2a = s2.ap if isinstance(s2, V) else s2
        if isinstance(s1, V): reads.append(s1)
        if isinstance(s2, V): reads.append(s2)
        writes = [out]
        kw = {}
        if op1 is not None:
            kw["op1"] = op1
        if accum is not None:
            writes.append(accum); kw["accum_out"] = accum.ap
        return self.add(eng, lambda e: e.tensor_scalar(out.ap, a.ap, s1a, s2a, op0, **kw), reads, writes)

    def stt(self, eng, out, a, s, b, op0, op1):
        reads = [a, b]
        sa = s.ap if isinstance(s, V) else s
        if isinstance(s, V): reads.append(s)
        return self.add(eng, lambda e: e.scalar_tensor_tensor(out.ap, a.ap, sa, b.ap, op0, op1), reads, [out])

    def scale(self, eng, out, in_, sc):
        if eng == "act":
            return self.act(out, in_, AF.Identity, scale=sc)
        return self.ts(eng, out, in_, sc, ALU.mult)

    def copy(self, eng, out, in_):
        if eng == "act":
            return self.add("act", lambda e: e.copy(out.ap, in_.ap), [in_], [out])
        return self.add(eng, lambda e: e.tensor_copy(out.ap, in_.ap), [in_], [out])

    def memset(self, eng, out, val):
        return self.add(eng, lambda e: e.memset(out.ap, val), [], [out])

    def recip(self, out, in_):
        return self.add("dve", lambda e: e.reciprocal(out.ap, in_.ap), [in_], [out])

    def reduce(self, eng, out, in_, op, axis=AX.X):
        return self.add(eng, lambda e: e.tensor_reduce(out.ap, in_.ap, axis, op), [in_], [out])


T = 2304; NT = 18; D = 1024; TP = 2312
PT_W = 1648


def tok_off(t):
    return 2 + t * 128 if t < 2 else 262 + (t - 2) * 128


W_CHUNKS = [
    ("tok", 512, [(2048, 368)]),
    ("feat", 0, [(0, 512)]),
    ("feat", 512, [(512, 512)]),
    ("feat", 1024, [(1024, 512)]),
    ("tok", 0, [(1536, 512)]),
    ("feat", 1536, [(2672, 256), (3184, 256)]),
    ("tok", 880, [(2416, 256), (2928, 256)]),
    ("tok", 1392, [(3440, 256)]),
]


class Ctx:
    pass


_ALLOC_N = [0]


def alloc(P, stack, name, shape, dtype, psum=False):
    _ALLOC_N[0] += 1
    name = f"{name}_{_ALLOC_N[0]}"
    if psum:
        nbytes = int(np.prod(shape[1:])) * (2 if dtype == BF16 else 4)
        assert nbytes <= 2048
        if nbytes < 2048:
            full = stack.enter_context(P.nc.psum_tensor(name, [128, 512], F32))
            v = full[:]
            if dtype == BF16:
                v = v.bitcast(BF16)
            n = int(np.prod(shape[1:]))
            v = v[0:shape[0], 0:n]
            if len(shape) == 3:
                v = v.rearrange("p (a b) -> p a b", b=shape[2])
            return v
        t = stack.enter_context(P.nc.psum_tensor(name, list(shape), dtype))
    else:
        t = stack.enter_context(P.nc.sbuf_tensor(name, list(shape), dtype))
    return t


def load_consts(P, stack, cin):
    C = Ctx()
    C.idf = alloc(P, stack, "c_idf", [128, 128], F32); C.b_idf = P.buf()
    C.idb = alloc(P, stack, "c_idb", [128, 128], BF16); C.b_idb = P.buf()
    C.onesf = alloc(P, stack, "c_onesf", [128, 128], F32); C.b_onesf = P.buf()
    C.sel = alloc(P, stack, "c_sel", [2, TP], BF16); C.b_sel = P.buf()
    C.self_ = alloc(P, stack, "c_self", [2, TP], F32); C.b_self = P.buf()
    P.dma(V(C.idf[:], C.b_idf), V(cin["ident"], P.buf()))
    P.copy("dve", V(C.idb[:], C.b_idb), V(C.idf[:], C.b_idf))
    P.memset("pool", V(C.onesf[:], C.b_onesf), 1.0)
    P.dma(V(C.self_[:], C.b_self), V(cin["sel"], P.buf()))
    P.copy("dve", V(C.sel[:], C.b_sel), V(C.self_[:], C.b_self))
    return C


def stage_A(P, C, l, xin, W, out, nxt=None, pre=None):
    nc = P.nc
    with ExitStack() as st:
        wbuf = [alloc(P, st, f"a_wbuf{i}", [128, 8, 512], F32) for i in range(3)]
        b_wbuf = [P.buf(), P.buf(), P.buf()]
        cc = alloc(P, st, "a_cc", [128, 8, 2], F32); b_cc = P.buf()
        scc = alloc(P, st, "a_scc", [128, 8, 2], F32); b_scc = P.buf()
        bada = alloc(P, st, "a_bada", [128, 24, 2], F32); b_bada = P.buf()
        ng = alloc(P, st, "a_ng", [128, 8, 2], F32); b_ng = P.buf()
        mod = alloc(P, st, "a_mod", [128, 24, 2], F32); b_mod = P.buf()
        Asc = alloc(P, st, "a_Asc", [128, 8, 2], F32); b_Asc = P.buf()
        hT = alloc(P, st, "a_hT", [128, 8, TP], BF16); b_hT = [P.buf() for _ in range(NT)]
        ps_mod = alloc(P, st, "a_psmod", [128, 24, 2], F32, psum=True); b_psmod = P.buf()
        ps_b = alloc(P, st, "a_psb", [2, 512], F32, psum=True); b_psb = P.buf()
        modrow = alloc(P, st, "a_modrow", [2, 3072], F32); b_modrow = P.buf()
        P.dma(V(cc[:], b_cc), V(W["cc"], P.buf()))
        P.dma(V(bada[:], b_bada), V(W["bada"], P.buf()))
        P.dma(V(ng[:], b_ng), V(W["ng"], P.buf()))
        P.act(V(scc[:], b_scc), V(cc[:], b_cc), AF.Silu)
        wada = W["w_ada"].rearrange("(kc p) n -> p kc n", p=128)
        xt = [alloc(P, st, f"a_xt{i}", [128, 1024], F32) for i in range(3)]; b_xt = [P.buf() for _ in range(3)]
        junk = alloc(P, st, "a_junk", [128, 1024], BF16); b_junk = P.buf()
        yb = [alloc(P, st, f"a_yb{i}", [128, 1024], BF16) for i in range(2)]; b_yb = [P.buf() for _ in range(2)]
        stt_ = alloc(P, st, "a_st", [128, NT, 4], F32); b_st = [P.buf() for _ in range(NT)]
        ps_t = [alloc(P, st, f"a_pst{i}", [128, 8, 128], BF16, psum=True) for i in range(2)]; b_pst = [P.buf() for _ in range(2)]
        def gen_mod():
            if pre is not None:
                P.dma(V(mod[:], b_mod), V(pre[0], pre[1]))
                P.stt("dve", V(Asc[:], b_Asc), V(mod[:, 8:16, :], b_mod), 1.0, V(ng[:], b_ng), ALU.add, ALU.mult)
                yield
                return
            for ch in range(6):
                wb = wbuf[ch % 2]; bw = b_wbuf[ch % 2]
                P.dma(V(wb[:], bw), V(wada[:, :, ch * 512:(ch + 1) * 512], P.buf()))
                for kc in range(8):
                    P.mm(V(ps_b[:, :], b_psb), V(scc[:, kc, :], b_scc), V(wb[:, kc, :], bw), start=(kc == 0), stop=(kc == 7))
                P.copy("act", V(modrow[:, ch * 512:(ch + 1) * 512], b_modrow), V(ps_b[:, :], b_psb))
                yield
                for o4 in range(4):
                    oc = ch * 4 + o4
                    P.mm(V(ps_mod[:, oc, :], b_psmod), V(modrow[:, oc * 128:(oc + 1) * 128], b_modrow), V(C.idf[0:2, 0:2], C.b_idf))
                yield
            P.tt("dve", V(mod[:], b_mod), V(ps_mod[:], b_psmod), V(bada[:], b_bada), ALU.add)
            P.dma(V(out["modo"], out["b_modo"]), V(mod[:], b_mod))
            P.stt("dve", V(Asc[:], b_Asc), V(mod[:, 8:16, :], b_mod), 1.0, V(ng[:], b_ng), ALU.add, ALU.mult)
            yield
        P.memset("dve", V(stt_[:], b_st), 0.0)
        def gen_norm(par):
            for t in range(NT):
                if t % 2 != par:
                    continue
                x_ = xt[t % 3]; bx = b_xt[t % 3]
                P.dma(V(x_[:], bx), V(xin[t * 128:(t + 1) * 128, :], P.buf()))
                s = stt_[:, t, :]
                P.act(V(junk[:], b_junk), V(x_[:], bx), AF.Square, accum=V(stt_[:, t, 0:1], b_st[t]))
                P.ts("dve", V(stt_[:, t, 1:2], b_st[t]), V(stt_[:, t, 0:1], b_st[t]), 1.0 / 1024, ALU.mult, 1e-6, ALU.add)
                P.act(V(stt_[:, t, 2:3], b_st[t]), V(stt_[:, t, 1:2], b_st[t]), AF.Ln)
                P.act(V(stt_[:, t, 3:4], b_st[t]), V(stt_[:, t, 2:3], b_st[t]), AF.Exp, scale=-0.5)
                y = yb[t % 2]; by = b_yb[t % 2]
                P.scale("dve" if t % 2 == 0 else "act", V(y[:], by), V(x_[:], bx), V(stt_[:, t, 3:4], b_st[t]))
                yield
                pt = ps_t[t % 2]; bp = b_pst[t % 2]
                for j in range(8):
                    P.transpose(V(pt[:, j, :], bp), V(y[:, j * 128:(j + 1) * 128], by), V(C.idb[:], C.b_idb))
                o0 = tok_off(t)
                P.copy("act" if t % 2 == 0 else "dve", V(hT[:, :, o0:o0 + 128], b_hT[t]), V(pt[:], bp))
                yield

        win = W["w_in"].rearrange("(kc p) n -> p kc n", p=128)

        def load_w(ci_):
            segs_ = W_CHUNKS[ci_][2]
            c0 = 0
            wi = (ci_ + 2) % 3
            for (w0, n) in segs_:
                P.dma(V(wbuf[wi][:, :, c0:c0 + n], b_wbuf[wi]), V(win[:, :, w0:w0 + n], P.buf()))
                c0 += n
        load_w(0)
        gens = [[gen_mod(), 1], [gen_norm(0), 2], [gen_norm(1), 2]]
        while gens:
            for gd in list(gens):
                try:
                    for _ in range(gd[1]):
                        next(gd[0])
                except StopIteration:
                    gens.remove(gd)
        win = W["w_in"].rearrange("(kc p) n -> p kc n", p=128)
        wx = [alloc(P, st, f"a_wx{i}", [128, 8, 512], BF16) for i in range(2)]; b_wx = [P.buf() for _ in range(2)]
        wc = [alloc(P, st, f"a_wc{i}", [128, 8, 512], BF16) for i in range(2)]; b_wc = [P.buf() for _ in range(2)]
        browf = [alloc(P, st, f"a_browf{i}", [2, 512], F32) for i in range(2)]; b_browf = [P.buf() for _ in range(2)]
        bcol = [alloc(P, st, f"a_bcol{i}", [128, 4, 2], F32) for i in range(2)]; b_bcol = [P.buf() for _ in range(2)]
        bbc = [alloc(P, st, f"a_bbc{i}", [128, 512], F32) for i in range(2)]; b_bbc = [P.buf() for _ in range(2)]
        ps_m = [alloc(P, st, f"a_psm{i}", [128, 512], F32, psum=True) for i in range(3)]; b_psm = [P.buf() for _ in range(3)]
        sgf = [alloc(P, st, f"a_sgf{i}", [128, 512], BF16) for i in range(3)]; b_sgf = [P.buf() for _ in range(3)]
        sgt = [alloc(P, st, f"a_sgt{i}", [128, 512], F32) for i in range(3)]; b_sgt = [P.buf() for _ in range(3)]
        all_hT = b_hT
        nmm = 0
        if nxt is not None:
            wbn = alloc(P, st, "a_wbn", [128, 8, 512], F32); b_wbn = P.buf()
            rown = alloc(P, st, "a_rown", [2, 512], F32); b_rown = P.buf()
            badan = alloc(P, st, "a_badan", [128, 24, 2], F32); b_badan = P.buf()
            modn = alloc(P, st, "a_modn", [128, 24, 2], F32); b_modn = P.buf()
            P.dma(V(badan[:], b_badan), V(nxt["bada"], P.buf()))
            wada_n = nxt["w_ada"].rearrange("(kc p) n -> p kc n", p=128)

        def next_mod_chunk(ch):
            for kc in range(8):
                P.mm(V(ps_b[:, :], b_psb), V(scc[:, kc, :], b_scc), V(wbn[:, kc, :], b_wbn), start=(kc == 0), stop=(kc == 7))
            P.copy("act", V(rown[:], b_rown), V(ps_b[:, :], b_psb))
            for o4 in range(4):
                P.mm(V(ps_mod[:, o4, :], b_psmod), V(rown[:, o4 * 128:(o4 + 1) * 128], b_rown), V(C.idf[0:2, 0:2], C.b_idf))
            P.tt("dve", V(modn[:, ch * 4:(ch + 1) * 4, :], b_modn), V(ps_mod[:, 0:4, :], b_psmod), V(badan[:, ch * 4:(ch + 1) * 4, :], b_badan), ALU.add)
            if ch == 5:
                P.dma(V(nxt["modo"], nxt["b_modo"]), V(modn[:], b_modn))

        for ci, (kind, dst, segs) in enumerate(W_CHUNKS):
            wb = wbuf[(ci + 2) % 3]; bw = b_wbuf[(ci + 2) % 3]
            ncol = sum(n for _, n in segs)
            if nxt is not None and ci < 6:
                P.dma(V(wbn[:], b_wbn), V(wada_n[:, :, ci * 512:(ci + 1) * 512], P.buf()))
            if ci + 1 < len(W_CHUNKS):
                load_w(ci + 1)
            wxx = wx[ci % 2]; bwx = b_wx[ci % 2]; wcc = wc[ci % 2]; bwc = b_wc[ci % 2]
            for kc in range(8):
                P.scale("dve", V(wxx[:, kc, :ncol], bwx), V(wb[:, kc, :ncol], bw), V(Asc[:, kc, 0:1], b_Asc))
                P.scale("act" if kc % 4 else "dve", V(wcc[:, kc, :ncol], bwc), V(wb[:, kc, :ncol], bw), V(Asc[:, kc, 1:2], b_Asc))
            for kc in range(8):
                P.mm(V(ps_b[:, :ncol], b_psb), V(mod[:, kc, :], b_mod), V(wb[:, kc, :ncol], bw), start=(kc == 0), stop=(kc == 7))
            br = browf[ci % 2]; bbr = b_browf[ci % 2]
            P.copy("act", V(br[:, :ncol], bbr), V(ps_b[:, :ncol], b_psb))
            if kind == "feat":
                for fc in range(ncol // 128):
                    P.mm(V(ps_mod[:, fc, :], b_psmod), V(br[:, fc * 128:(fc + 1) * 128], bbr), V(C.idf[0:2, 0:2], C.b_idf))
                P.copy("dve", V(bcol[ci % 2][:], b_bcol[ci % 2]), V(ps_mod[:, 0:4, :], b_psmod))
            else:
                for w_ in range(2):
                    pmb = ps_m[nmm % 3]; bpmb = b_psm[nmm % 3]; nmm += 1
                    selw = C.self_[:, 262:390] if w_ == 0 else C.self_[:, 2:130]
                    P.mm(V(pmb[:, :ncol], bpmb), V(selw, C.b_self), V(br[:, :ncol], bbr))
                    P.copy("act", V(bbc[w_][:, :ncol], b_bbc[w_]), V(pmb[:, :ncol], bpmb))
            if kind == "feat":
                for fc in range(ncol // 128):
                    for (g0, gn, wsel, bw_sel, hbufs) in ([] if (l == 1 and ci == 5) else [(2, 256, wcc, bwc, all_hT[0:2])]) + [
                            (262 + 512 * g, 512, wxx, bwx, all_hT[2 + 4 * g:6 + 4 * g]) for g in range(4)]:
                        pm = ps_m[nmm % 3]; bpm = b_psm[nmm % 3]
                        for kc in range(8):
                            P.mm(V(pm[:, :gn], bpm), V(wsel[:, kc, fc * 128:(fc + 1) * 128], bw_sel), V(hT[:, kc, g0:g0 + gn], hbufs),
                                 start=(kc == 0), stop=(kc == 7))
                        sg = sgf[nmm % 3]; bsg = b_sgf[nmm % 3]
                        wcol = 1 if g0 == 2 else 0
                        P.act(V(sg[:, :gn], bsg), V(pm[:, :gn], bpm), AF.Identity, bias=V(bcol[ci % 2][:, fc, wcol:wcol + 1], b_bcol[ci % 2]))
                        r0 = dst + fc * 128
                        P.dma(V(out["PF"][r0:r0 + 128, g0:g0 + gn], out["b_PF"]), V(sg[:, :gn], bsg), eng="sp")
                        nmm += 1
            else:
                for t in range(NT):
                    if l == 1 and t < 2 and ci in (4, 6, 7):
                        continue
                    o0 = tok_off(t)
                    wsel, bw_sel = (wcc, bwc) if t < 2 else (wxx, bwx)
                    pm = ps_m[nmm % 3]; bpm = b_psm[nmm % 3]
                    for kc in range(8):
                        P.mm(V(pm[:, :ncol], bpm), V(hT[:, kc, o0:o0 + 128], b_hT[t]), V(wsel[:, kc, :ncol], bw_sel), start=(kc == 0), stop=(kc == 7))
                    sg = sgt[nmm % 3]; bsg = b_sgt[nmm % 3]
                    wrow = 1 if t < 2 else 0
                    P.tt("dve", V(sg[:, :ncol], bsg), V(pm[:, :ncol], bpm), V(bbc[wrow][:, :ncol], b_bbc[wrow]), ALU.add)
                    P.dma(V(out["PT"][t * 128:(t + 1) * 128, dst:dst + ncol], out["b_PT"]), V(sg[:, :ncol], bsg), eng="sp")
                    nmm += 1
            if nxt is not None and ci < 6:
                next_mod_chunk(ci)
    P.barrier()


NEG = -1.0e30


def gdn_consts_host():
    i = np.arange(128)
    J, I = np.meshgrid(i, i, indexing="ij")
    d = {}
    d["mC0"] = np.where(I >= J, 0.0, NEG).astype(np.float32)
    d["mS0"] = np.where(I > J, 0.0, NEG).astype(np.float32)
    d["mC1"] = np.where(I <= J, 0.0, NEG).astype(np.float32)
    d["mS1"] = np.where(I < J, 0.0, NEG).astype(np.float32)
    d["triF"] = (J <= I).astype(np.float32)
    d["triB"] = (J >= I).astype(np.float32)
    d["bd32"] = (((J // 32) == (I // 32)) & (J != I)).astype(np.float32)
    d["off64"] = (((J // 64) == (I // 64)) & ((J // 32) != (I // 32))).astype(np.float32)
    d["off128"] = ((J // 64) != (I // 64)).astype(np.float32)
    return {"gmask": np.ascontiguousarray(np.stack([d[k] for k in ("mC0", "mS0", "mC1", "mS1", "triF", "triB", "bd32", "off64", "off128")], 1))}


def stage_B(P, C, l, Wd, io, stop=0):
    nc = P.nc
    with ExitStack() as st:
        gm = alloc(P, st, "b_gm", [128, 9, 128], F32); b_gm = P.buf()
        P.dma(V(gm[:], b_gm), V(Wd["gmask"], P.buf()))
        gmb = alloc(P, st, "b_gmb", [128, 3, 128], BF16); b_gmb = P.buf()
        P.copy("dve", V(gmb[:], b_gmb), V(gm[:, 6:9, :], b_gm))
        qkvc = alloc(P, st, "b_qkvc", [128, NT, 1536], BF16); b_qkvc = [P.buf() for _ in range(NT)]
        pp = [alloc(P, st, f"b_pp{i}", [128, 4, 128], F32, psum=True) for i in range(7)]; b_pp = [P.buf() for _ in range(7)]
        ppi = [0]

        def nps():
            k = ppi[0] % 7; ppi[0] += 1
            return pp[k], b_pp[k]

        if stop in (3, 4):
            return
        NS = NT * 8
        ba = alloc(P, st, "b_ba", [128, NT, 16], F32); b_ba = P.mbuf()
        for t in range(NT):
            P.dma(V(ba[:, t, :], b_ba), V(io["PT"][t * 128:(t + 1) * 128, 512:528], io["b_PT"]))
        alog = alloc(P, st, "b_alog", [128, NT, 8], F32); b_alog = P.buf()
        dtb = alloc(P, st, "b_dtb", [128, NT, 8], F32); b_dtb = P.buf()
        P.dma(V(alog[:], b_alog), V(Wd["alog"], P.buf()))
        P.dma(V(dtb[:], b_dtb), V(Wd["dtb"], P.buf()))
        names = ["beta", "negb", "g", "gc", "ngc2", "egc", "ekd", "cq", "ck1", "ck2", "bck2", "egt", "t1", "t2", "t3"]
        S_ = {}
        for n in names:
            S_[n] = (alloc(P, st, "b_s_" + n, [128, NT, 8], F32), P.buf())
        def sv(n): return V(S_[n][0][:], S_[n][1])
        bv = V(ba[:, :, 0:8], b_ba); av = V(ba[:, :, 8:16], b_ba)
        if stop == 7:
            return
        P.act(sv("t1"), bv, AF.Exp, scale=-1.0)
        P.ts("dve", sv("t2"), sv("t1"), 1.0, ALU.add)
        P.recip(sv("beta"), sv("t2"))
        P.ts("dve", sv("negb"), sv("beta"), -1.0, ALU.mult)
        if stop == 8:
            return
        P.tt("dve", sv("t1"), av, V(dtb[:], b_dtb), ALU.add)
        P.act(sv("t2"), sv("t1"), AF.Exp)
        P.ts("dve", sv("t3"), sv("t2"), 1.0, ALU.add)
        P.act(sv("t1"), sv("t3"), AF.Ln)
        P.act(sv("t2"), V(alog[:], b_alog), AF.Exp)
        P.stt("dve", sv("g"), sv("t1"), -1.0, sv("t2"), ALU.mult, ALU.mult)
        if stop == 5:
            return
        g2 = S_["g"][0][:].rearrange("p t e -> p (t e)")
        pm, bpm = nps(); pm2, bpm2 = nps(); pm3, bpm3 = nps()
        pmf = pm[:].rearrange("p a b -> p (a b)"); pm2f = pm2[:].rearrange("p a b -> p (a b)"); pm3f = pm3[:].rearrange("p a b -> p (a b)")
        P.mm(V(pmf[:, :NS], bpm), V(gm[:, 4, :], b_gm), V(g2, S_["g"][1]))
        P.mm(V(pm2f[:, :NS], bpm2), V(gm[:, 5, :], b_gm), V(g2, S_["g"][1]))
        P.mm(V(pm3f[:, :NS], bpm3), V(C.onesf[:], C.b_onesf), V(g2, S_["g"][1]))
        gc = S_["gc"][0]
        P.copy("dve", V(gc[:, :, 0:4], S_["gc"][1]), V(pmf[:, :NS].rearrange("p (t e) -> p t e", e=8)[:, :, 0:4], bpm))
        P.copy("dve", V(gc[:, :, 4:8], S_["gc"][1]), V(pm2f[:, :NS].rearrange("p (t e) -> p t e", e=8)[:, :, 4:8], bpm2))
        P.copy("dve", sv("t3"), V(pm3f[:, :NS].rearrange("p (t e) -> p t e", e=8), bpm3))
        P.act(sv("egt"), sv("t3"), AF.Exp)
        P.ts("dve", sv("ngc2"), sv("gc"), -1.0, ALU.mult)
        P.tt("dve", sv("t1"), sv("t3"), sv("ngc2"), ALU.add)
        P.act(sv("ekd"), sv("t1"), AF.Exp)
        P.act(sv("egc"), sv("gc"), AF.Exp)
        ssq = alloc(P, st, "b_ssq", [128, NT, 8], F32); b_ssq = P.buf()
        rqk = alloc(P, st, "b_rqk", [128, NT, 8], F32); b_rqk = P.buf()
        sqt = [alloc(P, st, f"b_sqt{i}", [128, 8, 128], BF16) for i in range(1)] * 2; b_sqt = [P.buf()] * 2
        P.memset("dve", V(ssq[:], b_ssq), 0.0)
        with ExitStack() as st2:
            PFs = alloc(P, st2, "b_PFs", [128, 12, TP], BF16); b_PFs = [P.mbuf() for _ in range(3)]
            gcv = alloc(P, st2, "b_gcv", [128, 12, 5], F32); b_gcv = P.buf()
            dg = alloc(P, st2, "b_dg", [128, 60, 128], BF16); b_dg = P.buf()
            P.dma(V(gcv[:], b_gcv), V(Wd["gconv"], P.buf()))
            P.memset("pool", V(PFs[:, :, 0:2], b_PFs), 0.0)
            P.memset("pool", V(PFs[:, :, 258:262], b_PFs), 0.0)
            P.memset("pool", V(PFs[:, :, 2310:2312], b_PFs), 0.0)
            pfv = io["PF"][0:1536, :].rearrange("(fc p) t -> p fc t", p=128)
            for fc in range(12):
                P.dma(V(PFs[:, fc, 2:258], b_PFs[fc // 4]), V(pfv[:, fc, 2:258], io["b_PF"]))
                P.dma(V(PFs[:, fc, 262:2310], b_PFs[fc // 4]), V(pfv[:, fc, 262:2310], io["b_PF"]))
            for fc in range(12):
                for j in range(5):
                    P.scale("act" if (fc + j) % 2 else "dve", V(dg[:, fc * 5 + j, :], b_dg), V(C.idb[:], C.b_idb), V(gcv[:, fc, j:j + 1], b_gcv))
            for g in range(3):
                for t in range(NT if stop != 3 else 0):
                    o0 = tok_off(t)
                    pm, bpm = nps()
                    for f4 in range(4):
                        fc = g * 4 + f4
                        for j in range(5):
                            P.mm(V(pm[:, f4, :], bpm), V(PFs[:, fc, o0 + j - 2:o0 + j - 2 + 128], b_PFs[g]), V(dg[:, fc * 5 + j, :], b_dg),
                                 start=(j == 0), stop=(j == 4))
                    P.act(V(qkvc[:, t, g * 512:(g + 1) * 512], b_qkvc[t]), V(pm[:].rearrange("p a b -> p (a b)"), bpm), AF.Silu)
                    if g == 1:
                        sq = sqt[0]; bsq = b_sqt[0]
                        for h8 in range(8):
                            P.act(V(sq[:, h8, :], bsq), V(qkvc[:, t, h8 * 128:(h8 + 1) * 128], b_qkvc[t]), AF.Square, accum=V(ssq[:, t, h8:h8 + 1], b_ssq))
        P.barrier()
        if "dbg" in io and False:
            for tt_ in range(2):
                stg0 = alloc(P, st, f"b_dstq{tt_}", [128, 512], F32); bs0 = P.buf()
                P.copy("dve", V(stg0[:], bs0), V(qkvc[:, tt_ * 8, 0:512], b_qkvc[tt_ * 8]))
                P.dma(V(io["dbg"][:, 22 + tt_, :], io["b_dbg"]), V(stg0[:], bs0))
        if stop == 6:
            return
        P.ts("dve", V(ssq[:], b_ssq), V(ssq[:], b_ssq), 1e-6, ALU.add)
        P.act(V(rqk[:], b_rqk), V(ssq[:], b_ssq), AF.Ln)
        P.act(V(rqk[:], b_rqk), V(rqk[:], b_rqk), AF.Exp, scale=-0.5)
        P.ts("dve", V(rqk[:, :, 0:4], b_rqk), V(rqk[:, :, 0:4], b_rqk), 128.0 ** -0.5, ALU.mult)
        if "dbg" in io:
            P.dma(V(io["dbg"][:, 20, 0:NS], io["b_dbg"]), V(ssq[:].rearrange("p a b -> p (a b)"), b_ssq))
            P.dma(V(io["dbg"][:, 21, 0:NS], io["b_dbg"]), V(rqk[:].rearrange("p a b -> p (a b)"), b_rqk))
        for d in range(2):
            sl = slice(4 * d, 4 * d + 4)
            P.tt("dve", V(S_["cq"][0][:, :, sl], S_["cq"][1]), V(S_["egc"][0][:, :, sl], S_["egc"][1]), V(rqk[:, :, 0:4], b_rqk), ALU.mult)
            P.tt("dve", V(S_["ck1"][0][:, :, sl], S_["ck1"][1]), V(S_["egc"][0][:, :, sl], S_["egc"][1]), V(rqk[:, :, 4:8], b_rqk), ALU.mult)
            P.tt("dve", V(S_["ck2"][0][:, :, sl], S_["ck2"][1]), V(S_["ekd"][0][:, :, sl], S_["ekd"][1]), V(rqk[:, :, 4:8], b_rqk), ALU.mult)
        P.tt("dve", sv("bck2"), sv("beta"), sv("ck2"), ALU.mult)
        NW = 4
        def wt(name, dt=BF16):
            return [(alloc(P, st, f"b_w_{name}{i}", [128, 4, 128], dt), P.buf()) for i in range(NW)]
        NW = 4
        SLOTS = ["Dk", "Dq", "Dkg", "Dqd", "khT", "qhT", "kgT", "qdT", "E1", "E2", "M", "Mn", "Md", "Mdn", "Mo1", "Mo2", "Ya", "Aqk", "kd"]
        ALIAS = {"Pa": "Dk", "Pb": "Dq", "Pna": "Dkg", "Pnb": "Dqd", "Yn": "khT", "Wm": "qhT", "r": "E1", "vn": "E2", "Yb": "M"}
        Wt = {n: wt(n) for n in SLOTS}
        for a_, b_ in ALIAS.items():
            Wt[a_] = Wt[b_]
        Wf = {n: wt(n, F32) for n in ["Dgc", "NG2", "ost"]}
        Sst2 = [(alloc(P, st, f"b_S{i}", [128, 4, 128], F32), P.buf()) for i in range(2)]
        Sb2 = [(alloc(P, st, f"b_Sb{i}", [128, 4, 128], BF16), P.buf()) for i in range(2)]
        idb4 = V(C.idb[:].unsqueeze(1).to_broadcast([128, 4, 128]), C.b_idb)
        idf4 = V(C.idf[:].unsqueeze(1).to_broadcast([128, 4, 128]), C.b_idf)
        it = [0]
        ev = [0]

        def evac(out, pm_v):
            ev[0] += 1
            P.copy("act" if ev[0] % 3 else "dve", out, pm_v)

        dcount = [0]; dstg = []
        def dump(v):
            if "dbg" not in io or dcount[0] >= 24:
                return
            if dcount[0] == 0:
                dstg.append((alloc(P, st, "b_dstg", [128, 4, 128], F32), P.buf()))
            stg, bs = dstg[0]
            P.copy("dve", V(stg[:], bs), v)
            P.dma(V(io["dbg"][:, dcount[0], :], io["b_dbg"]), V(stg[:].rearrange("p a b -> p (a b)"), bs))
            dcount[0] += 1

        def bc(name, t, d):
            a, b_ = S_[name]
            return V(a[:, t, 4 * d:4 * d + 4].unsqueeze(2).to_broadcast([128, 4, 128]), b_)

        def mask4(k):
            return V(gm[:, k, :].unsqueeze(1).to_broadcast([128, 4, 128]), b_gm)

        def maskb4(k):
            return V(gmb[:, k, :].unsqueeze(1).to_broadcast([128, 4, 128]), b_gmb)

        def mm4(lhs, rhs, lhs_k=None):
            pm, bpm = nps()
            for h in range(4):
                P.mm(V(pm[:, h, :], bpm), V(lhs[0][:, h, :], lhs[1]), V(rhs[0][:, h, :], rhs[1]))
            return V(pm[:], bpm)

        def tr4(src):
            pm, bpm = nps()
            pb = pm[:].rearrange("p a b -> p (a b)").bitcast(BF16)
            for h in range(4):
                P.transpose(V(pb[:, h * 128:(h + 1) * 128], bpm), V(src[0][:, h, :], src[1]), V(C.idb[:], C.b_idb))
            return V(pb[:, 0:512].rearrange("p (h i) -> p h i", i=128), bpm)


        def body(d, n_, t, w):
            X = {n: (Wt[n][w][0], Wt[n][w][1]) for n in Wt}
            Xf = {n: (Wf[n][w][0], Wf[n][w][1]) for n in Wf}
            xv = lambda n: V(X[n][0][:], X[n][1])
            xfv = lambda n: V(Xf[n][0][:], Xf[n][1])
            Sst, b_S = Sst2[d]; Sb, b_Sb = Sb2[d]
            rqb = V(rqk[:, t, 0:4].unsqueeze(2).to_broadcast([128, 4, 128]), b_rqk)
            rkb = V(rqk[:, t, 4:8].unsqueeze(2).to_broadcast([128, 4, 128]), b_rqk)
            P.tt("pool", xv("Dk"), idb4, rkb, ALU.mult)
            P.tt("pool", xv("Dq"), idb4, rqb, ALU.mult)
            yield
            P.tt("pool", xv("Dkg"), idb4, bc("ck1", t, d), ALU.mult)
            P.tt("pool", xv("Dqd"), idb4, bc("cq", t, d), ALU.mult)
            yield
            P.tt("pool", xfv("Dgc"), idf4, bc("gc", t, d), ALU.mult)
            P.tt("pool", xfv("NG2"), mask4(0 + 2 * d), bc("ngc2", t, d), ALU.add)
            yield
            qc = (qkvc[:, t, 0:512].rearrange("p (h d) -> p h d", d=128), b_qkvc[t])
            kc = (qkvc[:, t, 512:1024].rearrange("p (h d) -> p h d", d=128), b_qkvc[t])
            vc = (qkvc[:, t, 1024:1536].rearrange("p (h d) -> p h d", d=128), b_qkvc[t])
            evac(xv("khT"), mm4(kc, X["Dk"]))
            yield
            evac(xv("qhT"), mm4(qc, X["Dq"]))
            yield
            evac(xv("kgT"), mm4(kc, X["Dkg"]))
            yield
            evac(xv("qdT"), mm4(qc, X["Dqd"]))
            yield
            for nm, ng in (("E2", "NG2"),):
                pm, bpm = nps()
                pmf_ = pm[:].rearrange("p a b -> p (a b)")
                P.mm(V(pmf_, bpm), V(C.onesf[:], C.b_onesf), V(Xf["Dgc"][0][:].rearrange("p a b -> p (a b)"), Xf["Dgc"][1]), start=True, stop=False)
                P.mm(V(pmf_, bpm), V(C.idf[:], C.b_idf), V(Xf[ng][0][:].rearrange("p a b -> p (a b)"), Xf[ng][1]), start=False, stop=True)
                P.act(xv(nm), V(pm[:], bpm), AF.Exp)
                yield
            G = mm4(X["khT"], X["khT"])
            P.tt("dve", xv("E1"), G, xv("E2"), ALU.mult)
            P.tt("dve", xv("M"), xv("E1"), bc("negb", t, d), ALU.mult)
            yield
            QK = mm4(X["khT"], X["qhT"])
            P.tt("dve", xv("Aqk"), QK, xv("E2"), ALU.mult)
            yield
            evac(xv("Mn"), tr4(X["M"]))
            P.tt("pool", xv("Md"), xv("M"), maskb4(0), ALU.mult)
            P.tt("pool", xv("Ya"), xv("Md"), idb4, ALU.add)
            yield
            P.tt("pool", xv("Mdn"), xv("Mn"), maskb4(0), ALU.mult)
            P.tt("pool", xv("Mo1"), xv("Mn"), maskb4(1), ALU.mult)
            P.tt("pool", xv("Mo2"), xv("Mn"), maskb4(2), ALU.mult)
            yield
            Pc, Pn, Yc = "Md", "Mdn", "Ya"
            Pnext, Pnnext, Ynext = ["Pa", "Pb"], ["Pna", "Pnb"], ["Yb", "Ya"]
            for k in range(1, 5):
                pk_n = Pnnext[k % 2]
                evac(xv(pk_n), mm4(X[Pc], X[Pn]))
                yield
                if k < 4:
                    pk = Pnext[k % 2]
                    evac(xv(pk), mm4(X[Pn], X[Pc]))
                    yield
                yk = Ynext[(k - 1) % 2]
                P.tt("dve", xv(yk), mm4(X[pk_n], X[Yc]), xv(Yc), ALU.add)
                yield
                Pn = pk_n
                if k < 4:
                    Pc = pk
                Yc = yk
            for mo in ("Mo1", "Mo2"):
                evac(xv("Yn"), tr4(X[Yc]))
                yield
                evac(xv("Wm"), mm4(X[mo], X[Yc]))
                yield
                yk = "Ya" if Yc == "Yb" else "Yb"
                P.tt("dve", xv(yk), mm4(X["Yn"], X["Wm"]), xv(Yc), ALU.add)
                yield
                Yc = yk
            if n_ == 0:
                P.memset("pool", V(Sst[:], b_S), 0.0)
                P.memset("pool", V(Sb[:], b_Sb), 0.0)
            SbT = (Sb, b_Sb)
            P.stt("dve", xv("r"), mm4(X["kgT"], SbT), -1.0, V(vc[0], vc[1]), ALU.mult, ALU.add)
            yield
            vp = mm4(X[Yc], X["r"])
            P.tt("dve", xv("vn"), vp, bc("beta", t, d), ALU.mult)
            P.tt("dve", xv("kd"), vp, bc("bck2", t, d), ALU.mult)
            yield
            pm, bpm = nps()
            for h in range(4):
                P.mm(V(pm[:, h, :], bpm), V(X["qdT"][0][:, h, :], X["qdT"][1]), V(Sb[:, h, :], b_Sb), start=True, stop=False)
                P.mm(V(pm[:, h, :], bpm), V(X["Aqk"][0][:, h, :], X["Aqk"][1]), V(X["vn"][0][:, h, :], X["vn"][1]), start=False, stop=True)
            P.copy("act", xfv("ost"), V(pm[:], bpm))
            P.dma(V(io["OA"][t * 128:(t + 1) * 128, d * 512:(d + 1) * 512], io["b_OA"]),
                  V(Xf["ost"][0][:].rearrange("p h v -> p (h v)"), Xf["ost"][1]), eng="sp")
            yield
            Sp = mm4(kc, X["kd"])
            for h_ in range(4):
                P.stt("dve", V(Sst[:, h_, :], b_S), V(Sst[:, h_, :], b_S), V(S_["egt"][0][:, t, 4 * d + h_:4 * d + h_ + 1], S_["egt"][1]),
                      V(Sp.ap[:, h_, :], Sp.bufs), ALU.mult, ALU.add)
            P.copy("act", V(Sb[:], b_Sb), V(Sst[:], b_S))
            yield

        def stream(d, par):
            order = list(range(NT)) if d == 0 else [1, 0] + list(range(NT - 1, 1, -1))
            for n_, t in enumerate(order if stop != 2 else order[:1]):
                if n_ % 2 == par:
                    yield from body(d, n_, t, 2 * d + par)

        LROUND = 33
        streams = []
        if stop != 1:
            streams = [[stream(0, 0), 0], [stream(1, 0), 0], [stream(0, 1), LROUND // 2 + 1], [stream(1, 1), LROUND // 2 + 1]]
        rnd = 0
        while streams:
            for sd in list(streams):
                if rnd < sd[1]:
                    continue
                try:
                    next(sd[0])
                except StopIteration:
                    streams.remove(sd)
            rnd += 1
    P.barrier()


CSTOP = [0]


def mla_perm():
    idx = []
    for h in range(4):
        base = h * 96
        idx += list(range(base, base + 64)) + list(range(base + 64, base + 96, 2)) + list(range(base + 65, base + 96, 2))
    return np.array(idx)


def stage_C(P, C, l, Wd, io, do_ctx):
    with ExitStack() as st:
        pp = [alloc(P, st, f"c_pp{i}", [128, 512], F32, psum=True) for i in range(7)]; b_pp = [P.buf() for _ in range(7)]
        ppi = [0]

        def nps3():
            k = ppi[0] % 7; ppi[0] += 1
            return pp[k], b_pp[k]
        wq1f = alloc(P, st, "c_wq1f", [128, 384], F32); wq2f = alloc(P, st, "c_wq2f", [64, 384], F32); wkf = alloc(P, st, "c_wkf", [128, 512], F32)
        gq1 = alloc(P, st, "c_gq1", [128, 1], F32); gq2 = alloc(P, st, "c_gq2", [64, 1], F32); gk = alloc(P, st, "c_gk", [128, 1], F32)
        wq1 = alloc(P, st, "c_wq1", [128, 384], BF16); wq2 = alloc(P, st, "c_wq2", [128, 384], BF16); wk = alloc(P, st, "c_wk", [128, 512], BF16)
        bw = P.mbuf(); bwb = P.buf()
        P.dma(V(wq1f[:], bw), V(Wd["wqb"][0:128, :], P.buf())); P.dma(V(wq2f[:], bw), V(Wd["wqb"][128:192, :], P.buf()))
        P.dma(V(wkf[:], bw), V(Wd["wkvb"], P.buf()))
        P.dma(V(gq1[:], bw), V(Wd["gq"][0:128, :], P.buf())); P.dma(V(gq2[:], bw), V(Wd["gq"][128:192, :], P.buf()))
        P.dma(V(gk[:], bw), V(Wd["gkv"], P.buf()))
        P.ts("dve", V(wq1[:], bwb), V(wq1f[:], bw), V(gq1[:], bw), ALU.mult)
        P.memset("dve", V(wq2[:], bwb), 0.0)
        P.ts("dve", V(wq2[0:64, :], bwb), V(wq2f[:], bw), V(gq2[:], bw), ALU.mult)
        P.ts("dve", V(wk[:], bwb), V(wkf[:], bw), V(gk[:], bw), ALU.mult)
        rp = alloc(P, st, "c_rope", [128, 16, 48], F32); b_rp = P.buf()
        P.dma(V(rp[:], b_rp), V(Wd["rope"], P.buf()))
        kT = alloc(P, st, "c_kT", [96, 4, T], BF16); b_kT = [P.buf() for _ in range(NT)]
        qT = alloc(P, st, "c_qT", [96, 4, T], BF16); b_qT = [P.buf() for _ in range(NT)]
        Va = alloc(P, st, "c_Va", [128, NT, 4, 66], BF16); b_Va = [P.buf() for _ in range(NT)]
        P.memset("pool", V(Va[:], b_Va), 1.0)
        qa = [alloc(P, st, f"c_qa{i}", [128, 352], F32) for i in range(4)]; b_qa = [P.buf() for _ in range(4)]
        stt_ = alloc(P, st, "c_st", [128, NT, 8], F32); b_stl = [P.buf() for _ in range(NT)]
        P.memset("dve", V(stt_[:], b_stl), 0.0)
        junk = alloc(P, st, "c_junk", [128, 192], BF16); b_junk = P.buf()
        qn = [alloc(P, st, f"c_qn{i}", [128, 320], BF16) for i in range(4)]; b_qn = [P.buf() for _ in range(4)]
        qnT = [alloc(P, st, f"c_qnT{i}", [128, 3, 128], BF16) for i in range(4)]; b_qnT = [P.buf() for _ in range(4)]
        for i_ in range(4):
            P.memset("pool", V(qnT[i_][:], b_qnT[i_]), 0.0)
        qf = [alloc(P, st, f"c_qf{i}", [128, 4, 96], F32) for i in range(4)]; b_qf = [P.buf() for _ in range(4)]
        kpe = [alloc(P, st, f"c_kpe{i}", [128, 32], F32) for i in range(4)]; b_kpe = [P.buf() for _ in range(4)]
        tmp = [alloc(P, st, f"c_tmp{i}", [128, 4, 4, 16], F32) for i in range(4)]; b_tmp = [P.buf() for _ in range(4)]
        qtok = [alloc(P, st, f"c_qtok{i}", [128, 4, 96], BF16) for i in range(4)]; b_qtok = [P.buf() for _ in range(4)]
        ktok = [alloc(P, st, f"c_ktok{i}", [128, 4, 96], BF16) for i in range(4)]; b_ktok = [P.buf() for _ in range(4)]
        def c1_body(t):
                w = t % 4
                P.dma(V(qa[w][:], b_qa[w]), V(io["PT"][t * 128:(t + 1) * 128, 528:880], io["b_PT"]))
                P.act(V(junk[:, 0:192], b_junk), V(qa[w][:, 0:192], b_qa[w]), AF.Square, accum=V(stt_[:, t, 0:1], b_stl[t]))
                P.act(V(junk[:, 0:128], b_junk), V(qa[w][:, 192:320], b_qa[w]), AF.Square, accum=V(stt_[:, t, 1:2], b_stl[t]))
                P.ts("dve", V(stt_[:, t, 2:3], b_stl[t]), V(stt_[:, t, 0:1], b_stl[t]), 1.0 / 192, ALU.mult, 1e-6, ALU.add)
                P.ts("dve", V(stt_[:, t, 3:4], b_stl[t]), V(stt_[:, t, 1:2], b_stl[t]), 1.0 / 128, ALU.mult, 1e-6, ALU.add)
                P.act(V(stt_[:, t, 4:6], b_stl[t]), V(stt_[:, t, 2:4], b_stl[t]), AF.Ln)
                P.act(V(stt_[:, t, 6:8], b_stl[t]), V(stt_[:, t, 4:6], b_stl[t]), AF.Exp, scale=-0.5)
                P.ts("dve", V(qn[w][:, 0:192], b_qn[w]), V(qa[w][:, 0:192], b_qa[w]), V(stt_[:, t, 6:7], b_stl[t]), ALU.mult)
                P.ts("dve", V(qn[w][:, 192:320], b_qn[w]), V(qa[w][:, 192:320], b_qa[w]), V(stt_[:, t, 7:8], b_stl[t]), ALU.mult)
                yield
                pm, bpm = nps3()
                pb = pm[:].bitcast(BF16)
                P.transpose(V(pb[:, 0:128], bpm), V(qn[w][:, 0:128], b_qn[w]), V(C.idb[:], C.b_idb))
                P.transpose(V(pb[0:64, 128:256], bpm), V(qn[w][:, 128:192], b_qn[w]), V(C.idb[:], C.b_idb))
                P.transpose(V(pb[:, 256:384], bpm), V(qn[w][:, 192:320], b_qn[w]), V(C.idb[:], C.b_idb))
                P.copy("act", V(qnT[w][:, 0, :], b_qnT[w]), V(pb[:, 0:128], bpm))
                P.copy("act", V(qnT[w][0:64, 1, :], b_qnT[w]), V(pb[0:64, 128:256], bpm))
                P.copy("act", V(qnT[w][:, 2, :], b_qnT[w]), V(pb[:, 256:384], bpm))
                if CSTOP[0] == 1:
                    return
                yield
                pq, bpq = nps3()
                P.mm(V(pq[:, 0:384], bpq), V(qnT[w][:, 0, :], b_qnT[w]), V(wq1[:], bwb), start=True, stop=False)
                P.mm(V(pq[:, 0:384], bpq), V(qnT[w][:, 1, :], b_qnT[w]), V(wq2[:], bwb), start=False, stop=True)
                if CSTOP[0] == 5:
                    return
                pk, bpk = nps3()
                P.mm(V(pk[:], bpk), V(qnT[w][:, 2, :], b_qnT[w]), V(wk[:], bwb))
                yield
                pq4 = pq[:, 0:384].rearrange("p (h d) -> p h d", d=96)
                pk4 = pk[:].rearrange("p (h d) -> p h d", d=128)
                if CSTOP[0] == 6:
                    return
                P.copy("dve", V(Va[:, t, :, 0:64], b_Va[t]), V(pk4[:, :, 64:128], bpk))
                if CSTOP[0] == 7:
                    return
                P.copy("dve", V(ktok[w][:, :, 0:64], b_ktok[w]), V(pk4[:, :, 0:64], bpk))
                if CSTOP[0] == 2:
                    return
                if t < 2:
                    P.copy("act", V(qtok[w][:], b_qtok[w]), V(pq4, bpq))
                    P.copy("dve", V(ktok[w][:, :, 64:80], b_ktok[w]), V(qa[w][:, 320:352:2].unsqueeze(1).to_broadcast([128, 4, 16]), b_qa[w]))
                    P.copy("dve", V(ktok[w][:, :, 80:96], b_ktok[w]), V(qa[w][:, 321:352:2].unsqueeze(1).to_broadcast([128, 4, 16]), b_qa[w]))
                else:
                    P.copy("act", V(qf[w][:], b_qf[w]), V(pq4, bpq))
                    P.copy("dve", V(qtok[w][:, :, 0:64], b_qtok[w]), V(qf[w][:, :, 0:64], b_qf[w]))
                    cosb = V(rp[:, t - 2, 0:16].unsqueeze(1).to_broadcast([128, 4, 16]), b_rp)
                    sinb = V(rp[:, t - 2, 16:32].unsqueeze(1).to_broadcast([128, 4, 16]), b_rp)
                    nsinb = V(rp[:, t - 2, 32:48].unsqueeze(1).to_broadcast([128, 4, 16]), b_rp)
                    x0 = V(qf[w][:, :, 64:80], b_qf[w]); x1 = V(qf[w][:, :, 80:96], b_qf[w])
                    tm = tmp[w]; btm = b_tmp[w]
                    P.tt("dve", V(tm[:, 0], btm), x0, cosb, ALU.mult)
                    P.tt("dve", V(tm[:, 1], btm), x1, nsinb, ALU.mult)
                    P.tt("dve", V(tm[:, 2], btm), x0, sinb, ALU.mult)
                    P.tt("dve", V(tm[:, 3], btm), x1, cosb, ALU.mult)
                    P.tt("dve", V(qtok[w][:, :, 64:80], b_qtok[w]), V(tm[:, 0], btm), V(tm[:, 1], btm), ALU.add)
                    P.tt("dve", V(qtok[w][:, :, 80:96], b_qtok[w]), V(tm[:, 2], btm), V(tm[:, 3], btm), ALU.add)
                    k0 = V(qa[w][:, 320:352:2], b_qa[w]); k1 = V(qa[w][:, 321:352:2], b_qa[w])
                    c1 = V(rp[:, t - 2, 0:16], b_rp); s1 = V(rp[:, t - 2, 16:32], b_rp); n1 = V(rp[:, t - 2, 32:48], b_rp)
                    kp = kpe[w]; bkp = b_kpe[w]
                    P.tt("dve", V(kp[:, 0:16], bkp), k0, c1, ALU.mult)
                    P.stt("dve", V(kp[:, 0:16], bkp), k1, 1.0, V(kp[:, 0:16], bkp), ALU.mult, ALU.add) if False else None
                    P.tt("dve", V(tm[:, 0, 0, :], btm), k1, n1, ALU.mult)
                    P.tt("dve", V(kp[:, 0:16], bkp), V(kp[:, 0:16], bkp), V(tm[:, 0, 0, :], btm), ALU.add)
                    P.tt("dve", V(kp[:, 16:32], bkp), k0, s1, ALU.mult)
                    P.tt("dve", V(tm[:, 1, 0, :], btm), k1, c1, ALU.mult)
                    P.tt("dve", V(kp[:, 16:32], bkp), V(kp[:, 16:32], bkp), V(tm[:, 1, 0, :], btm), ALU.add)
                    P.copy("dve", V(ktok[w][:, :, 64:96], b_ktok[w]), V(kp[:].unsqueeze(1).to_broadcast([128, 4, 32]), bkp))
                if CSTOP[0] == 3:
                    return
                yield
                for (src, bsrc, dstT, bdst) in ((ktok[w], b_ktok[w], kT, b_kT[t]), (qtok[w], b_qtok[w], qT, b_qT[t])):
                    pt_, bpt = nps3()
                    ptb = pt_[:].bitcast(BF16)
                    for h in range(4):
                        P.transpose(V(ptb[0:96, h * 128:(h + 1) * 128], bpt), V(src[:, h, :], bsrc), V(C.idb[:], C.b_idb))
                    P.copy("act" if dstT is kT else "dve", V(dstT[:, :, t * 128:(t + 1) * 128], bdst),
                           V(ptb[0:96, 0:512].rearrange("p (h i) -> p h i", i=128), bpt))

        def c1_stream(sidx):
            for t in range(NT):
                if t % 4 == sidx:
                    yield from c1_body(t)
                    yield
        c1s = [[c1_stream(i), 2 * i] for i in range(4)]
        rnd = 0
        while c1s:
            for sd in list(c1s):
                if rnd < sd[1]:
                    continue
                try:
                    next(sd[0])
                except StopIteration:
                    c1s.remove(sd)
            rnd += 1
        if CSTOP[0] in (1, 2, 3, 4, 5, 6, 7):
            return
        pT = [alloc(P, st, f"c_pT{i}", [128, 512], BF16) for i in range(4)]; b_pT = [P.buf() for _ in range(4)]
        SCB = (3, 4, 1, 2)
        ob = [alloc(P, st, f"c_ob{i}", [128, 4, 256], F32) for i in range(2)]; b_ob = [P.buf(), P.buf()]
        rec = alloc(P, st, "c_rec", [128, 64], F32); b_rec = P.buf()
        scale = 96.0 ** -0.5
        groups = [(2 + 4 * g, 4, list(range(NT))) for g in range(4)]
        if do_ctx:
            groups.append((0, 2, [0, 1]))
        oTs = [alloc(P, st, f"c_oTs{i}", [65, 512], F32) for i in range(2)]; b_oTs = [P.buf(), P.buf()]
        rec4 = [alloc(P, st, f"c_rec4{i}", [128, 4], F32) for i in range(2)]; b_rec4 = [P.buf(), P.buf()]
        its = []
        for gi, (t0, nq, kts) in enumerate(groups):
            for h in range(4):
                for ki, kt in enumerate(kts):
                    its.append((gi, t0, nq, h, ki, kt, len(kts)))

        def score(i):
            gi, t0, nq, h, ki, kt, nk = its[i]
            ps, bps = pp[SCB[i % 4]], b_pp[SCB[i % 4]]
            P.mm(V(ps[:, 0:nq * 128], bps), V(kT[:, h, kt * 128:(kt + 1) * 128], b_kT[kt]),
                 V(qT[:, h, t0 * 128:(t0 + nq) * 128], b_qT[t0:t0 + nq]))

        score(0)
        for i, (gi, t0, nq, h, ki, kt, nk) in enumerate(its):
            nqc = nq * 128
            gh = gi * 4 + h
            obw = ob[gi % 2]; bobw = b_ob[gi % 2]
            acc, bacc = pp[5 + gh % 2], b_pp[5 + gh % 2]
            ps, bps = pp[SCB[i % 4]], b_pp[SCB[i % 4]]
            pt_ = pT[i % 4]; bpt = b_pT[i % 4]
            P.act(V(pt_[:, 0:nqc], bpt), V(ps[:, 0:nqc], bps), AF.Exp, scale=scale)
            if i + 1 < len(its):
                score(i + 1)
            P.mm(V(acc[0:65, 0:nqc], bacc), V(Va[:, kt, h, 0:65], b_Va[kt]), V(pt_[:, 0:nqc], bpt),
                 start=(ki == 0), stop=(ki == nk - 1))
            if ki == nk - 1:
                ot = oTs[gh % 2]; bot = b_oTs[gh % 2]
                P.copy("dve", V(ot[:, 0:nqc], bot), V(acc[0:65, 0:nqc], bacc))
                ptr_, bptr = pp[0], b_pp[0]
                p4 = ptr_[:].rearrange("p (a b) -> p a b", b=128)
                for qi in range(nq):
                    P.transpose(V(p4[:, qi, 0:65], bptr), V(ot[:, qi * 128:(qi + 1) * 128], bot), V(C.idf[0:65, 0:65], C.b_idf))
                r4 = rec4[gh % 2]; br4 = b_rec4[gh % 2]
                P.recip(V(r4[:, 0:nq].unsqueeze(2), br4), V(p4[:, 0:nq, 64:65], bptr))
                P.tt("dve", V(obw[:, 0:nq, h * 64:(h + 1) * 64], bobw), V(p4[:, 0:nq, 0:64], bptr),
                     V(r4[:, 0:nq].unsqueeze(2).to_broadcast([128, nq, 64]), br4), ALU.mult)
                if h == 3:
                    for qi in range(nq):
                        tt_ = t0 + qi
                        P.dma(V(io["OB"][tt_ * 128:(tt_ + 1) * 128, :], io["b_OB"]), V(obw[:, qi, :], bobw), eng="sp")
    P.barrier()


def stage_D(P, C, l, Wd, io, last):
    with ExitStack() as st:
        mod = alloc(P, st, "d_mod", [128, 24, 2], F32); b_mod = P.buf()
        P.dma(V(mod[:], b_mod), V(io["modo"], io["b_modo"]))
        pg = [alloc(P, st, f"d_pg{i}", [128, 512], F32, psum=True) for i in range(2)]; b_pg = [P.buf(), P.buf()]
        pcv = alloc(P, st, "d_pcv", [128, 256], F32, psum=True); b_pcv = P.buf()
        ptr = alloc(P, st, "d_ptr", [128, 8, 128], BF16, psum=True); b_ptr = P.buf()
        pw = [alloc(P, st, f"d_pw{i}", [128, 512], F32, psum=True) for i in range(2)]; b_pw = [P.buf(), P.buf()]
        Dg = [alloc(P, st, f"d_Dg{i}", [128, 128], F32) for i in range(2)]; b_Dg = [P.buf(), P.buf()]
        Wo = [alloc(P, st, f"d_Wo{i}", [128, 8, 1024], BF16) for i in range(2)]; b_Wo = [P.buf(), P.buf()]
        wf = [alloc(P, st, f"d_wf{i}", [128, 1024], F32) for i in range(2)]; b_wf = [P.buf(), P.buf()]
        hc = alloc(P, st, "d_hc", [128, 4, TP], BF16); b_hc = P.mbuf()
        prod = alloc(P, st, "d_prod", [128, 2, TP], BF16); b_prod = P.buf()
        ccv = alloc(P, st, "d_ccv", [128, 2, 3], F32); b_ccv = P.buf()
        dgc = alloc(P, st, "d_dgc", [128, 6, 128], BF16); b_dgc = P.buf()
        gng = alloc(P, st, "d_gng", [128, 4, 128], F32); b_gng = P.buf()
        P.dma(V(ccv[:], b_ccv), V(Wd["cconv"], P.buf()))
        P.dma(V(gng[:], b_gng), V(Wd["gng"], P.buf()))
        P.memset("pool", V(hc[:, :, 0:2], b_hc), 0.0)
        P.memset("pool", V(hc[:, :, 258:262], b_hc), 0.0)
        P.memset("pool", V(hc[:, :, 2310:2312], b_hc), 0.0)
        pfv = io["PF"][1536:2048, :].rearrange("(fc p) t -> p fc t", p=128)
        for fc in range(4):
            P.dma(V(hc[:, fc, 2:258], b_hc), V(pfv[:, fc, 2:258], io["b_PF"]))
            P.dma(V(hc[:, fc, 262:2310], b_hc), V(pfv[:, fc, 262:2310], io["b_PF"]))
        wov = Wd["w_out"].rearrange("(kc p) n -> p kc n", p=128)
        n = 0
        for w in ((0,) if last else (0, 1)):
            for j in range(8):
                dgt = Dg[j % 2]; bd = b_Dg[j % 2]
                P.scale("act", V(dgt[:], bd), V(C.idf[:], C.b_idf), V(mod[:, 16 + j, w:w + 1], b_mod))
                P.mm(V(pg[j // 4][:, (j % 4) * 128:(j % 4 + 1) * 128], b_pg[j // 4]), V(C.onesf[:], C.b_onesf), V(dgt[:], bd))
            for kc in range(8):
                wb = wf[n % 2]; bw = b_wf[n % 2]; n += 1
                P.dma(V(wb[:], bw), V(wov[:, kc, :], P.buf()))
                for hf in range(2):
                    P.tt("dve", V(Wo[w][:, kc, hf * 512:(hf + 1) * 512], b_Wo[w]), V(wb[:, hf * 512:(hf + 1) * 512], bw), V(pg[hf][:], b_pg[hf]), ALU.mult)
        for fc in range(2):
            P.tt("dve", V(prod[:, fc, :], b_prod), V(hc[:, fc, :], b_hc), V(hc[:, 2 + fc, :], b_hc), ALU.mult)
            for j in range(3):
                P.ts("dve", V(dgc[:, fc * 3 + j, :], b_dgc), V(C.idb[:], C.b_idb), V(ccv[:, fc, j:j + 1], b_ccv), ALU.mult)
        if last:
            fng = alloc(P, st, "d_fng", [128, 1024], F32); b_fng = P.buf()
            P.dma(V(fng[:], b_fng), V(Wd["fng"], P.buf()))
        NB = 3
        def mk(name, shape, dt):
            return [(alloc(P, st, f"d_{name}{i}", shape, dt), P.buf()) for i in range(NB)]
        B_ = {"oa": mk("oa", [128, 4, 128], F32), "oa2": mk("oa2", [128, 1024], F32), "ob": mk("ob", [128, 256], F32), "za": mk("za", [128, 512], F32), "zbg": mk("zbg", [128, 512], F32),
              "zc": mk("zc", [128, 256], F32), "x": mk("x", [128, 1024], F32), "sza": mk("sza", [128, 512], F32), "szb": mk("szb", [128, 256], F32),
              "szc": mk("szc", [128, 256], F32), "mix": mk("mix", [128, 1024], BF16), "mixT": mk("mixT", [128, 8, 128], BF16),
              "a1": mk("a1", [128, 4, 128], F32), "a2": mk("a2", [128, 4, 128], F32), "tc": mk("tc", [128, 256], F32), "xn": mk("xn", [128, 1024], F32),
              "y": mk("y", [128, 1024], F32)}
        stt_ = alloc(P, st, "d_st", [128, NT, 16], F32); b_st = P.buf()
        junk = alloc(P, st, "d_junk", [128, 1024], BF16); b_junk = P.buf()
        fin = alloc(P, st, "d_fin", [128, NT, 4], F32); b_fin = P.buf()
        PT = io["PT"]
        b_stt = [P.buf() for _ in range(NT)]; b_fint = [P.buf() for _ in range(NT)]
        P.memset("dve", V(stt_[:], b_stt), 0.0)
        P.memset("dve", V(fin[:], b_fint), 0.0)
        ptr2 = alloc(P, st, "d_ptr2", [128, 8, 128], BF16, psum=True); b_ptr2 = P.buf()
        ptr3 = pg[0][:].bitcast(BF16).rearrange("p (a b) -> p a b", b=128)
        PS = [(ptr, b_ptr, [pw[0], pw[0]], [b_pw[0], b_pw[0]]), (ptr2, b_ptr2, [pw[1], pw[1]], [b_pw[1], b_pw[1]]),
              (ptr3, b_pg[0], [pg[1], pg[1]], [b_pg[1], b_pg[1]])]

        def body(it, t):
            w = it % NB
            ptr_, b_ptr_, pw_, b_pw_ = PS[w]
            X = {k: v[w] for k, v in B_.items()}
            xv = lambda k: V(X[k][0][:], X[k][1])
            bst = b_stt[t]; bfin = b_fint[t]
            r = slice(t * 128, (t + 1) * 128)
            P.dma(xv("oa2"), V(io["OA"][r, :], io["b_OA"]))
            P.dma(xv("ob"), V(io["OB"][r, :], io["b_OB"]))
            P.dma(xv("za"), V(PT[r, 0:512], io["b_PT"]))
            P.dma(xv("zbg"), V(PT[r, 880:1392], io["b_PT"]))
            P.dma(xv("zc"), V(PT[r, 1392:1648], io["b_PT"]))
            P.dma(xv("x"), V(io["xin"][r, :], io["b_xin"]))
            yield
            P.tt("dve", V(X["oa"][0][:].rearrange("p h v -> p (h v)"), X["oa"][1]), V(X["oa2"][0][:, 0:512], X["oa2"][1]), V(X["oa2"][0][:, 512:1024], X["oa2"][1]), ALU.add)
            P.act(xv("sza"), xv("za"), AF.Silu)
            P.act(xv("szb"), V(X["zbg"][0][:, 0:256], X["zbg"][1]), AF.Silu)
            P.act(xv("szc"), xv("zc"), AF.Silu)
            yield
            for h in range(4):
                P.act(V(junk[:, 0:128], b_junk), V(X["oa"][0][:, h, :], X["oa"][1]), AF.Square, accum=V(stt_[:, t, h:h + 1], bst))
            P.ts("dve", V(stt_[:, t, 4:8], bst), V(stt_[:, t, 0:4], bst), 1.0 / 128, ALU.mult, 1e-6, ALU.add)
            P.act(V(stt_[:, t, 8:12], bst), V(stt_[:, t, 4:8], bst), AF.Ln)
            P.act(V(stt_[:, t, 12:16], bst), V(stt_[:, t, 8:12], bst), AF.Exp, scale=-0.5)
            yield
            rb = V(stt_[:, t, 12:16].unsqueeze(2).to_broadcast([128, 4, 128]), bst)
            P.tt("dve", xv("a1"), xv("oa"), rb, ALU.mult)
            P.tt("dve", xv("a2"), xv("a1"), V(gng[:], b_gng), ALU.mult)
            P.tt("dve", V(X["mix"][0][:, 0:512], X["mix"][1]), V(X["a2"][0][:].rearrange("p h v -> p (h v)"), X["a2"][1]), xv("sza"), ALU.mult)
            yield
            P.tt("dve", V(X["mix"][0][:, 512:768], X["mix"][1]), xv("ob"), xv("szb"), ALU.mult)
            o0 = tok_off(t)
            for fc in range(2):
                for j in range(3):
                    P.mm(V(pcv[:, fc * 128:(fc + 1) * 128], b_pcv), V(prod[:, fc, o0 + j - 1:o0 + j - 1 + 128], b_prod), V(dgc[:, fc * 3 + j, :], b_dgc),
                         start=(j == 0), stop=(j == 2))
            P.tt("dve", xv("tc"), V(X["zbg"][0][:, 256:512], X["zbg"][1]), xv("szc"), ALU.mult)
            P.tt("dve", V(X["mix"][0][:, 768:1024], X["mix"][1]), V(pcv[:], b_pcv), xv("tc"), ALU.mult)
            yield
            for j in range(8):
                P.transpose(V(ptr_[:, j, :], b_ptr_), V(X["mix"][0][:, j * 128:(j + 1) * 128], X["mix"][1]), V(C.idb[:], C.b_idb))
            P.copy("act", xv("mixT"), V(ptr_[:], b_ptr_))
            yield
            wsel = 1 if t < 2 else 0
            for hf in range(2):
                for kc in range(8):
                    P.mm(V(pw_[hf][:], b_pw_[hf]), V(X["mixT"][0][:, kc, :], X["mixT"][1]), V(Wo[wsel][:, kc, hf * 512:(hf + 1) * 512], b_Wo[wsel]),
                         start=(kc == 0), stop=(kc == 7))
                P.tt("dve", V(X["xn"][0][:, hf * 512:(hf + 1) * 512], X["xn"][1]), V(pw_[hf][:], b_pw_[hf]), V(X["x"][0][:, hf * 512:(hf + 1) * 512], X["x"][1]), ALU.add)
                yield
            if not last:
                P.dma(V(io["xout"][r, :], io["b_xout"]), xv("xn"), eng="sp")
            else:
                P.act(V(junk[:], b_junk), xv("xn"), AF.Square, accum=V(fin[:, t, 0:1], bfin))
                P.ts("dve", V(fin[:, t, 1:2], bfin), V(fin[:, t, 0:1], bfin), 1.0 / 1024, ALU.mult, 1e-6, ALU.add)
                P.act(V(fin[:, t, 2:3], bfin), V(fin[:, t, 1:2], bfin), AF.Ln)
                P.act(V(fin[:, t, 3:4], bfin), V(fin[:, t, 2:3], bfin), AF.Exp, scale=-0.5)
                yield
                P.stt("dve", xv("y"), xv("xn"), V(fin[:, t, 3:4], bfin), V(fng[:], b_fng), ALU.mult, ALU.mult)
                ro = slice((t - 2) * 128, (t - 1) * 128)
                P.dma(V(io["out"][ro, :], io["b_out"]), xv("y"), eng="sp")
            yield

        tiles = list(range(2 if last else 0, NT))

        def stream(sidx):
            for it, t in enumerate(tiles):
                if it % NB == sidx:
                    yield from body(it, t)

        streams = [[stream(i), 3 * i] for i in range(NB)]
        rnd = 0
        while streams:
            for sd in list(streams):
                if rnd < sd[1]:
                    continue
                try:
                    next(sd[0])
                except StopIteration:
                    streams.remove(sd)
            rnd += 1
    P.barrier()


def prep_consts():
    sel = np.zeros((2, TP), np.float32)
    sel[1, 2:258] = 1.0; sel[0, 262:2310] = 1.0
    return {"ident": np.eye(128, dtype=np.float32), "sel": sel}


def fm(v, n):
    return np.ascontiguousarray(v.reshape(n, 128).T)


def prep_layer_inputs(inp, b, l):
    d = {}
    cc = np.stack([fm(inp["c"][b], 8), fm(inp["c_ctx"], 8)], -1)
    d["cc"] = np.ascontiguousarray(cc)
    ba = fm(inp["b_ada"][l], 24)
    d[f"bada{l}"] = np.ascontiguousarray(np.stack([ba, ba], -1))
    g = fm(inp["norm_g"][l], 8)
    d[f"ng{l}"] = np.ascontiguousarray(np.stack([g, g], -1))
    d[f"w_ada{l}"] = inp["w_ada"][l]
    d[f"w_in{l}"] = inp["w_in"][l]
    return d


def prep_layer_inputs_b(inp, l):
    d = {}
    gc = inp["gdn_conv"][l]
    d[f"gconv{l}"] = np.ascontiguousarray(gc.T.reshape(12, 128, 5).transpose(1, 0, 2))
    d[f"alog{l}"] = np.ascontiguousarray(np.broadcast_to(inp["gdn_a_log"][l].reshape(1, 1, 8), (128, NT, 8))).astype(np.float32)
    d[f"dtb{l}"] = np.ascontiguousarray(np.broadcast_to(inp["gdn_dt_bias"][l].reshape(1, 1, 8), (128, NT, 8))).astype(np.float32)
    return d


def rope_host():
    rows = 2048 // 64
    r = np.repeat(np.arange(rows), 64); col = np.tile(np.arange(64), rows)
    inv = (10000.0 ** (-np.arange(8, dtype=np.float32) / 8)).astype(np.float32)
    ang = np.concatenate([r[:, None] * inv, col[:, None] * inv], -1).astype(np.float32)
    cs, sn = np.cos(ang), np.sin(ang)
    tab = np.concatenate([cs, sn, -sn], -1).astype(np.float32)
    return np.ascontiguousarray(tab.reshape(16, 128, 48).transpose(1, 0, 2))


def prep_layer_inputs_cd(inp, l):
    d = {}
    d[f"wqb{l}"] = np.ascontiguousarray(inp["mla_w_qb"][l][:, mla_perm()])
    d[f"gq{l}"] = np.ascontiguousarray(inp["mla_q_norm_g"][l].reshape(192, 1))
    d[f"wkvb{l}"] = inp["mla_w_kvb"][l]
    d[f"gkv{l}"] = np.ascontiguousarray(inp["mla_kv_norm_g"][l].reshape(128, 1))
    d[f"w_out{l}"] = inp["w_out"][l]
    d[f"gng{l}"] = np.ascontiguousarray(np.broadcast_to(inp["gdn_norm_g"][l].reshape(1, 1, 128), (128, 4, 128))).astype(np.float32)
    d[f"cconv{l}"] = np.ascontiguousarray(inp["conv_w"][l].T.reshape(2, 128, 3).transpose(1, 0, 2))
    return d


KSTOP = [99]
DBG = [False]


def build_program():
    nc = bass.Bass("TRN2", target_bir_lowering=False)
    P = Prog(nc)

    def din(name, shape, dt=F32):
        return nc.dram_tensor(name, list(shape), dt, kind="ExternalInput").ap()

    def dint(name, shape, dt=F32):
        return nc.dram_tensor(name, list(shape), dt, kind="ExternalOutput" if DBG[0] else "Internal").ap()
    cin = {"ident": din("ident", [128, 128]), "sel": din("sel", [2, TP])}
    gmask = din("gmask", [128, 9, 128]); rope = din("rope", [128, 16, 48]); fng = din("fng", [128, 1024])
    xin0 = din("xin0", [T, 1024]); cc = din("cc", [128, 8, 2])
    out = nc.dram_tensor("out", [2048, 1024], F32, kind="ExternalOutput").ap()
    with ExitStack() as st:
        P.init_bar(st)
        C = load_consts(P, st, cin)
        xin = xin0; b_xin = P.buf()
        wada_ap = [din(f"w_ada{l}", [1024, 3072]) for l in range(2)]
        bada_ap = [din(f"bada{l}", [128, 24, 2]) for l in range(2)]
        modpre = dint("modpre1", [128, 24, 2]); b_modpre = P.buf()
        for l in range(2):
            WA = {"cc": cc, "w_ada": wada_ap[l], "bada": bada_ap[l], "ng": din(f"ng{l}", [128, 8, 2]),
                  "w_in": din(f"w_in{l}", [1024, 3696])}
            io = {"PF": dint(f"PF{l}", [2048, TP], BF16), "b_PF": P.mbuf(), "PT": dint(f"PT{l}", [T, PT_W]), "b_PT": P.mbuf(),
                  "modo": dint(f"modo{l}", [128, 24, 2]), "b_modo": P.buf(), "OA": dint(f"OA{l}", [T, 1024]), "b_OA": P.mbuf(),
                  "OB": dint(f"OB{l}", [T, 256]), "b_OB": P.mbuf(), "xin": xin, "b_xin": b_xin,
                  "xout": dint(f"xmid{l}", [T, 1024]), "b_xout": P.mbuf(), "out": out, "b_out": P.mbuf()}
            if l == 1 and KSTOP[0] >= 99:
                io["modo"] = modpre; io["b_modo"] = b_modpre
            if KSTOP[0] >= 4 * l + 1:
                if KSTOP[0] >= 99:
                    stage_A(P, C, l, xin, WA, io,
                            nxt=({"w_ada": wada_ap[1], "bada": bada_ap[1], "modo": modpre, "b_modo": b_modpre} if l == 0 else None),
                            pre=((modpre, b_modpre) if l == 1 else None))
                else:
                    stage_A(P, C, l, xin, WA, io)
            if KSTOP[0] >= 4 * l + 2:
              stage_B(P, C, l, {"gconv": din(f"gconv{l}", [128, 12, 5]), "alog": din(f"alog{l}", [128, NT, 8]), "dtb": din(f"dtb{l}", [128, NT, 8]),
                              "gmask": gmask}, io)
            if KSTOP[0] >= 4 * l + 3:
              stage_C(P, C, l, {"wqb": din(f"wqb{l}", [192, 384]), "gq": din(f"gq{l}", [192, 1]), "wkvb": din(f"wkvb{l}", [128, 512]),
                              "gkv": din(f"gkv{l}", [128, 1]), "rope": rope}, io, do_ctx=(l == 0))
            if KSTOP[0] >= 4 * l + 4:
              stage_D(P, C, l, {"w_out": din(f"w_out{l}", [1024, 1024]), "gng": din(f"gng{l}", [128, 4, 128]), "cconv": din(f"cconv{l}", [128, 2, 3]),
                              "fng": fng}, io, last=(l == 1))
            xin = io["xout"]; b_xin = io["b_xout"]
        P.emit()
    return nc


def make_in_maps(inp, cores):
    shared = {**prep_consts(), **gdn_consts_host(), "rope": rope_host(),
              "fng": np.ascontiguousarray(np.broadcast_to(inp["final_norm_g"].reshape(1, 1024), (128, 1024))).astype(np.float32)}
    for l in range(2):
        shared.update(prep_layer_inputs_b(inp, l)); shared.update(prep_layer_inputs_cd(inp, l))
    maps = []
    for b in cores:
        m = dict(shared)
        for l in range(2):
            m.update(prep_layer_inputs(inp, b, l))
        m["xin0"] = np.ascontiguousarray(np.concatenate([inp["ctx"][b], inp["x"][b]], 0))
        maps.append(m)
    return maps


def kernel(**inputs):
    inp = {k: np.asarray(v, dtype=np.float32) for k, v in inputs.items()}
    nc = build_program()
    maps = make_in_maps(inp, list(range(8)))
    res = run_bass_kernel_spmd(nc, maps, core_ids=list(range(8)))
    return np.stack([np.asarray(r["out"], dtype=np.float32) for r in res.results], 0)
```

```python
from contextlib import ExitStack
import numpy as np
import concourse.bass as bass
import concourse.mybir as mybir
from concourse.bass_utils import run_bass_kernel_spmd

F32 = mybir.dt.float32
BF16 = mybir.dt.bfloat16
AF = mybir.ActivationFunctionType
ALU = mybir.AluOpType
AX = mybir.AxisListType

EPOCH = 1000
STRICT_SAME_ENGINE = False
DMA_ROT = 12


class Buf:
    __slots__ = ("name", "last_w", "readers", "multi", "writers")

    def __init__(self, name, multi=False):
        self.name = name
        self.last_w = None
        self.readers = []
        self.multi = multi
        self.writers = []


class V:
    __slots__ = ("ap", "bufs")

    def __init__(self, ap, bufs):
        self.ap = ap
        self.bufs = bufs if isinstance(bufs, (list, tuple)) else [bufs]


class Op:
    __slots__ = ("eng", "fn", "reads", "writes", "waits", "signal", "count", "is_dma", "dsem", "dval", "prewait")

    def __init__(self, eng, fn, reads, writes, is_dma=False):
        self.eng = eng; self.fn = fn; self.reads = reads; self.writes = writes
        self.waits = []; self.signal = False; self.count = None
        self.is_dma = is_dma; self.dsem = None; self.dval = None; self.prewait = None


ENGS = ("pe", "act", "dve", "pool", "sp")


class Prog:
    def __init__(self, nc):
        self.nc = nc
        self.ops = []
        self.per_eng = {e: [] for e in ENGS}
        self.nbuf = 0
        self.dma_n = {e: 0 for e in ENGS}
        self.dma_last = {}

    def init_bar(self, stack):
        self._bar_sb = stack.enter_context(self.nc.sbuf_tensor("bar_sb", [128, 8], F32))
        self._bar_ps = stack.enter_context(self.nc.psum_tensor("bar_ps", [128, 512], F32))
        self._bar_tok = {e: self.buf("bartok_" + e) for e in ("act", "dve", "pool")}
        self._bar_init = self.buf("barinit")
        self.add("dve", lambda e: e.memset(self._bar_sb[:], 0.0), [], [V(None, [self._bar_init] + list(self._bar_tok.values()))])

    def buf(self, name=None):
        self.nbuf += 1
        return Buf(name or f"b{self.nbuf}")

    def mbuf(self, name=None):
        self.nbuf += 1
        return Buf(name or f"m{self.nbuf}", multi=True)

    def sb(self, name, shape, dtype):
        t = self.nc.alloc_sbuf_tensor(name, list(shape), dtype)
        return t

    def ps(self, name, shape, dtype=F32):
        return self.nc.alloc_psum_tensor(name, list(shape), dtype)

    def add(self, eng, fn, reads, writes, is_dma=False):
        rb = [b for v in reads for b in v.bufs]
        wb = [b for v in writes for b in v.bufs]
        op = Op(eng, fn, rb, wb, is_dma)
        deps = []
        for b in rb:
            if b.multi:
                for w_ in b.writers:
                    deps.append((w_, "raw"))
            elif b.last_w is not None:
                deps.append((b.last_w, "raw"))
        for b in wb:
            if b.multi and is_dma:
                pass
            elif b.multi:
                for w_ in b.writers:
                    deps.append((w_, "waw"))
            elif b.last_w is not None:
                deps.append((b.last_w, "waw"))
            for r in b.readers:
                deps.append((r, "war"))
        seen = set()
        for d, kind in deps:
            if d is op or id(d) in seen:
                continue
            if d.eng == eng and not d.is_dma:
                if eng == "pe" or (kind != "raw" and not STRICT_SAME_ENGINE):
                    continue
            seen.add(id(d))
            d.signal = True
            op.waits.append(d)
        for b in rb:
            b.readers.append(op)
        for b in wb:
            if b.multi:
                if b.readers or not is_dma:
                    b.writers = []
                b.writers.append(op)
            b.last_w = op
            b.readers = []
        self.ops.append(op)
        self.per_eng[eng].append(op)
        return op

    def emit(self):
        nc = self.nc
        nsig = {e: 0 for e in ENGS}
        dma_ops = {e: [] for e in ENGS}
        for op in self.ops:
            if op.is_dma:
                dma_ops[op.eng].append(op)
            elif op.signal:
                nsig[op.eng] += 1
                op.count = nsig[op.eng]
        self.nsig = nsig
        sems = {}
        for e in ENGS:
            n = (nsig[e] + EPOCH - 1) // EPOCH
            sems[e] = [nc.alloc_semaphore(f"s_{e}_{i}") for i in range(max(n, 1))]
        dsems = {}
        for e in ENGS:
            if dma_ops[e]:
                dsems[e] = [nc.alloc_semaphore(f"d_{e}_{i}") for i in range(DMA_ROT)]
                cnt = [0] * DMA_ROT
                for i, op in enumerate(dma_ops[e]):
                    k = i % DMA_ROT
                    if cnt[k] > 0:
                        op.prewait = (dsems[e][k], 16 * cnt[k])
                    cnt[k] += 1
                    op.dsem = dsems[e][k]; op.dval = 16 * cnt[k]
        handles = {"pe": nc.tensor, "act": nc.scalar, "dve": nc.vector, "pool": nc.gpsimd, "sp": nc.sync}

        def emit_engine(ename, eng):
            known = {}
            for op in self.per_eng[ename]:
                if op.prewait is not None:
                    eng.wait_ge(op.prewait[0], op.prewait[1])
                for d in op.waits:
                    if d.is_dma:
                        key = ("d", id(d.dsem)); val = d.dval; sem = d.dsem
                    else:
                        ep = (d.count - 1) // EPOCH
                        key = (d.eng, ep); val = d.count - ep * EPOCH; sem = sems[d.eng][ep]
                    if known.get(key, 0) >= val:
                        continue
                    known[key] = val
                    eng.wait_ge(sem, val)
                if op.fn is None:
                    continue
                ins = op.fn(eng)
                if op.is_dma:
                    ins.then_inc(op.dsem, 16)
                elif op.signal:
                    ep = (op.count - 1) // EPOCH
                    ins.then_inc(sems[ename][ep], 1)
            if ename in dsems:
                last = {}
                for op in dma_ops[ename]:
                    last[id(op.dsem)] = (op.dsem, op.dval)
                for sem, val in last.values():
                    eng.wait_ge(sem, val)

        with nc.Block() as block:
            @block.tensor
            def _(e):
                emit_engine("pe", e)

            @block.scalar
            def _(e):
                emit_engine("act", e)

            @block.vector
            def _(e):
                emit_engine("dve", e)

            @block.gpsimd
            def _(e):
                emit_engine("pool", e)

            @block.sync
            def _(e):
                emit_engine("sp", e)

    def barrier(self):
        toks = []
        if not hasattr(self, "_bar_tok"):
            self._bar_tok = {e: self.buf("bartok_" + e) for e in ("act", "dve", "pool")}
        for e in ("pe", "act", "dve", "pool"):
            b = self.buf("bar_" + e)
            if e == "pe":
                self.add("pe", lambda en: en.matmul(self._bar_ps[0:1, 0:1], self._bar_sb[0:1, 0:1], self._bar_sb[0:1, 0:1], start=True, stop=True), [V(None, self._bar_init)], [V(None, b)])
            else:
                i = ("act", "dve", "pool").index(e) + 1
                tk_ = self._bar_tok[e]
                self.add(e, (lambda i, e_: (lambda en: en.memzero(self._bar_sb[0:1, i:i + 1]) if e_ == 'act' else en.memset(self._bar_sb[0:1, i:i + 1], 0.0)))(i, e),
                         [V(None, tk_)], [V(None, [b, tk_])])
            toks.append(b)
        dm = []
        for q in ENGS:
            dm += [op for op in self.per_eng[q] if op.is_dma][-DMA_ROT:]
        for e in ENGS:
            op = self.add(e, None, [V(None, toks)], [])
            op.eng = e
            for d in dm:
                if d not in op.waits:
                    op.waits.append(d)

    def dma(self, out, in_, eng="sp", **kw):
        return self.add(eng, lambda e: e.dma_start(out=out.ap, in_=in_.ap, **kw), [in_], [out], is_dma=True)

    def mm(self, out, lhsT, rhs, start=True, stop=True, extra_reads=()):
        reads = [lhsT, rhs] + list(extra_reads)
        return self.add("pe", lambda e: e.matmul(out.ap, lhsT.ap, rhs.ap, start=start, stop=stop), reads, [out])

    def transpose(self, out, in_, ident):
        return self.add("pe", lambda e: e.transpose(out.ap, in_.ap, ident.ap), [in_, ident], [out])

    def act(self, out, in_, func, bias=None, scale=None, accum=None, eng="sp"):
        reads = [in_]
        kw = {}
        if bias is not None:
            if isinstance(bias, V):
                reads.append(bias); kw["bias"] = bias.ap
            else:
                kw["bias"] = bias
        if scale is not None:
            if isinstance(scale, V):
                reads.append(scale); kw["scale"] = scale.ap
            else:
                kw["scale"] = scale
        writes = [out]
        if accum is not None:
            writes.append(accum); kw["accum_out"] = accum.ap
        return self.add("act", lambda e: e.activation(out.ap, in_.ap, func, **kw), reads, writes)

    def tt(self, eng, out, a, b, op):
        return self.add(eng, lambda e: e.tensor_tensor(out.ap, a.ap, b.ap, op), [a, b], [out])

    def ts(self, eng, out, a, s1, op0, s2=None, op1=None, accum=None):
        reads = [a]
        s1a = s1.ap if isinstance(s1, V) else s1
        s2a = s2.ap if isinstance(s2, V) else s2
        if isinstance(s1, V): reads.append(s1)
        if isinstance(s2, V): reads.append(s2)
        writes = [out]
        kw = {}
        if op1 is not None:
            kw["op1"] = op1
        if accum is not None:
            writes.append(accum); kw["accum_out"] = accum.ap
        return self.add(eng, lambda e: e.tensor_scalar(out.ap, a.ap, s1a, s2a, op0, **kw), reads, writes)

    def stt(self, eng, out, a, s, b, op0, op1):
        reads = [a, b]
        sa = s.ap if isinstance(s, V) else s
        if isinstance(s, V): reads.append(s)
        return self.add(eng, lambda e: e.scalar_tensor_tensor(out.ap, a.ap, sa, b.ap, op0, op1), reads, [out])

    def scale(self, eng, out, in_, sc):
        if eng == "act":
            return self.act(out, in_, AF.Identity, scale=sc)
        return self.ts(eng, out, in_, sc, ALU.mult)

    def copy(self, eng, out, in_):
        if eng == "act":
            return self.add("act", lambda e: e.copy(out.ap, in_.ap), [in_], [out])
        return self.add(eng, lambda e: e.tensor_copy(out.ap, in_.ap), [in_], [out])

    def memset(self, eng, out, val):
        return self.add(eng, lambda e: e.memset(out.ap, val), [], [out])

    def recip(self, out, in_):
        return self.add("dve", lambda e: e.reciprocal(out.ap, in_.ap), [in_], [out])

    def reduce(self, eng, out, in_, op, axis=AX.X):
        return self.add(eng, lambda e: e.tensor_reduce(out.ap, in_.ap, axis, op), [in_], [out])


T = 2304; NT = 18; D = 1024; TP = 2312
PT_W = 1648


def tok_off(t):
    return 2 + t * 128 if t < 2 else 262 + (t - 2) * 128


W_CHUNKS = [
    ("tok", 512, [(2048, 368)]),
    ("feat", 0, [(0, 512)]),
    ("feat", 512, [(512, 512)]),
    ("feat", 1024, [(1024, 512)]),
    ("tok", 0, [(1536, 512)]),
    ("feat", 1536, [(2672, 256), (3184, 256)]),
    ("tok", 880, [(2416, 256), (2928, 256)]),
    ("tok", 1392, [(3440, 256)]),
]


class Ctx:
    pass


_ALLOC_N = [0]


def alloc(P, stack, name, shape, dtype, psum=False):
    _ALLOC_N[0] += 1
    name = f"{name}_{_ALLOC_N[0]}"
    if psum:
        nbytes = int(np.prod(shape[1:])) * (2 if dtype == BF16 else 4)
        assert nbytes <= 2048
        if nbytes < 2048:
            full = stack.enter_context(P.nc.psum_tensor(name, [128, 512], F32))
            v = full[:]
            if dtype == BF16:
                v = v.bitcast(BF16)
            n = int(np.prod(shape[1:]))
            v = v[0:shape[0], 0:n]
            if len(shape) == 3:
                v = v.rearrange("p (a b) -> p a b", b=shape[2])
            return v
        t = stack.enter_context(P.nc.psum_tensor(name, list(shape), dtype))
    else:
        t = stack.enter_context(P.nc.sbuf_tensor(name, list(shape), dtype))
    return t


def load_consts(P, stack, cin):
    C = Ctx()
    C.idf = alloc(P, stack, "c_idf", [128, 128], F32); C.b_idf = P.buf()
    C.idb = alloc(P, stack, "c_idb", [128, 128], BF16); C.b_idb = P.buf()
    C.onesf = alloc(P, stack, "c_onesf", [128, 128], F32); C.b_onesf = P.buf()
    C.self_ = alloc(P, stack, "c_self", [2, 256], F32); C.b_self = P.buf()
    P.dma(V(C.idf[:], C.b_idf), V(cin["ident"], P.buf()))
    P.copy("dve", V(C.idb[:], C.b_idb), V(C.idf[:], C.b_idf))
    P.memset("pool", V(C.onesf[:], C.b_onesf), 1.0)
    P.dma(V(C.self_[:], C.b_self), V(cin["sel"], P.buf()))
    return C


def stage_A(P, C, l, xin, W, out):
    nc = P.nc
    with ExitStack() as st:
        wbuf = [alloc(P, st, f"a_wbuf{i}", [128, 8, 512], F32) for i in range(3)]
        b_wbuf = [P.buf(), P.buf(), P.buf()]
        cc = alloc(P, st, "a_cc", [128, 8, 2], F32); b_cc = P.buf()
        scc = alloc(P, st, "a_scc", [128, 8, 2], F32); b_scc = P.buf()
        bada = alloc(P, st, "a_bada", [128, 24, 2], F32); b_bada = P.buf()
        ng = alloc(P, st, "a_ng", [128, 8, 2], F32); b_ng = P.buf()
        mod = alloc(P, st, "a_mod", [128, 24, 2], F32); b_mod = P.buf()
        Asc = alloc(P, st, "a_Asc", [128, 8, 2], F32); b_Asc = P.buf()
        hT = alloc(P, st, "a_hT", [128, 8, TP], BF16); b_hT = [P.buf() for _ in range(NT)]
        ps_mod = alloc(P, st, "a_psmod", [128, 24, 2], F32, psum=True); b_psmod = P.buf()
        ps_b = alloc(P, st, "a_psb", [2, 512], F32, psum=True); b_psb = P.buf()
        modrow = alloc(P, st, "a_modrow", [2, 3072], F32); b_modrow = P.buf()
        P.dma(V(cc[:], b_cc), V(W["cc"], P.buf()))
        P.dma(V(bada[:], b_bada), V(W["bada"], P.buf()))
        P.dma(V(ng[:], b_ng), V(W["ng"], P.buf()))
        P.act(V(scc[:], b_scc), V(cc[:], b_cc), AF.Silu)
        wada = W["w_ada"].rearrange("(kc p) n -> p kc n", p=128)
        xt = [alloc(P, st, f"a_xt{i}", [128, 1024], F32) for i in range(3)]; b_xt = [P.buf() for _ in range(3)]
        junk = alloc(P, st, "a_junk", [128, 1024], BF16); b_junk = P.buf()
        yb = [alloc(P, st, f"a_yb{i}", [128, 1024], BF16) for i in range(2)]; b_yb = [P.buf() for _ in range(2)]
        stt_ = alloc(P, st, "a_st", [128, NT, 4], F32); b_st = [P.buf() for _ in range(NT)]
        ps_t = [alloc(P, st, f"a_pst{i}", [128, 8, 128], BF16, psum=True) for i in range(2)]; b_pst = [P.buf() for _ in range(2)]
        def gen_mod():
            for ch in range(6):
                wb = wbuf[ch % 2]; bw = b_wbuf[ch % 2]
                P.dma(V(wb[:], bw), V(wada[:, :, ch * 512:(ch + 1) * 512], P.buf()))
                for kc in range(8):
                    P.mm(V(ps_b[:, :], b_psb), V(scc[:, kc, :], b_scc), V(wb[:, kc, :], bw), start=(kc == 0), stop=(kc == 7))
                P.copy("act", V(modrow[:, ch * 512:(ch + 1) * 512], b_modrow), V(ps_b[:, :], b_psb))
                yield
                for o4 in range(4):
                    oc = ch * 4 + o4
                    P.mm(V(ps_mod[:, oc, :], b_psmod), V(modrow[:, oc * 128:(oc + 1) * 128], b_modrow), V(C.idf[0:2, 0:2], C.b_idf))
                yield
            P.tt("dve", V(mod[:], b_mod), V(ps_mod[:], b_psmod), V(bada[:], b_bada), ALU.add)
            P.dma(V(out["modo"], out["b_modo"]), V(mod[:], b_mod))
            P.stt("dve", V(Asc[:], b_Asc), V(mod[:, 8:16, :], b_mod), 1.0, V(ng[:], b_ng), ALU.add, ALU.mult)
            yield
        P.memset("dve", V(stt_[:], b_st), 0.0)
        def gen_norm():
            for t in range(NT):
                x_ = xt[t % 3]; bx = b_xt[t % 3]
                P.dma(V(x_[:], bx), V(xin[t * 128:(t + 1) * 128, :], P.buf()))
                s = stt_[:, t, :]
                P.act(V(junk[:], b_junk), V(x_[:], bx), AF.Square, accum=V(stt_[:, t, 0:1], b_st[t]))
                P.ts("dve", V(stt_[:, t, 1:2], b_st[t]), V(stt_[:, t, 0:1], b_st[t]), 1.0 / 1024, ALU.mult, 1e-6, ALU.add)
                P.act(V(stt_[:, t, 2:3], b_st[t]), V(stt_[:, t, 1:2], b_st[t]), AF.Ln)
                P.act(V(stt_[:, t, 3:4], b_st[t]), V(stt_[:, t, 2:3], b_st[t]), AF.Exp, scale=-0.5)
                y = yb[t % 2]; by = b_yb[t % 2]
                P.scale("dve" if t % 2 == 0 else "act", V(y[:], by), V(x_[:], bx), V(stt_[:, t, 3:4], b_st[t]))
                yield
                pt = ps_t[t % 2]; bp = b_pst[t % 2]
                for j in range(8):
                    P.transpose(V(pt[:, j, :], bp), V(y[:, j * 128:(j + 1) * 128], by), V(C.idb[:], C.b_idb))
                o0 = tok_off(t)
                P.copy("act" if t % 2 == 0 else "dve", V(hT[:, :, o0:o0 + 128], b_hT[t]), V(pt[:], bp))
                yield

        win = W["w_in"].rearrange("(kc p) n -> p kc n", p=128)

        def load_w(ci_):
            segs_ = W_CHUNKS[ci_][2]
            c0 = 0
            wi = (ci_ + 2) % 3
            for (w0, n) in segs_:
                P.dma(V(wbuf[wi][:, :, c0:c0 + n], b_wbuf[wi]), V(win[:, :, w0:w0 + n], P.buf()))
                c0 += n
        load_w(0)
        gens = [[gen_mod(), 1], [gen_norm(), 3]]
        while gens:
            for gd in list(gens):
                try:
                    for _ in range(gd[1]):
                        next(gd[0])
                except StopIteration:
                    gens.remove(gd)
        win = W["w_in"].rearrange("(kc p) n -> p kc n", p=128)
        wx = [alloc(P, st, f"a_wx{i}", [128, 8, 512], BF16) for i in range(2)]; b_wx = [P.buf() for _ in range(2)]
        wc = [alloc(P, st, f"a_wc{i}", [128, 8, 512], BF16) for i in range(2)]; b_wc = [P.buf() for _ in range(2)]
        browf = [alloc(P, st, f"a_browf{i}", [2, 512], F32) for i in range(2)]; b_browf = [P.buf() for _ in range(2)]
        bcol = [alloc(P, st, f"a_bcol{i}", [128, 4, 2], F32) for i in range(2)]; b_bcol = [P.buf() for _ in range(2)]
        bbc = [alloc(P, st, f"a_bbc{i}", [128, 512], F32) for i in range(2)]; b_bbc = [P.buf() for _ in range(2)]
        ps_m = [alloc(P, st, f"a_psm{i}", [128, 512], F32, psum=True) for i in range(3)]; b_psm = [P.buf() for _ in range(3)]
        sgf = [alloc(P, st, f"a_sgf{i}", [128, 512], BF16) for i in range(3)]; b_sgf = [P.buf() for _ in range(3)]
        sgt = [alloc(P, st, f"a_sgt{i}", [128, 512], F32) for i in range(3)]; b_sgt = [P.buf() for _ in range(3)]
        all_hT = b_hT
        nmm = 0
        for ci, (kind, dst, segs) in enumerate(W_CHUNKS):
            wb = wbuf[(ci + 2) % 3]; bw = b_wbuf[(ci + 2) % 3]
            ncol = sum(n for _, n in segs)
            if ci + 1 < len(W_CHUNKS):
                load_w(ci + 1)
            wxx = wx[ci % 2]; bwx = b_wx[ci % 2]; wcc = wc[ci % 2]; bwc = b_wc[ci % 2]
            for kc in range(8):
                P.scale("dve", V(wxx[:, kc, :ncol], bwx), V(wb[:, kc, :ncol], bw), V(Asc[:, kc, 0:1], b_Asc))
                P.scale("act" if kc % 4 else "dve", V(wcc[:, kc, :ncol], bwc), V(wb[:, kc, :ncol], bw), V(Asc[:, kc, 1:2], b_Asc))
            for kc in range(8):
                P.mm(V(ps_b[:, :ncol], b_psb), V(mod[:, kc, :], b_mod), V(wb[:, kc, :ncol], bw), start=(kc == 0), stop=(kc == 7))
            br = browf[ci % 2]; bbr = b_browf[ci % 2]
            P.copy("act", V(br[:, :ncol], bbr), V(ps_b[:, :ncol], b_psb))
            if kind == "feat":
                for fc in range(ncol // 128):
                    P.mm(V(ps_mod[:, fc, :], b_psmod), V(br[:, fc * 128:(fc + 1) * 128], bbr), V(C.idf[0:2, 0:2], C.b_idf))
                P.copy("dve", V(bcol[ci % 2][:], b_bcol[ci % 2]), V(ps_mod[:, 0:4, :], b_psmod))
            else:
                for w_ in range(2):
                    pmb = ps_m[nmm % 3]; bpmb = b_psm[nmm % 3]; nmm += 1
                    selw = C.self_[:, 0:128] if w_ == 0 else C.self_[:, 128:256]
                    P.mm(V(pmb[:, :ncol], bpmb), V(selw, C.b_self), V(br[:, :ncol], bbr))
                    P.copy("act", V(bbc[w_][:, :ncol], b_bbc[w_]), V(pmb[:, :ncol], bpmb))
            if kind == "feat":
                for fc in range(ncol // 128):
                    for (g0, gn, wsel, bw_sel, hbufs) in ([] if (l == 1 and ci == 5) else [(2, 256, wcc, bwc, all_hT[0:2])]) + [
                            (262 + 512 * g, 512, wxx, bwx, all_hT[2 + 4 * g:6 + 4 * g]) for g in range(4)]:
                        pm = ps_m[nmm % 3]; bpm = b_psm[nmm % 3]
                        for kc in range(8):
                            P.mm(V(pm[:, :gn], bpm), V(wsel[:, kc, fc * 128:(fc + 1) * 128], bw_sel), V(hT[:, kc, g0:g0 + gn], hbufs),
                                 start=(kc == 0), stop=(kc == 7))
                        sg = sgf[nmm % 3]; bsg = b_sgf[nmm % 3]
                        wcol = 1 if g0 == 2 else 0
                        P.act(V(sg[:, :gn], bsg), V(pm[:, :gn], bpm), AF.Identity, bias=V(bcol[ci % 2][:, fc, wcol:wcol + 1], b_bcol[ci % 2]))
                        r0 = dst + fc * 128
                        P.dma(V(out["PF"][r0:r0 + 128, g0:g0 + gn], out["b_PF"]), V(sg[:, :gn], bsg), eng="sp")
                        nmm += 1
            else:
                for t in range(NT):
                    if l == 1 and t < 2 and ci in (4, 6, 7):
                        continue
                    o0 = tok_off(t)
                    wsel, bw_sel = (wcc, bwc) if t < 2 else (wxx, bwx)
                    pm = ps_m[nmm % 3]; bpm = b_psm[nmm % 3]
                    for kc in range(8):
                        P.mm(V(pm[:, :ncol], bpm), V(hT[:, kc, o0:o0 + 128], b_hT[t]), V(wsel[:, kc, :ncol], bw_sel), start=(kc == 0), stop=(kc == 7))
                    sg = sgt[nmm % 3]; bsg = b_sgt[nmm % 3]
                    wrow = 1 if t < 2 else 0
                    P.tt("dve", V(sg[:, :ncol], bsg), V(pm[:, :ncol], bpm), V(bbc[wrow][:, :ncol], b_bbc[wrow]), ALU.add)
                    P.dma(V(out["PT"][t * 128:(t + 1) * 128, dst:dst + ncol], out["b_PT"]), V(sg[:, :ncol], bsg), eng="sp")
                    nmm += 1
    P.barrier()


NEG = -1.0e30


def gdn_consts_host():
    i = np.arange(128)
    J, I = np.meshgrid(i, i, indexing="ij")
    d = {}
    d["mC0"] = np.where(I >= J, 0.0, NEG).astype(np.float32)
    d["mS0"] = np.where(I > J, 0.0, NEG).astype(np.float32)
    d["mC1"] = np.where(I <= J, 0.0, NEG).astype(np.float32)
    d["mS1"] = np.where(I < J, 0.0, NEG).astype(np.float32)
    d["triF"] = (J <= I).astype(np.float32)
    d["triB"] = (J >= I).astype(np.float32)
    d["bd32"] = (((J // 32) == (I // 32)) & (J != I)).astype(np.float32)
    d["off64"] = (((J // 64) == (I // 64)) & ((J // 32) != (I // 32))).astype(np.float32)
    d["off128"] = ((J // 64) != (I // 64)).astype(np.float32)
    return {"gmask": np.ascontiguousarray(np.stack([d[k] for k in ("mC0", "mS0", "mC1", "mS1", "triF", "triB", "bd32", "off64", "off128")], 1))}


def stage_B(P, C, l, Wd, io, stop=0):
    nc = P.nc
    with ExitStack() as st:
        gm = alloc(P, st, "b_gm", [128, 9, 128], F32); b_gm = P.buf()
        P.dma(V(gm[:], b_gm), V(Wd["gmask"], P.buf()))
        gmb = alloc(P, st, "b_gmb", [128, 3, 128], BF16); b_gmb = P.buf()
        P.copy("dve", V(gmb[:], b_gmb), V(gm[:, 6:9, :], b_gm))
        qkvc = alloc(P, st, "b_qkvc", [128, NT, 1536], BF16); b_qkvc = [P.buf() for _ in range(NT)]
        pp = [alloc(P, st, f"b_pp{i}", [128, 4, 128], F32, psum=True) for i in range(7)]; b_pp = [P.buf() for _ in range(7)]
        ppi = [0]

        def nps():
            k = ppi[0] % 7; ppi[0] += 1
            return pp[k], b_pp[k]

        if stop in (3, 4):
            return
        NS = NT * 8
        ba = alloc(P, st, "b_ba", [128, NT, 16], F32); b_ba = P.mbuf()
        for t in range(NT):
            P.dma(V(ba[:, t, :], b_ba), V(io["PT"][t * 128:(t + 1) * 128, 512:528], io["b_PT"]))
        alog = alloc(P, st, "b_alog", [128, NT, 8], F32); b_alog = P.buf()
        dtb = alloc(P, st, "b_dtb", [128, NT, 8], F32); b_dtb = P.buf()
        P.dma(V(alog[:], b_alog), V(Wd["alog"], P.buf()))
        P.dma(V(dtb[:], b_dtb), V(Wd["dtb"], P.buf()))
        names = ["beta", "negb", "g", "gc", "ngc2", "egc", "ekd", "cq", "ck1", "ck2", "bck2", "egt", "t1", "t2", "t3"]
        S_ = {}
        for n in names:
            S_[n] = (alloc(P, st, "b_s_" + n, [128, NT, 8], F32), P.buf())
        def sv(n): return V(S_[n][0][:], S_[n][1])
        bv = V(ba[:, :, 0:8], b_ba); av = V(ba[:, :, 8:16], b_ba)
        if stop == 7:
            return
        P.act(sv("t1"), bv, AF.Exp, scale=-1.0)
        P.ts("dve", sv("t2"), sv("t1"), 1.0, ALU.add)
        P.recip(sv("beta"), sv("t2"))
        P.ts("dve", sv("negb"), sv("beta"), -1.0, ALU.mult)
        if stop == 8:
            return
        P.tt("dve", sv("t1"), av, V(dtb[:], b_dtb), ALU.add)
        P.act(sv("t2"), sv("t1"), AF.Exp)
        P.ts("dve", sv("t3"), sv("t2"), 1.0, ALU.add)
        P.act(sv("t1"), sv("t3"), AF.Ln)
        P.act(sv("t2"), V(alog[:], b_alog), AF.Exp)
        P.stt("dve", sv("g"), sv("t1"), -1.0, sv("t2"), ALU.mult, ALU.mult)
        if stop == 5:
            return
        g2 = S_["g"][0][:].rearrange("p t e -> p (t e)")
        pm, bpm = nps(); pm2, bpm2 = nps(); pm3, bpm3 = nps()
        pmf = pm[:].rearrange("p a b -> p (a b)"); pm2f = pm2[:].rearrange("p a b -> p (a b)"); pm3f = pm3[:].rearrange("p a b -> p (a b)")
        P.mm(V(pmf[:, :NS], bpm), V(gm[:, 4, :], b_gm), V(g2, S_["g"][1]))
        P.mm(V(pm2f[:, :NS], bpm2), V(gm[:, 5, :], b_gm), V(g2, S_["g"][1]))
        P.mm(V(pm3f[:, :NS], bpm3), V(C.onesf[:], C.b_onesf), V(g2, S_["g"][1]))
        gc = S_["gc"][0]
        P.copy("dve", V(gc[:, :, 0:4], S_["gc"][1]), V(pmf[:, :NS].rearrange("p (t e) -> p t e", e=8)[:, :, 0:4], bpm))
        P.copy("dve", V(gc[:, :, 4:8], S_["gc"][1]), V(pm2f[:, :NS].rearrange("p (t e) -> p t e", e=8)[:, :, 4:8], bpm2))
        P.copy("dve", sv("t3"), V(pm3f[:, :NS].rearrange("p (t e) -> p t e", e=8), bpm3))
        P.act(sv("egt"), sv("t3"), AF.Exp)
        P.ts("dve", sv("ngc2"), sv("gc"), -1.0, ALU.mult)
        P.tt("dve", sv("t1"), sv("t3"), sv("ngc2"), ALU.add)
        P.act(sv("ekd"), sv("t1"), AF.Exp)
        P.act(sv("egc"), sv("gc"), AF.Exp)
        ssq = alloc(P, st, "b_ssq", [128, NT, 8], F32); b_ssq = P.buf()
        rqk = alloc(P, st, "b_rqk", [128, NT, 8], F32); b_rqk = P.buf()
        sqt = [alloc(P, st, f"b_sqt{i}", [128, 8, 128], BF16) for i in range(1)] * 2; b_sqt = [P.buf()] * 2
        P.memset("dve", V(ssq[:], b_ssq), 0.0)
        with ExitStack() as st2:
            PFs = alloc(P, st2, "b_PFs", [128, 12, TP], BF16); b_PFs = [P.mbuf() for _ in range(3)]
            gcv = alloc(P, st2, "b_gcv", [128, 12, 5], F32); b_gcv = P.buf()
            dg = alloc(P, st2, "b_dg", [128, 60, 128], BF16); b_dg = P.buf()
            P.dma(V(gcv[:], b_gcv), V(Wd["gconv"], P.buf()))
            P.memset("pool", V(PFs[:, :, 0:2], b_PFs), 0.0)
            P.memset("pool", V(PFs[:, :, 258:262], b_PFs), 0.0)
            P.memset("pool", V(PFs[:, :, 2310:2312], b_PFs), 0.0)
            pfv = io["PF"][0:1536, :].rearrange("(fc p) t -> p fc t", p=128)
            for fc in range(12):
                P.dma(V(PFs[:, fc, 2:258], b_PFs[fc // 4]), V(pfv[:, fc, 2:258], io["b_PF"]))
                P.dma(V(PFs[:, fc, 262:2310], b_PFs[fc // 4]), V(pfv[:, fc, 262:2310], io["b_PF"]))
            for fc in range(12):
                for j in range(5):
                    P.scale("act" if (fc + j) % 2 else "dve", V(dg[:, fc * 5 + j, :], b_dg), V(C.idb[:], C.b_idb), V(gcv[:, fc, j:j + 1], b_gcv))
            for g in range(3):
                for t in range(NT if stop != 3 else 0):
                    o0 = tok_off(t)
                    pm, bpm = nps()
                    for f4 in range(4):
                        fc = g * 4 + f4
                        for j in range(5):
                            P.mm(V(pm[:, f4, :], bpm), V(PFs[:, fc, o0 + j - 2:o0 + j - 2 + 128], b_PFs[g]), V(dg[:, fc * 5 + j, :], b_dg),
                                 start=(j == 0), stop=(j == 4))
                    P.act(V(qkvc[:, t, g * 512:(g + 1) * 512], b_qkvc[t]), V(pm[:].rearrange("p a b -> p (a b)"), bpm), AF.Silu)
                    if g == 1:
                        sq = sqt[0]; bsq = b_sqt[0]
                        for h8 in range(8):
                            P.act(V(sq[:, h8, :], bsq), V(qkvc[:, t, h8 * 128:(h8 + 1) * 128], b_qkvc[t]), AF.Square, accum=V(ssq[:, t, h8:h8 + 1], b_ssq))
        P.barrier()
        if "dbg" in io and False:
            for tt_ in range(2):
                stg0 = alloc(P, st, f"b_dstq{tt_}", [128, 512], F32); bs0 = P.buf()
                P.copy("dve", V(stg0[:], bs0), V(qkvc[:, tt_ * 8, 0:512], b_qkvc[tt_ * 8]))
                P.dma(V(io["dbg"][:, 22 + tt_, :], io["b_dbg"]), V(stg0[:], bs0))
        if stop == 6:
            return
        P.ts("dve", V(ssq[:], b_ssq), V(ssq[:], b_ssq), 1e-6, ALU.add)
        P.act(V(rqk[:], b_rqk), V(ssq[:], b_ssq), AF.Ln)
        P.act(V(rqk[:], b_rqk), V(rqk[:], b_rqk), AF.Exp, scale=-0.5)
        P.ts("dve", V(rqk[:, :, 0:4], b_rqk), V(rqk[:, :, 0:4], b_rqk), 128.0 ** -0.5, ALU.mult)
        if "dbg" in io:
            P.dma(V(io["dbg"][:, 20, 0:NS], io["b_dbg"]), V(ssq[:].rearrange("p a b -> p (a b)"), b_ssq))
            P.dma(V(io["dbg"][:, 21, 0:NS], io["b_dbg"]), V(rqk[:].rearrange("p a b -> p (a b)"), b_rqk))
        for d in range(2):
            sl = slice(4 * d, 4 * d + 4)
            P.tt("dve", V(S_["cq"][0][:, :, sl], S_["cq"][1]), V(S_["egc"][0][:, :, sl], S_["egc"][1]), V(rqk[:, :, 0:4], b_rqk), ALU.mult)
            P.tt("dve", V(S_["ck1"][0][:, :, sl], S_["ck1"][1]), V(S_["egc"][0][:, :, sl], S_["egc"][1]), V(rqk[:, :, 4:8], b_rqk), ALU.mult)
            P.tt("dve", V(S_["ck2"][0][:, :, sl], S_["ck2"][1]), V(S_["ekd"][0][:, :, sl], S_["ekd"][1]), V(rqk[:, :, 4:8], b_rqk), ALU.mult)
        P.tt("dve", sv("bck2"), sv("beta"), sv("ck2"), ALU.mult)
        NW = 4
        def wt(name, dt=BF16):
            return [(alloc(P, st, f"b_w_{name}{i}", [128, 4, 128], dt), P.buf()) for i in range(NW)]
        NW = 4
        SLOTS = ["Dk", "Dq", "Dkg", "Dqd", "khT", "qhT", "kgT", "qdT", "E1", "E2", "M", "Mn", "Md", "Mdn", "Mo1", "Mo2", "Ya", "Aqk", "kd"]
        ALIAS = {"Pa": "Dk", "Pb": "Dq", "Pna": "Dkg", "Pnb": "Dqd", "Yn": "khT", "Wm": "qhT", "r": "E1", "vn": "E2", "Yb": "M"}
        Wt = {n: wt(n) for n in SLOTS}
        for a_, b_ in ALIAS.items():
            Wt[a_] = Wt[b_]
        Wf = {n: wt(n, F32) for n in ["Dgc", "NG2", "ost"]}
        Sst2 = [(alloc(P, st, f"b_S{i}", [128, 4, 128], F32), P.buf()) for i in range(2)]
        Sb2 = [(alloc(P, st, f"b_Sb{i}", [128, 4, 128], BF16), P.buf()) for i in range(2)]
        idb4 = V(C.idb[:].unsqueeze(1).to_broadcast([128, 4, 128]), C.b_idb)
        idf4 = V(C.idf[:].unsqueeze(1).to_broadcast([128, 4, 128]), C.b_idf)
        it = [0]
        ev = [0]

        def evac(out, pm_v):
            ev[0] += 1
            P.copy("act" if ev[0] % 3 else "dve", out, pm_v)

        dcount = [0]; dstg = []
        def dump(v):
            if "dbg" not in io or dcount[0] >= 24:
                return
            if dcount[0] == 0:
                dstg.append((alloc(P, st, "b_dstg", [128, 4, 128], F32), P.buf()))
            stg, bs = dstg[0]
            P.copy("dve", V(stg[:], bs), v)
            P.dma(V(io["dbg"][:, dcount[0], :], io["b_dbg"]), V(stg[:].rearrange("p a b -> p (a b)"), bs))
            dcount[0] += 1

        def bc(name, t, d):
            a, b_ = S_[name]
            return V(a[:, t, 4 * d:4 * d + 4].unsqueeze(2).to_broadcast([128, 4, 128]), b_)

        def mask4(k):
            return V(gm[:, k, :].unsqueeze(1).to_broadcast([128, 4, 128]), b_gm)

        def maskb4(k):
            return V(gmb[:, k, :].unsqueeze(1).to_broadcast([128, 4, 128]), b_gmb)

        def mm4(lhs, rhs, lhs_k=None):
            pm, bpm = nps()
            for h in range(4):
                P.mm(V(pm[:, h, :], bpm), V(lhs[0][:, h, :], lhs[1]), V(rhs[0][:, h, :], rhs[1]))
            return V(pm[:], bpm)

        def tr4(src):
            pm, bpm = nps()
            pb = pm[:].rearrange("p a b -> p (a b)").bitcast(BF16)
            for h in range(4):
                P.transpose(V(pb[:, h * 128:(h + 1) * 128], bpm), V(src[0][:, h, :], src[1]), V(C.idb[:], C.b_idb))
            return V(pb[:, 0:512].rearrange("p (h i) -> p h i", i=128), bpm)


        def body(d, n_, t, w):
            X = {n: (Wt[n][w][0], Wt[n][w][1]) for n in Wt}
            Xf = {n: (Wf[n][w][0], Wf[n][w][1]) for n in Wf}
            xv = lambda n: V(X[n][0][:], X[n][1])
            xfv = lambda n: V(Xf[n][0][:], Xf[n][1])
            Sst, b_S = Sst2[d]; Sb, b_Sb = Sb2[d]
            rqb = V(rqk[:, t, 0:4].unsqueeze(2).to_broadcast([128, 4, 128]), b_rqk)
            rkb = V(rqk[:, t, 4:8].unsqueeze(2).to_broadcast([128, 4, 128]), b_rqk)
            P.tt("pool", xv("Dk"), idb4, rkb, ALU.mult)
            P.tt("pool", xv("Dq"), idb4, rqb, ALU.mult)
            yield
            P.tt("pool", xv("Dkg"), idb4, bc("ck1", t, d), ALU.mult)
            P.tt("pool", xv("Dqd"), idb4, bc("cq", t, d), ALU.mult)
            yield
            P.tt("pool", xfv("Dgc"), idf4, bc("gc", t, d), ALU.mult)
            P.tt("pool", xfv("NG2"), mask4(0 + 2 * d), bc("ngc2", t, d), ALU.add)
            yield
            qc = (qkvc[:, t, 0:512].rearrange("p (h d) -> p h d", d=128), b_qkvc[t])
            kc = (qkvc[:, t, 512:1024].rearrange("p (h d) -> p h d", d=128), b_qkvc[t])
            vc = (qkvc[:, t, 1024:1536].rearrange("p (h d) -> p h d", d=128), b_qkvc[t])
            evac(xv("khT"), mm4(kc, X["Dk"]))
            yield
            evac(xv("qhT"), mm4(qc, X["Dq"]))
            yield
            evac(xv("kgT"), mm4(kc, X["Dkg"]))
            yield
            evac(xv("qdT"), mm4(qc, X["Dqd"]))
            yield
            for nm, ng in (("E2", "NG2"),):
                pm, bpm = nps()
                pmf_ = pm[:].rearrange("p a b -> p (a b)")
                P.mm(V(pmf_, bpm), V(C.onesf[:], C.b_onesf), V(Xf["Dgc"][0][:].rearrange("p a b -> p (a b)"), Xf["Dgc"][1]), start=True, stop=False)
                P.mm(V(pmf_, bpm), V(C.idf[:], C.b_idf), V(Xf[ng][0][:].rearrange("p a b -> p (a b)"), Xf[ng][1]), start=False, stop=True)
                P.act(xv(nm), V(pm[:], bpm), AF.Exp)
                yield
            G = mm4(X["khT"], X["khT"])
            P.tt("dve", xv("E1"), G, xv("E2"), ALU.mult)
            P.tt("dve", xv("M"), xv("E1"), bc("negb", t, d), ALU.mult)
            yield
            QK = mm4(X["khT"], X["qhT"])
            P.tt("dve", xv("Aqk"), QK, xv("E2"), ALU.mult)
            yield
            evac(xv("Mn"), tr4(X["M"]))
            P.tt("pool", xv("Md"), xv("M"), maskb4(0), ALU.mult)
            P.tt("pool", xv("Ya"), xv("Md"), idb4, ALU.add)
            yield
            P.tt("pool", xv("Mdn"), xv("Mn"), maskb4(0), ALU.mult)
            P.tt("pool", xv("Mo1"), xv("Mn"), maskb4(1), ALU.mult)
            P.tt("pool", xv("Mo2"), xv("Mn"), maskb4(2), ALU.mult)
            yield
            Pc, Pn, Yc = "Md", "Mdn", "Ya"
            Pnext, Pnnext, Ynext = ["Pa", "Pb"], ["Pna", "Pnb"], ["Yb", "Ya"]
            for k in range(1, 5):
                pk_n = Pnnext[k % 2]
                evac(xv(pk_n), mm4(X[Pc], X[Pn]))
                yield
                if k < 4:
                    pk = Pnext[k % 2]
                    evac(xv(pk), mm4(X[Pn], X[Pc]))
                    yield
                yk = Ynext[(k - 1) % 2]
                P.tt("dve", xv(yk), mm4(X[pk_n], X[Yc]), xv(Yc), ALU.add)
                yield
                Pn = pk_n
                if k < 4:
                    Pc = pk
                Yc = yk
            for mo in ("Mo1", "Mo2"):
                evac(xv("Yn"), tr4(X[Yc]))
                yield
                evac(xv("Wm"), mm4(X[mo], X[Yc]))
                yield
                yk = "Ya" if Yc == "Yb" else "Yb"
                P.tt("dve", xv(yk), mm4(X["Yn"], X["Wm"]), xv(Yc), ALU.add)
                yield
                Yc = yk
            if n_ == 0:
                P.memset("pool", V(Sst[:], b_S), 0.0)
                P.memset("pool", V(Sb[:], b_Sb), 0.0)
            SbT = (Sb, b_Sb)
            P.stt("dve", xv("r"), mm4(X["kgT"], SbT), -1.0, V(vc[0], vc[1]), ALU.mult, ALU.add)
            yield
            vp = mm4(X[Yc], X["r"])
            P.tt("dve", xv("vn"), vp, bc("beta", t, d), ALU.mult)
            P.tt("dve", xv("kd"), vp, bc("bck2", t, d), ALU.mult)
            yield
            pm, bpm = nps()
            for h in range(4):
                P.mm(V(pm[:, h, :], bpm), V(X["qdT"][0][:, h, :], X["qdT"][1]), V(Sb[:, h, :], b_Sb), start=True, stop=False)
                P.mm(V(pm[:, h, :], bpm), V(X["Aqk"][0][:, h, :], X["Aqk"][1]), V(X["vn"][0][:, h, :], X["vn"][1]), start=False, stop=True)
            P.copy("act", xfv("ost"), V(pm[:], bpm))
            P.dma(V(io["OA"][t * 128:(t + 1) * 128, d * 512:(d + 1) * 512], io["b_OA"]),
                  V(Xf["ost"][0][:].rearrange("p h v -> p (h v)"), Xf["ost"][1]), eng="sp")
            yield
            Sp = mm4(kc, X["kd"])
            for h_ in range(4):
                P.stt("dve", V(Sst[:, h_, :], b_S), V(Sst[:, h_, :], b_S), V(S_["egt"][0][:, t, 4 * d + h_:4 * d + h_ + 1], S_["egt"][1]),
                      V(Sp.ap[:, h_, :], Sp.bufs), ALU.mult, ALU.add)
            P.copy("act", V(Sb[:], b_Sb), V(Sst[:], b_S))
            yield

        def stream(d, par):
            order = list(range(NT)) if d == 0 else [1, 0] + list(range(NT - 1, 1, -1))
            for n_, t in enumerate(order if stop != 2 else order[:1]):
                if n_ % 2 == par:
                    yield from body(d, n_, t, 2 * d + par)

        LROUND = 33
        streams = []
        if stop != 1:
            streams = [[stream(0, 0), 0], [stream(1, 0), 0], [stream(0, 1), LROUND // 2 + 1], [stream(1, 1), LROUND // 2 + 1]]
        rnd = 0
        while streams:
            for sd in list(streams):
                if rnd < sd[1]:
                    continue
                try:
                    next(sd[0])
                except StopIteration:
                    streams.remove(sd)
            rnd += 1
    P.barrier()


CSTOP = [0]


def mla_perm():
    idx = []
    for h in range(4):
        base = h * 96
        idx += list(range(base, base + 64)) + list(range(base + 64, base + 96, 2)) + list(range(base + 65, base + 96, 2))
    return np.array(idx)


def stage_C(P, C, l, Wd, io, do_ctx):
    with ExitStack() as st:
        pp = [alloc(P, st, f"c_pp{i}", [128, 512], F32, psum=True) for i in range(7)]; b_pp = [P.buf() for _ in range(7)]
        ppi = [0]

        def nps3():
            k = ppi[0] % 7; ppi[0] += 1
            return pp[k], b_pp[k]
        wq1f = alloc(P, st, "c_wq1f", [128, 384], F32); wq2f = alloc(P, st, "c_wq2f", [64, 384], F32); wkf = alloc(P, st, "c_wkf", [128, 512], F32)
        gq1 = alloc(P, st, "c_gq1", [128, 1], F32); gq2 = alloc(P, st, "c_gq2", [64, 1], F32); gk = alloc(P, st, "c_gk", [128, 1], F32)
        wq1 = alloc(P, st, "c_wq1", [128, 384], BF16); wq2 = alloc(P, st, "c_wq2", [128, 384], BF16); wk = alloc(P, st, "c_wk", [128, 512], BF16)
        bw = P.mbuf(); bwb = P.buf()
        P.dma(V(wq1f[:], bw), V(Wd["wqb"][0:128, :], P.buf())); P.dma(V(wq2f[:], bw), V(Wd["wqb"][128:192, :], P.buf()))
        P.dma(V(wkf[:], bw), V(Wd["wkvb"], P.buf()))
        P.dma(V(gq1[:], bw), V(Wd["gq"][0:128, :], P.buf())); P.dma(V(gq2[:], bw), V(Wd["gq"][128:192, :], P.buf()))
        P.dma(V(gk[:], bw), V(Wd["gkv"], P.buf()))
        P.ts("dve", V(wq1[:], bwb), V(wq1f[:], bw), V(gq1[:], bw), ALU.mult)
        P.memset("dve", V(wq2[:], bwb), 0.0)
        P.ts("dve", V(wq2[0:64, :], bwb), V(wq2f[:], bw), V(gq2[:], bw), ALU.mult)
        P.ts("dve", V(wk[:], bwb), V(wkf[:], bw), V(gk[:], bw), ALU.mult)
        rp = alloc(P, st, "c_rope", [128, 16, 48], F32); b_rp = P.buf()
        P.dma(V(rp[:], b_rp), V(Wd["rope"], P.buf()))
        kT = alloc(P, st, "c_kT", [96, 4, T], BF16); b_kT = [P.buf() for _ in range(NT)]
        qT = alloc(P, st, "c_qT", [96, 4, T], BF16); b_qT = [P.buf() for _ in range(NT)]
        Va = alloc(P, st, "c_Va", [128, NT, 4, 66], BF16); b_Va = [P.buf() for _ in range(NT)]
        P.memset("pool", V(Va[:], b_Va), 1.0)
        qa = [alloc(P, st, f"c_qa{i}", [128, 352], F32) for i in range(4)]; b_qa = [P.buf() for _ in range(4)]
        stt_ = alloc(P, st, "c_st", [128, NT, 8], F32); b_stl = [P.buf() for _ in range(NT)]
        P.memset("dve", V(stt_[:], b_stl), 0.0)
        junk = alloc(P, st, "c_junk", [128, 192], BF16); b_junk = P.buf()
        qn = [alloc(P, st, f"c_qn{i}", [128, 320], BF16) for i in range(4)]; b_qn = [P.buf() for _ in range(4)]
        qnT = [alloc(P, st, f"c_qnT{i}", [128, 3, 128], BF16) for i in range(4)]; b_qnT = [P.buf() for _ in range(4)]
        for i_ in range(4):
            P.memset("pool", V(qnT[i_][:], b_qnT[i_]), 0.0)
        qf = [alloc(P, st, f"c_qf{i}", [128, 4, 96], F32) for i in range(4)]; b_qf = [P.buf() for _ in range(4)]
        kpe = [alloc(P, st, f"c_kpe{i}", [128, 32], F32) for i in range(4)]; b_kpe = [P.buf() for _ in range(4)]
        tmp = [alloc(P, st, f"c_tmp{i}", [128, 4, 4, 16], F32) for i in range(4)]; b_tmp = [P.buf() for _ in range(4)]
        qtok = [alloc(P, st, f"c_qtok{i}", [128, 4, 96], BF16) for i in range(4)]; b_qtok = [P.buf() for _ in range(4)]
        ktok = [alloc(P, st, f"c_ktok{i}", [128, 4, 96], BF16) for i in range(4)]; b_ktok = [P.buf() for _ in range(4)]
        def c1_body(t):
                w = t % 4
                P.dma(V(qa[w][:], b_qa[w]), V(io["PT"][t * 128:(t + 1) * 128, 528:880], io["b_PT"]))
                P.act(V(junk[:, 0:192], b_junk), V(qa[w][:, 0:192], b_qa[w]), AF.Square, accum=V(stt_[:, t, 0:1], b_stl[t]))
                P.act(V(junk[:, 0:128], b_junk), V(qa[w][:, 192:320], b_qa[w]), AF.Square, accum=V(stt_[:, t, 1:2], b_stl[t]))
                P.ts("dve", V(stt_[:, t, 2:3], b_stl[t]), V(stt_[:, t, 0:1], b_stl[t]), 1.0 / 192, ALU.mult, 1e-6, ALU.add)
                P.ts("dve", V(stt_[:, t, 3:4], b_stl[t]), V(stt_[:, t, 1:2], b_stl[t]), 1.0 / 128, ALU.mult, 1e-6, ALU.add)
                P.act(V(stt_[:, t, 4:6], b_stl[t]), V(stt_[:, t, 2:4], b_stl[t]), AF.Ln)
                P.act(V(stt_[:, t, 6:8], b_stl[t]), V(stt_[:, t, 4:6], b_stl[t]), AF.Exp, scale=-0.5)
                P.ts("dve", V(qn[w][:, 0:192], b_qn[w]), V(qa[w][:, 0:192], b_qa[w]), V(stt_[:, t, 6:7], b_stl[t]), ALU.mult)
                P.ts("dve", V(qn[w][:, 192:320], b_qn[w]), V(qa[w][:, 192:320], b_qa[w]), V(stt_[:, t, 7:8], b_stl[t]), ALU.mult)
                yield
                pm, bpm = nps3()
                pb = pm[:].bitcast(BF16)
                P.transpose(V(pb[:, 0:128], bpm), V(qn[w][:, 0:128], b_qn[w]), V(C.idb[:], C.b_idb))
                P.transpose(V(pb[0:64, 128:256], bpm), V(qn[w][:, 128:192], b_qn[w]), V(C.idb[:], C.b_idb))
                P.transpose(V(pb[:, 256:384], bpm), V(qn[w][:, 192:320], b_qn[w]), V(C.idb[:], C.b_idb))
                P.copy("act", V(qnT[w][:, 0, :], b_qnT[w]), V(pb[:, 0:128], bpm))
                P.copy("act", V(qnT[w][0:64, 1, :], b_qnT[w]), V(pb[0:64, 128:256], bpm))
                P.copy("act", V(qnT[w][:, 2, :], b_qnT[w]), V(pb[:, 256:384], bpm))
                if CSTOP[0] == 1:
                    return
                yield
                pq, bpq = nps3()
                P.mm(V(pq[:, 0:384], bpq), V(qnT[w][:, 0, :], b_qnT[w]), V(wq1[:], bwb), start=True, stop=False)
                P.mm(V(pq[:, 0:384], bpq), V(qnT[w][:, 1, :], b_qnT[w]), V(wq2[:], bwb), start=False, stop=True)
                if CSTOP[0] == 5:
                    return
                pk, bpk = nps3()
                P.mm(V(pk[:], bpk), V(qnT[w][:, 2, :], b_qnT[w]), V(wk[:], bwb))
                yield
                pq4 = pq[:, 0:384].rearrange("p (h d) -> p h d", d=96)
                pk4 = pk[:].rearrange("p (h d) -> p h d", d=128)
                if CSTOP[0] == 6:
                    return
                P.copy("dve", V(Va[:, t, :, 0:64], b_Va[t]), V(pk4[:, :, 64:128], bpk))
                if CSTOP[0] == 7:
                    return
                P.copy("dve", V(ktok[w][:, :, 0:64], b_ktok[w]), V(pk4[:, :, 0:64], bpk))
                if CSTOP[0] == 2:
                    return
                if t < 2:
                    P.copy("act", V(qtok[w][:], b_qtok[w]), V(pq4, bpq))
                    P.copy("dve", V(ktok[w][:, :, 64:80], b_ktok[w]), V(qa[w][:, 320:352:2].unsqueeze(1).to_broadcast([128, 4, 16]), b_qa[w]))
                    P.copy("dve", V(ktok[w][:, :, 80:96], b_ktok[w]), V(qa[w][:, 321:352:2].unsqueeze(1).to_broadcast([128, 4, 16]), b_qa[w]))
                else:
                    P.copy("act", V(qf[w][:], b_qf[w]), V(pq4, bpq))
                    P.copy("dve", V(qtok[w][:, :, 0:64], b_qtok[w]), V(qf[w][:, :, 0:64], b_qf[w]))
                    cosb = V(rp[:, t - 2, 0:16].unsqueeze(1).to_broadcast([128, 4, 16]), b_rp)
                    sinb = V(rp[:, t - 2, 16:32].unsqueeze(1).to_broadcast([128, 4, 16]), b_rp)
                    nsinb = V(rp[:, t - 2, 32:48].unsqueeze(1).to_broadcast([128, 4, 16]), b_rp)
                    x0 = V(qf[w][:, :, 64:80], b_qf[w]); x1 = V(qf[w][:, :, 80:96], b_qf[w])
                    tm = tmp[w]; btm = b_tmp[w]
                    P.tt("dve", V(tm[:, 0], btm), x0, cosb, ALU.mult)
                    P.tt("dve", V(tm[:, 1], btm), x1, nsinb, ALU.mult)
                    P.tt("dve", V(tm[:, 2], btm), x0, sinb, ALU.mult)
                    P.tt("dve", V(tm[:, 3], btm), x1, cosb, ALU.mult)
                    P.tt("dve", V(qtok[w][:, :, 64:80], b_qtok[w]), V(tm[:, 0], btm), V(tm[:, 1], btm), ALU.add)
                    P.tt("dve", V(qtok[w][:, :, 80:96], b_qtok[w]), V(tm[:, 2], btm), V(tm[:, 3], btm), ALU.add)
                    k0 = V(qa[w][:, 320:352:2], b_qa[w]); k1 = V(qa[w][:, 321:352:2], b_qa[w])
                    c1 = V(rp[:, t - 2, 0:16], b_rp); s1 = V(rp[:, t - 2, 16:32], b_rp); n1 = V(rp[:, t - 2, 32:48], b_rp)
                    kp = kpe[w]; bkp = b_kpe[w]
                    P.tt("dve", V(kp[:, 0:16], bkp), k0, c1, ALU.mult)
                    P.stt("dve", V(kp[:, 0:16], bkp), k1, 1.0, V(kp[:, 0:16], bkp), ALU.mult, ALU.add) if False else None
                    P.tt("dve", V(tm[:, 0, 0, :], btm), k1, n1, ALU.mult)
                    P.tt("dve", V(kp[:, 0:16], bkp), V(kp[:, 0:16], bkp), V(tm[:, 0, 0, :], btm), ALU.add)
                    P.tt("dve", V(kp[:, 16:32], bkp), k0, s1, ALU.mult)
                    P.tt("dve", V(tm[:, 1, 0, :], btm), k1, c1, ALU.mult)
                    P.tt("dve", V(kp[:, 16:32], bkp), V(kp[:, 16:32], bkp), V(tm[:, 1, 0, :], btm), ALU.add)
                    P.copy("dve", V(ktok[w][:, :, 64:96], b_ktok[w]), V(kp[:].unsqueeze(1).to_broadcast([128, 4, 32]), bkp))
                if CSTOP[0] == 3:
                    return
                yield
                for (src, bsrc, dstT, bdst) in ((ktok[w], b_ktok[w], kT, b_kT[t]), (qtok[w], b_qtok[w], qT, b_qT[t])):
                    pt_, bpt = nps3()
                    ptb = pt_[:].bitcast(BF16)
                    for h in range(4):
                        P.transpose(V(ptb[0:96, h * 128:(h + 1) * 128], bpt), V(src[:, h, :], bsrc), V(C.idb[:], C.b_idb))
                    P.copy("act" if dstT is kT else "dve", V(dstT[:, :, t * 128:(t + 1) * 128], bdst),
                           V(ptb[0:96, 0:512].rearrange("p (h i) -> p h i", i=128), bpt))

        def c1_stream(sidx):
            for t in range(NT):
                if t % 4 == sidx:
                    yield from c1_body(t)
                    yield
        c1s = [[c1_stream(i), 2 * i] for i in range(4)]
        rnd = 0
        while c1s:
            for sd in list(c1s):
                if rnd < sd[1]:
                    continue
                try:
                    next(sd[0])
                except StopIteration:
                    c1s.remove(sd)
            rnd += 1
        if CSTOP[0] in (1, 2, 3, 4, 5, 6, 7):
            return
        pT = [alloc(P, st, f"c_pT{i}", [128, 512], BF16) for i in range(4)]; b_pT = [P.buf() for _ in range(4)]
        SCB = (3, 4, 1, 2)
        ob = [alloc(P, st, f"c_ob{i}", [128, 4, 256], F32) for i in range(2)]; b_ob = [P.buf(), P.buf()]
        rec = alloc(P, st, "c_rec", [128, 64], F32); b_rec = P.buf()
        scale = 96.0 ** -0.5
        groups = [(2 + 4 * g, 4, list(range(NT))) for g in range(4)]
        if do_ctx:
            groups.append((0, 2, [0, 1]))
        oTs = [alloc(P, st, f"c_oTs{i}", [65, 512], F32) for i in range(2)]; b_oTs = [P.buf(), P.buf()]
        rec4 = [alloc(P, st, f"c_rec4{i}", [128, 4], F32) for i in range(2)]; b_rec4 = [P.buf(), P.buf()]
        its = []
        for gi, (t0, nq, kts) in enumerate(groups):
            for h in range(4):
                for ki, kt in enumerate(kts):
                    its.append((gi, t0, nq, h, ki, kt, len(kts)))

        def score(i):
            gi, t0, nq, h, ki, kt, nk = its[i]
            ps, bps = pp[SCB[i % 4]], b_pp[SCB[i % 4]]
            P.mm(V(ps[:, 0:nq * 128], bps), V(kT[:, h, kt * 128:(kt + 1) * 128], b_kT[kt]),
                 V(qT[:, h, t0 * 128:(t0 + nq) * 128], b_qT[t0:t0 + nq]))

        score(0)
        for i, (gi, t0, nq, h, ki, kt, nk) in enumerate(its):
            nqc = nq * 128
            gh = gi * 4 + h
            obw = ob[gi % 2]; bobw = b_ob[gi % 2]
            acc, bacc = pp[5 + gh % 2], b_pp[5 + gh % 2]
            ps, bps = pp[SCB[i % 4]], b_pp[SCB[i % 4]]
            pt_ = pT[i % 4]; bpt = b_pT[i % 4]
            P.act(V(pt_[:, 0:nqc], bpt), V(ps[:, 0:nqc], bps), AF.Exp, scale=scale)
            if i + 1 < len(its):
                score(i + 1)
            P.mm(V(acc[0:65, 0:nqc], bacc), V(Va[:, kt, h, 0:65], b_Va[kt]), V(pt_[:, 0:nqc], bpt),
                 start=(ki == 0), stop=(ki == nk - 1))
            if ki == nk - 1:
                ot = oTs[gh % 2]; bot = b_oTs[gh % 2]
                P.copy("dve", V(ot[:, 0:nqc], bot), V(acc[0:65, 0:nqc], bacc))
                ptr_, bptr = pp[0], b_pp[0]
                p4 = ptr_[:].rearrange("p (a b) -> p a b", b=128)
                for qi in range(nq):
                    P.transpose(V(p4[:, qi, 0:65], bptr), V(ot[:, qi * 128:(qi + 1) * 128], bot), V(C.idf[0:65, 0:65], C.b_idf))
                r4 = rec4[gh % 2]; br4 = b_rec4[gh % 2]
                P.recip(V(r4[:, 0:nq].unsqueeze(2), br4), V(p4[:, 0:nq, 64:65], bptr))
                P.tt("dve", V(obw[:, 0:nq, h * 64:(h + 1) * 64], bobw), V(p4[:, 0:nq, 0:64], bptr),
                     V(r4[:, 0:nq].unsqueeze(2).to_broadcast([128, nq, 64]), br4), ALU.mult)
                if h == 3:
                    for qi in range(nq):
                        tt_ = t0 + qi
                        P.dma(V(io["OB"][tt_ * 128:(tt_ + 1) * 128, :], io["b_OB"]), V(obw[:, qi, :], bobw), eng="sp")
    P.barrier()


def stage_D(P, C, l, Wd, io, last):
    with ExitStack() as st:
        mod = alloc(P, st, "d_mod", [128, 24, 2], F32); b_mod = P.buf()
        P.dma(V(mod[:], b_mod), V(io["modo"], io["b_modo"]))
        pg = [alloc(P, st, f"d_pg{i}", [128, 512], F32, psum=True) for i in range(2)]; b_pg = [P.buf(), P.buf()]
        pcv = alloc(P, st, "d_pcv", [128, 256], F32, psum=True); b_pcv = P.buf()
        ptr = alloc(P, st, "d_ptr", [128, 8, 128], BF16, psum=True); b_ptr = P.buf()
        pw = [alloc(P, st, f"d_pw{i}", [128, 512], F32, psum=True) for i in range(2)]; b_pw = [P.buf(), P.buf()]
        Dg = [alloc(P, st, f"d_Dg{i}", [128, 128], F32) for i in range(2)]; b_Dg = [P.buf(), P.buf()]
        Wo = [alloc(P, st, f"d_Wo{i}", [128, 8, 1024], BF16) for i in range(2)]; b_Wo = [P.buf(), P.buf()]
        wf = [alloc(P, st, f"d_wf{i}", [128, 1024], F32) for i in range(4)]; b_wf = [P.buf() for _ in range(4)]
        hc = alloc(P, st, "d_hc", [128, 4, TP], BF16); b_hc = P.mbuf()
        prod = alloc(P, st, "d_prod", [128, 2, TP], BF16); b_prod = P.buf()
        ccv = alloc(P, st, "d_ccv", [128, 2, 3], F32); b_ccv = P.buf()
        dgc = alloc(P, st, "d_dgc", [128, 6, 128], BF16); b_dgc = P.buf()
        gng = alloc(P, st, "d_gng", [128, 4, 128], F32); b_gng = P.buf()
        P.dma(V(ccv[:], b_ccv), V(Wd["cconv"], P.buf()))
        P.dma(V(gng[:], b_gng), V(Wd["gng"], P.buf()))
        P.memset("pool", V(hc[:, :, 0:2], b_hc), 0.0)
        P.memset("pool", V(hc[:, :, 258:262], b_hc), 0.0)
        P.memset("pool", V(hc[:, :, 2310:2312], b_hc), 0.0)
        pfv = io["PF"][1536:2048, :].rearrange("(fc p) t -> p fc t", p=128)
        for fc in range(4):
            P.dma(V(hc[:, fc, 2:258], b_hc), V(pfv[:, fc, 2:258], io["b_PF"]))
            P.dma(V(hc[:, fc, 262:2310], b_hc), V(pfv[:, fc, 262:2310], io["b_PF"]))
        wov = Wd["w_out"].rearrange("(kc p) n -> p kc n", p=128)
        n = 0
        for w in ((0,) if last else (0, 1)):
            for j in range(8):
                dgt = Dg[j % 2]; bd = b_Dg[j % 2]
                P.scale("act", V(dgt[:], bd), V(C.idf[:], C.b_idf), V(mod[:, 16 + j, w:w + 1], b_mod))
                P.mm(V(pg[j // 4][:, (j % 4) * 128:(j % 4 + 1) * 128], b_pg[j // 4]), V(C.onesf[:], C.b_onesf), V(dgt[:], bd))
            for kc in range(8):
                wb = wf[n % 4]; bw = b_wf[n % 4]; n += 1
                P.dma(V(wb[:], bw), V(wov[:, kc, :], P.buf()))
                for hf in range(2):
                    P.tt("dve", V(Wo[w][:, kc, hf * 512:(hf + 1) * 512], b_Wo[w]), V(wb[:, hf * 512:(hf + 1) * 512], bw), V(pg[hf][:], b_pg[hf]), ALU.mult)
        for fc in range(2):
            P.tt("dve", V(prod[:, fc, :], b_prod), V(hc[:, fc, :], b_hc), V(hc[:, 2 + fc, :], b_hc), ALU.mult)
            for j in range(3):
                P.ts("dve", V(dgc[:, fc * 3 + j, :], b_dgc), V(C.idb[:], C.b_idb), V(ccv[:, fc, j:j + 1], b_ccv), ALU.mult)
        if last:
            fng = alloc(P, st, "d_fng", [128, 1024], F32); b_fng = P.buf()
            P.dma(V(fng[:], b_fng), V(Wd["fng"], P.buf()))
        NB = 3
        def mk(name, shape, dt):
            return [(alloc(P, st, f"d_{name}{i}", shape, dt), P.buf()) for i in range(NB)]
        B_ = {"oa": mk("oa", [128, 4, 128], F32), "oa2": mk("oa2", [128, 1024], F32), "ob": mk("ob", [128, 256], F32), "za": mk("za", [128, 512], F32), "zbg": mk("zbg", [128, 512], F32),
              "zc": mk("zc", [128, 256], F32), "x": mk("x", [128, 1024], F32), "sza": mk("sza", [128, 512], F32), "szb": mk("szb", [128, 256], F32),
              "szc": mk("szc", [128, 256], F32), "mix": mk("mix", [128, 1024], BF16), "mixT": mk("mixT", [128, 8, 128], BF16),
              "a1": mk("a1", [128, 4, 128], F32), "a2": mk("a2", [128, 4, 128], F32), "tc": mk("tc", [128, 256], F32), "xn": mk("xn", [128, 1024], F32),
              "y": mk("y", [128, 1024], F32)}
        stt_ = alloc(P, st, "d_st", [128, NT, 16], F32); b_st = P.buf()
        junk = alloc(P, st, "d_junk", [128, 1024], BF16); b_junk = P.buf()
        fin = alloc(P, st, "d_fin", [128, NT, 4], F32); b_fin = P.buf()
        PT = io["PT"]
        b_stt = [P.buf() for _ in range(NT)]; b_fint = [P.buf() for _ in range(NT)]
        P.memset("dve", V(stt_[:], b_stt), 0.0)
        P.memset("dve", V(fin[:], b_fint), 0.0)
        ptr2 = alloc(P, st, "d_ptr2", [128, 8, 128], BF16, psum=True); b_ptr2 = P.buf()
        ptr3 = pg[0][:].bitcast(BF16).rearrange("p (a b) -> p a b", b=128)
        PS = [(ptr, b_ptr, [pw[0], pw[0]], [b_pw[0], b_pw[0]]), (ptr2, b_ptr2, [pw[1], pw[1]], [b_pw[1], b_pw[1]]),
              (ptr3, b_pg[0], [pg[1], pg[1]], [b_pg[1], b_pg[1]])]

        def body(it, t):
            w = it % NB
            ptr_, b_ptr_, pw_, b_pw_ = PS[w]
            X = {k: v[w] for k, v in B_.items()}
            xv = lambda k: V(X[k][0][:], X[k][1])
            bst = b_stt[t]; bfin = b_fint[t]
            r = slice(t * 128, (t + 1) * 128)
            P.dma(xv("oa2"), V(io["OA"][r, :], io["b_OA"]))
            P.dma(xv("ob"), V(io["OB"][r, :], io["b_OB"]))
            P.dma(xv("za"), V(PT[r, 0:512], io["b_PT"]))
            P.dma(xv("zbg"), V(PT[r, 880:1392], io["b_PT"]))
            P.dma(xv("zc"), V(PT[r, 1392:1648], io["b_PT"]))
            P.dma(xv("x"), V(io["xin"][r, :], io["b_xin"]))
            yield
            P.tt("dve", V(X["oa"][0][:].rearrange("p h v -> p (h v)"), X["oa"][1]), V(X["oa2"][0][:, 0:512], X["oa2"][1]), V(X["oa2"][0][:, 512:1024], X["oa2"][1]), ALU.add)
            P.act(xv("sza"), xv("za"), AF.Silu)
            P.act(xv("szb"), V(X["zbg"][0][:, 0:256], X["zbg"][1]), AF.Silu)
            P.act(xv("szc"), xv("zc"), AF.Silu)
            yield
            for h in range(4):
                P.act(V(junk[:, 0:128], b_junk), V(X["oa"][0][:, h, :], X["oa"][1]), AF.Square, accum=V(stt_[:, t, h:h + 1], bst))
            P.ts("dve", V(stt_[:, t, 4:8], bst), V(stt_[:, t, 0:4], bst), 1.0 / 128, ALU.mult, 1e-6, ALU.add)
            P.act(V(stt_[:, t, 8:12], bst), V(stt_[:, t, 4:8], bst), AF.Ln)
            P.act(V(stt_[:, t, 12:16], bst), V(stt_[:, t, 8:12], bst), AF.Exp, scale=-0.5)
            yield
            rb = V(stt_[:, t, 12:16].unsqueeze(2).to_broadcast([128, 4, 128]), bst)
            P.tt("dve", xv("a1"), xv("oa"), rb, ALU.mult)
            P.tt("dve", xv("a2"), xv("a1"), V(gng[:], b_gng), ALU.mult)
            P.tt("dve", V(X["mix"][0][:, 0:512], X["mix"][1]), V(X["a2"][0][:].rearrange("p h v -> p (h v)"), X["a2"][1]), xv("sza"), ALU.mult)
            yield
            P.tt("dve", V(X["mix"][0][:, 512:768], X["mix"][1]), xv("ob"), xv("szb"), ALU.mult)
            o0 = tok_off(t)
            for fc in range(2):
                for j in range(3):
                    P.mm(V(pcv[:, fc * 128:(fc + 1) * 128], b_pcv), V(prod[:, fc, o0 + j - 1:o0 + j - 1 + 128], b_prod), V(dgc[:, fc * 3 + j, :], b_dgc),
                         start=(j == 0), stop=(j == 2))
            P.tt("dve", xv("tc"), V(X["zbg"][0][:, 256:512], X["zbg"][1]), xv("szc"), ALU.mult)
            P.tt("dve", V(X["mix"][0][:, 768:1024], X["mix"][1]), V(pcv[:], b_pcv), xv("tc"), ALU.mult)
            yield
            for j in range(8):
                P.transpose(V(ptr_[:, j, :], b_ptr_), V(X["mix"][0][:, j * 128:(j + 1) * 128], X["mix"][1]), V(C.idb[:], C.b_idb))
            P.copy("act", xv("mixT"), V(ptr_[:], b_ptr_))
            yield
            wsel = 1 if t < 2 else 0
            for hf in range(2):
                for kc in range(8):
                    P.mm(V(pw_[hf][:], b_pw_[hf]), V(X["mixT"][0][:, kc, :], X["mixT"][1]), V(Wo[wsel][:, kc, hf * 512:(hf + 1) * 512], b_Wo[wsel]),
                         start=(kc == 0), stop=(kc == 7))
                P.tt("dve", V(X["xn"][0][:, hf * 512:(hf + 1) * 512], X["xn"][1]), V(pw_[hf][:], b_pw_[hf]), V(X["x"][0][:, hf * 512:(hf + 1) * 512], X["x"][1]), ALU.add)
                yield
            if not last:
                P.dma(V(io["xout"][r, :], io["b_xout"]), xv("xn"), eng="sp")
            else:
                P.act(V(junk[:], b_junk), xv("xn"), AF.Square, accum=V(fin[:, t, 0:1], bfin))
                P.ts("dve", V(fin[:, t, 1:2], bfin), V(fin[:, t, 0:1], bfin), 1.0 / 1024, ALU.mult, 1e-6, ALU.add)
                P.act(V(fin[:, t, 2:3], bfin), V(fin[:, t, 1:2], bfin), AF.Ln)
                P.act(V(fin[:, t, 3:4], bfin), V(fin[:, t, 2:3], bfin), AF.Exp, scale=-0.5)
                yield
                P.stt("dve", xv("y"), xv("xn"), V(fin[:, t, 3:4], bfin), V(fng[:], b_fng), ALU.mult, ALU.mult)
                ro = slice((t - 2) * 128, (t - 1) * 128)
                P.dma(V(io["out"][ro, :], io["b_out"]), xv("y"), eng="sp")
            yield

        tiles = list(range(2 if last else 0, NT))

        def stream(sidx):
            for it, t in enumerate(tiles):
                if it % NB == sidx:
                    yield from body(it, t)

        streams = [[stream(i), 3 * i] for i in range(NB)]
        rnd = 0
        while streams:
            for sd in list(streams):
                if rnd < sd[1]:
                    continue
                try:
                    next(sd[0])
                except StopIteration:
                    streams.remove(sd)
            rnd += 1
    P.barrier()


def prep_consts():
    sel = np.zeros((2, 256), np.float32)
    sel[0, 0:128] = 1.0; sel[1, 128:256] = 1.0
    return {"ident": np.eye(128, dtype=np.float32), "sel": sel}


def fm(v, n):
    return np.ascontiguousarray(v.reshape(n, 128).T)


def prep_layer_inputs(inp, b, l):
    d = {}
    cc = np.stack([fm(inp["c"][b], 8), fm(inp["c_ctx"], 8)], -1)
    d["cc"] = np.ascontiguousarray(cc)
    ba = fm(inp["b_ada"][l], 24)
    d[f"bada{l}"] = np.ascontiguousarray(np.stack([ba, ba], -1))
    g = fm(inp["norm_g"][l], 8)
    d[f"ng{l}"] = np.ascontiguousarray(np.stack([g, g], -1))
    d[f"w_ada{l}"] = inp["w_ada"][l]
    d[f"w_in{l}"] = inp["w_in"][l]
    return d


def prep_layer_inputs_b(inp, l):
    d = {}
    gc = inp["gdn_conv"][l]
    d[f"gconv{l}"] = np.ascontiguousarray(gc.T.reshape(12, 128, 5).transpose(1, 0, 2))
    d[f"alog{l}"] = np.ascontiguousarray(np.broadcast_to(inp["gdn_a_log"][l].reshape(1, 1, 8), (128, NT, 8))).astype(np.float32)
    d[f"dtb{l}"] = np.ascontiguousarray(np.broadcast_to(inp["gdn_dt_bias"][l].reshape(1, 1, 8), (128, NT, 8))).astype(np.float32)
    return d


def rope_host():
    rows = 2048 // 64
    r = np.repeat(np.arange(rows), 64); col = np.tile(np.arange(64), rows)
    inv = (10000.0 ** (-np.arange(8, dtype=np.float32) / 8)).astype(np.float32)
    ang = np.concatenate([r[:, None] * inv, col[:, None] * inv], -1).astype(np.float32)
    cs, sn = np.cos(ang), np.sin(ang)
    tab = np.concatenate([cs, sn, -sn], -1).astype(np.float32)
    return np.ascontiguousarray(tab.reshape(16, 128, 48).transpose(1, 0, 2))


def prep_layer_inputs_cd(inp, l):
    d = {}
    d[f"wqb{l}"] = np.ascontiguousarray(inp["mla_w_qb"][l][:, mla_perm()])
    d[f"gq{l}"] = np.ascontiguousarray(inp["mla_q_norm_g"][l].reshape(192, 1))
    d[f"wkvb{l}"] = inp["mla_w_kvb"][l]
    d[f"gkv{l}"] = np.ascontiguousarray(inp["mla_kv_norm_g"][l].reshape(128, 1))
    d[f"w_out{l}"] = inp["w_out"][l]
    d[f"gng{l}"] = np.ascontiguousarray(np.broadcast_to(inp["gdn_norm_g"][l].reshape(1, 1, 128), (128, 4, 128))).astype(np.float32)
    d[f"cconv{l}"] = np.ascontiguousarray(inp["conv_w"][l].T.reshape(2, 128, 3).transpose(1, 0, 2))
    return d


KSTOP = [99]
DBG = [False]


def build_program():
    nc = bass.Bass("TRN2", target_bir_lowering=False)
    P = Prog(nc)

    def din(name, shape, dt=F32):
        return nc.dram_tensor(name, list(shape), dt, kind="ExternalInput").ap()

    def dint(name, shape, dt=F32):
        return nc.dram_tensor(name, list(shape), dt, kind="ExternalOutput" if DBG[0] else "Internal").ap()
    cin = {"ident": din("ident", [128, 128]), "sel": din("sel", [2, 256])}
    gmask = din("gmask", [128, 9, 128]); rope = din("rope", [128, 16, 48]); fng = din("fng", [128, 1024])
    xin0 = din("xin0", [T, 1024]); cc = din("cc", [128, 8, 2])
    out = nc.dram_tensor("out", [2048, 1024], F32, kind="ExternalOutput").ap()
    with ExitStack() as st:
        P.init_bar(st)
        C = load_consts(P, st, cin)
        xin = xin0; b_xin = P.buf()
        for l in range(2):
            WA = {"cc": cc, "w_ada": din(f"w_ada{l}", [1024, 3072]), "bada": din(f"bada{l}", [128, 24, 2]), "ng": din(f"ng{l}", [128, 8, 2]),
                  "w_in": din(f"w_in{l}", [1024, 3696])}
            io = {"PF": dint(f"PF{l}", [2048, TP], BF16), "b_PF": P.mbuf(), "PT": dint(f"PT{l}", [T, PT_W]), "b_PT": P.mbuf(),
                  "modo": dint(f"modo{l}", [128, 24, 2]), "b_modo": P.buf(), "OA": dint(f"OA{l}", [T, 1024]), "b_OA": P.mbuf(),
                  "OB": dint(f"OB{l}", [T, 256]), "b_OB": P.mbuf(), "xin": xin, "b_xin": b_xin,
                  "xout": dint(f"xmid{l}", [T, 1024]), "b_xout": P.mbuf(), "out": out, "b_out": P.mbuf()}
            if KSTOP[0] >= 4 * l + 1:
                stage_A(P, C, l, xin, WA, io)
            if KSTOP[0] >= 4 * l + 2:
              stage_B(P, C, l, {"gconv": din(f"gconv{l}", [128, 12, 5]), "alog": din(f"alog{l}", [128, NT, 8]), "dtb": din(f"dtb{l}", [128, NT, 8]),
                              "gmask": gmask}, io)
            if KSTOP[0] >= 4 * l + 3:
              stage_C(P, C, l, {"wqb": din(f"wqb{l}", [192, 384]), "gq": din(f"gq{l}", [192, 1]), "wkvb": din(f"wkvb{l}", [128, 512]),
                              "gkv": din(f"gkv{l}", [128, 1]), "rope": rope}, io, do_ctx=(l == 0))
            if KSTOP[0] >= 4 * l + 4:
              stage_D(P, C, l, {"w_out": din(f"w_out{l}", [1024, 1024]), "gng": din(f"gng{l}", [128, 4, 128]), "cconv": din(f"cconv{l}", [128, 2, 3]),
                              "fng": fng}, io, last=(l == 1))
            xin = io["xout"]; b_xin = io["b_xout"]
        P.emit()
    return nc


def make_in_maps(inp, cores):
    shared = {**prep_consts(), **gdn_consts_host(), "rope": rope_host(),
              "fng": np.ascontiguousarray(np.broadcast_to(inp["final_norm_g"].reshape(1, 1024), (128, 1024))).astype(np.float32)}
    for l in range(2):
        shared.update(prep_layer_inputs_b(inp, l)); shared.update(prep_layer_inputs_cd(inp, l))
    maps = []
    for b in cores:
        m = dict(shared)
        for l in range(2):
            m.update(prep_layer_inputs(inp, b, l))
        m["xin0"] = np.ascontiguousarray(np.concatenate([inp["ctx"][b], inp["x"][b]], 0))
        maps.append(m)
    return maps


def kernel(**inputs):
    inp = {k: np.asarray(v, dtype=np.float32) for k, v in inputs.items()}
    nc = build_program()
    maps = make_in_maps(inp, list(range(8)))
    res = run_bass_kernel_spmd(nc, maps, core_ids=list(range(8)))
    return np.stack([np.asarray(r["out"], dtype=np.float32) for r in res.results], 0)
```

```python
from contextlib import ExitStack
import numpy as np
import concourse.bass as bass
import concourse.mybir as mybir
from concourse.bass_utils import run_bass_kernel_spmd

F32 = mybir.dt.float32
BF16 = mybir.dt.bfloat16
AF = mybir.ActivationFunctionType
ALU = mybir.AluOpType
AX = mybir.AxisListType

EPOCH = 1000
STRICT_SAME_ENGINE = False
DMA_ROT = 12


class Buf:
    __slots__ = ("name", "last_w", "readers", "multi", "writers")

    def __init__(self, name, multi=False):
        self.name = name
        self.last_w = None
        self.readers = []
        self.multi = multi
        self.writers = []


class V:
    __slots__ = ("ap", "bufs")

    def __init__(self, ap, bufs):
        self.ap = ap
        self.bufs = bufs if isinstance(bufs, (list, tuple)) else [bufs]


class Op:
    __slots__ = ("eng", "fn", "reads", "writes", "waits", "signal", "count", "is_dma", "dsem", "dval", "prewait")

    def __init__(self, eng, fn, reads, writes, is_dma=False):
        self.eng = eng; self.fn = fn; self.reads = reads; self.writes = writes
        self.waits = []; self.signal = False; self.count = None
        self.is_dma = is_dma; self.dsem = None; self.dval = None; self.prewait = None


ENGS = ("pe", "act", "dve", "pool", "sp")


class Prog:
    def __init__(self, nc):
        self.nc = nc
        self.ops = []
        self.per_eng = {e: [] for e in ENGS}
        self.nbuf = 0
        self.dma_n = {e: 0 for e in ENGS}
        self.dma_last = {}

    def init_bar(self, stack):
        self._bar_sb = stack.enter_context(self.nc.sbuf_tensor("bar_sb", [128, 8], F32))
        self._bar_ps = stack.enter_context(self.nc.psum_tensor("bar_ps", [128, 512], F32))
        self._bar_tok = {e: self.buf("bartok_" + e) for e in ("act", "dve", "pool")}
        self._bar_init = self.buf("barinit")
        self.add("dve", lambda e: e.memset(self._bar_sb[:], 0.0), [], [V(None, [self._bar_init] + list(self._bar_tok.values()))])

    def buf(self, name=None):
        self.nbuf += 1
        return Buf(name or f"b{self.nbuf}")

    def mbuf(self, name=None):
        self.nbuf += 1
        return Buf(name or f"m{self.nbuf}", multi=True)

    def sb(self, name, shape, dtype):
        t = self.nc.alloc_sbuf_tensor(name, list(shape), dtype)
        return t

    def ps(self, name, shape, dtype=F32):
        return self.nc.alloc_psum_tensor(name, list(shape), dtype)

    def add(self, eng, fn, reads, writes, is_dma=False):
        rb = [b for v in reads for b in v.bufs]
        wb = [b for v in writes for b in v.bufs]
        op = Op(eng, fn, rb, wb, is_dma)
        deps = []
        for b in rb:
            if b.multi:
                for w_ in b.writers:
                    deps.append((w_, "raw"))
            elif b.last_w is not None:
                deps.append((b.last_w, "raw"))
        for b in wb:
            if b.multi and is_dma:
                pass
            elif b.multi:
                for w_ in b.writers:
                    deps.append((w_, "waw"))
            elif b.last_w is not None:
                deps.append((b.last_w, "waw"))
            for r in b.readers:
                deps.append((r, "war"))
        seen = set()
        for d, kind in deps:
            if d is op or id(d) in seen:
                continue
            if d.eng == eng and not d.is_dma:
                if eng == "pe" or (kind != "raw" and not STRICT_SAME_ENGINE):
                    continue
            seen.add(id(d))
            d.signal = True
            op.waits.append(d)
        for b in rb:
            b.readers.append(op)
        for b in wb:
            if b.multi:
                if b.readers or not is_dma:
                    b.writers = []
                b.writers.append(op)
            b.last_w = op
            b.readers = []
        self.ops.append(op)
        self.per_eng[eng].append(op)
        return op

    def emit(self):
        nc = self.nc
        nsig = {e: 0 for e in ENGS}
        dma_ops = {e: [] for e in ENGS}
        for op in self.ops:
            if op.is_dma:
                dma_ops[op.eng].append(op)
            elif op.signal:
                nsig[op.eng] += 1
                op.count = nsig[op.eng]
        self.nsig = nsig
        sems = {}
        for e in ENGS:
            n = (nsig[e] + EPOCH - 1) // EPOCH
            sems[e] = [nc.alloc_semaphore(f"s_{e}_{i}") for i in range(max(n, 1))]
        dsems = {}
        for e in ENGS:
            if dma_ops[e]:
                dsems[e] = [nc.alloc_semaphore(f"d_{e}_{i}") for i in range(DMA_ROT)]
                cnt = [0] * DMA_ROT
                for i, op in enumerate(dma_ops[e]):
                    k = i % DMA_ROT
                    if cnt[k] > 0:
                        op.prewait = (dsems[e][k], 16 * cnt[k])
                    cnt[k] += 1
                    op.dsem = dsems[e][k]; op.dval = 16 * cnt[k]
        handles = {"pe": nc.tensor, "act": nc.scalar, "dve": nc.vector, "pool": nc.gpsimd, "sp": nc.sync}

        def emit_engine(ename, eng):
            known = {}
            for op in self.per_eng[ename]:
                if op.prewait is not None:
                    eng.wait_ge(op.prewait[0], op.prewait[1])
                for d in op.waits:
                    if d.is_dma:
                        key = ("d", id(d.dsem)); val = d.dval; sem = d.dsem
                    else:
                        ep = (d.count - 1) // EPOCH
                        key = (d.eng, ep); val = d.count - ep * EPOCH; sem = sems[d.eng][ep]
                    if known.get(key, 0) >= val:
                        continue
                    known[key] = val
                    eng.wait_ge(sem, val)
                if op.fn is None:
                    continue
                ins = op.fn(eng)
                if op.is_dma:
                    ins.then_inc(op.dsem, 16)
                elif op.signal:
                    ep = (op.count - 1) // EPOCH
                    ins.then_inc(sems[ename][ep], 1)
            if ename in dsems:
                last = {}
                for op in dma_ops[ename]:
                    last[id(op.dsem)] = (op.dsem, op.dval)
                for sem, val in last.values():
                    eng.wait_ge(sem, val)

        with nc.Block() as block:
            @block.tensor
            def _(e):
                emit_engine("pe", e)

            @block.scalar
            def _(e):
                emit_engine("act", e)

            @block.vector
            def _(e):
                emit_engine("dve", e)

            @block.gpsimd
            def _(e):
                emit_engine("pool", e)

            @block.sync
            def _(e):
                emit_engine("sp", e)

    def barrier(self):
        toks = []
        if not hasattr(self, "_bar_tok"):
            self._bar_tok = {e: self.buf("bartok_" + e) for e in ("act", "dve", "pool")}
        for e in ("pe", "act", "dve", "pool"):
            b = self.buf("bar_" + e)
            if e == "pe":
                self.add("pe", lambda en: en.matmul(self._bar_ps[0:1, 0:1], self._bar_sb[0:1, 0:1], self._bar_sb[0:1, 0:1], start=True, stop=True), [V(None, self._bar_init)], [V(None, b)])
            else:
                i = ("act", "dve", "pool").index(e) + 1
                tk_ = self._bar_tok[e]
                self.add(e, (lambda i, e_: (lambda en: en.memzero(self._bar_sb[0:1, i:i + 1]) if e_ == 'act' else en.memset(self._bar_sb[0:1, i:i + 1], 0.0)))(i, e),
                         [V(None, tk_)], [V(None, [b, tk_])])
            toks.append(b)
        dm = []
        for q in ENGS:
            dm += [op for op in self.per_eng[q] if op.is_dma][-DMA_ROT:]
        for e in ENGS:
            op = self.add(e, None, [V(None, toks)], [])
            op.eng = e
            for d in dm:
                if d not in op.waits:
                    op.waits.append(d)

    def dma(self, out, in_, eng="sp", **kw):
        return self.add(eng, lambda e: e.dma_start(out=out.ap, in_=in_.ap, **kw), [in_], [out], is_dma=True)

    def mm(self, out, lhsT, rhs, start=True, stop=True, extra_reads=()):
        reads = [lhsT, rhs] + list(extra_reads)
        return self.add("pe", lambda e: e.matmul(out.ap, lhsT.ap, rhs.ap, start=start, stop=stop), reads, [out])

    def transpose(self, out, in_, ident):
        return self.add("pe", lambda e: e.transpose(out.ap, in_.ap, ident.ap), [in_, ident], [out])

    def act(self, out, in_, func, bias=None, scale=None, accum=None, eng="sp"):
        reads = [in_]
        kw = {}
        if bias is not None:
            if isinstance(bias, V):
                reads.append(bias); kw["bias"] = bias.ap
            else:
                kw["bias"] = bias
        if scale is not None:
            if isinstance(scale, V):
                reads.append(scale); kw["scale"] = scale.ap
            else:
                kw["scale"] = scale
        writes = [out]
        if accum is not None:
            writes.append(accum); kw["accum_out"] = accum.ap
        return self.add("act", lambda e: e.activation(out.ap, in_.ap, func, **kw), reads, writes)

    def tt(self, eng, out, a, b, op):
        return self.add(eng, lambda e: e.tensor_tensor(out.ap, a.ap, b.ap, op), [a, b], [out])

    def ts(self, eng, out, a, s1, op0, s2=None, op1=None, accum=None):
        reads = [a]
        s1a = s1.ap if isinstance(s1, V) else s1
        s2a = s2.ap if isinstance(s2, V) else s2
        if isinstance(s1, V): reads.append(s1)
        if isinstance(s2, V): reads.append(s2)
        writes = [out]
        kw = {}
        if op1 is not None:
            kw["op1"] = op1
        if accum is not None:
            writes.append(accum); kw["accum_out"] = accum.ap
        return self.add(eng, lambda e: e.tensor_scalar(out.ap, a.ap, s1a, s2a, op0, **kw), reads, writes)

    def stt(self, eng, out, a, s, b, op0, op1):
        reads = [a, b]
        sa = s.ap if isinstance(s, V) else s
        if isinstance(s, V): reads.append(s)
        return self.add(eng, lambda e: e.scalar_tensor_tensor(out.ap, a.ap, sa, b.ap, op0, op1), reads, [out])

    def scale(self, eng, out, in_, sc):
        if eng == "act":
            return self.act(out, in_, AF.Identity, scale=sc)
        return self.ts(eng, out, in_, sc, ALU.mult)

    def copy(self, eng, out, in_):
        if eng == "act":
            return self.add("act", lambda e: e.copy(out.ap, in_.ap), [in_], [out])
        return self.add(eng, lambda e: e.tensor_copy(out.ap, in_.ap), [in_], [out])

    def memset(self, eng, out, val):
        return self.add(eng, lambda e: e.memset(out.ap, val), [], [out])

    def recip(self, out, in_):
        return self.add("dve", lambda e: e.reciprocal(out.ap, in_.ap), [in_], [out])

    def reduce(self, eng, out, in_, op, axis=AX.X):
        return self.add(eng, lambda e: e.tensor_reduce(out.ap, in_.ap, axis, op), [in_], [out])


T = 2304; NT = 18; D = 1024; TP = 2312
PT_W = 1648


def tok_off(t):
    return 2 + t * 128 if t < 2 else 262 + (t - 2) * 128


W_CHUNKS = [
    ("tok", 512, [(2048, 368)]),
    ("feat", 0, [(0, 512)]),
    ("feat", 512, [(512, 512)]),
    ("feat", 1024, [(1024, 512)]),
    ("tok", 0, [(1536, 512)]),
    ("feat", 1536, [(2672, 256), (3184, 256)]),
    ("tok", 880, [(2416, 256), (2928, 256)]),
    ("tok", 1392, [(3440, 256)]),
]


class Ctx:
    pass


_ALLOC_N = [0]


def alloc(P, stack, name, shape, dtype, psum=False):
    _ALLOC_N[0] += 1
    name = f"{name}_{_ALLOC_N[0]}"
    if psum:
        nbytes = int(np.prod(shape[1:])) * (2 if dtype == BF16 else 4)
        assert nbytes <= 2048
        if nbytes < 2048:
            full = stack.enter_context(P.nc.psum_tensor(name, [128, 512], F32))
            v = full[:]
            if dtype == BF16:
                v = v.bitcast(BF16)
            n = int(np.prod(shape[1:]))
            v = v[0:shape[0], 0:n]
            if len(shape) == 3:
                v = v.rearrange("p (a b) -> p a b", b=shape[2])
            return v
        t = stack.enter_context(P.nc.psum_tensor(name, list(shape), dtype))
    else:
        t = stack.enter_context(P.nc.sbuf_tensor(name, list(shape), dtype))
    return t


def load_consts(P, stack, cin):
    C = Ctx()
    C.idf = alloc(P, stack, "c_idf", [128, 128], F32); C.b_idf = P.buf()
    C.idb = alloc(P, stack, "c_idb", [128, 128], BF16); C.b_idb = P.buf()
    C.onesf = alloc(P, stack, "c_onesf", [128, 128], F32); C.b_onesf = P.buf()
    C.self_ = alloc(P, stack, "c_self", [2, 256], F32); C.b_self = P.buf()
    P.dma(V(C.idf[:], C.b_idf), V(cin["ident"], P.buf()))
    P.copy("dve", V(C.idb[:], C.b_idb), V(C.idf[:], C.b_idf))
    P.memset("pool", V(C.onesf[:], C.b_onesf), 1.0)
    P.dma(V(C.self_[:], C.b_self), V(cin["sel"], P.buf()))
    return C


def stage_A(P, C, l, xin, W, out):
    nc = P.nc
    with ExitStack() as st:
        wbuf = [alloc(P, st, f"a_wbuf{i}", [128, 8, 512], F32) for i in range(3)]
        b_wbuf = [P.buf(), P.buf(), P.buf()]
        cc = alloc(P, st, "a_cc", [128, 8, 2], F32); b_cc = P.buf()
        scc = alloc(P, st, "a_scc", [128, 8, 2], F32); b_scc = P.buf()
        bada = alloc(P, st, "a_bada", [128, 24, 2], F32); b_bada = P.buf()
        ng = alloc(P, st, "a_ng", [128, 8, 2], F32); b_ng = P.buf()
        mod = alloc(P, st, "a_mod", [128, 24, 2], F32); b_mod = P.buf()
        Asc = alloc(P, st, "a_Asc", [128, 8, 2], F32); b_Asc = P.buf()
        hT = alloc(P, st, "a_hT", [128, 8, TP], BF16); b_hT = [P.buf() for _ in range(NT)]
        ps_mod = alloc(P, st, "a_psmod", [128, 24, 2], F32, psum=True); b_psmod = P.buf()
        ps_b = alloc(P, st, "a_psb", [2, 512], F32, psum=True); b_psb = P.buf()
        modrow = alloc(P, st, "a_modrow", [2, 3072], F32); b_modrow = P.buf()
        P.dma(V(cc[:], b_cc), V(W["cc"], P.buf()))
        P.dma(V(bada[:], b_bada), V(W["bada"], P.buf()))
        P.dma(V(ng[:], b_ng), V(W["ng"], P.buf()))
        P.act(V(scc[:], b_scc), V(cc[:], b_cc), AF.Silu)
        wada = W["w_ada"].rearrange("(kc p) n -> p kc n", p=128)
        xt = [alloc(P, st, f"a_xt{i}", [128, 1024], F32) for i in range(3)]; b_xt = [P.buf() for _ in range(3)]
        junk = alloc(P, st, "a_junk", [128, 1024], BF16); b_junk = P.buf()
        yb = [alloc(P, st, f"a_yb{i}", [128, 1024], BF16) for i in range(2)]; b_yb = [P.buf() for _ in range(2)]
        stt_ = alloc(P, st, "a_st", [128, NT, 4], F32); b_st = [P.buf() for _ in range(NT)]
        ps_t = [alloc(P, st, f"a_pst{i}", [128, 8, 128], BF16, psum=True) for i in range(2)]; b_pst = [P.buf() for _ in range(2)]
        def gen_mod():
            for ch in range(6):
                wb = wbuf[ch % 2]; bw = b_wbuf[ch % 2]
                P.dma(V(wb[:], bw), V(wada[:, :, ch * 512:(ch + 1) * 512], P.buf()))
                for kc in range(8):
                    P.mm(V(ps_b[:, :], b_psb), V(scc[:, kc, :], b_scc), V(wb[:, kc, :], bw), start=(kc == 0), stop=(kc == 7))
                P.copy("act", V(modrow[:, ch * 512:(ch + 1) * 512], b_modrow), V(ps_b[:, :], b_psb))
                yield
                for o4 in range(4):
                    oc = ch * 4 + o4
                    P.mm(V(ps_mod[:, oc, :], b_psmod), V(modrow[:, oc * 128:(oc + 1) * 128], b_modrow), V(C.idf[0:2, 0:2], C.b_idf))
                yield
            P.tt("dve", V(mod[:], b_mod), V(ps_mod[:], b_psmod), V(bada[:], b_bada), ALU.add)
            P.dma(V(out["modo"], out["b_modo"]), V(mod[:], b_mod))
            P.stt("dve", V(Asc[:], b_Asc), V(mod[:, 8:16, :], b_mod), 1.0, V(ng[:], b_ng), ALU.add, ALU.mult)
            yield
        P.memset("dve", V(stt_[:], b_st), 0.0)
        def gen_norm():
            for t in range(NT):
                x_ = xt[t % 3]; bx = b_xt[t % 3]
                P.dma(V(x_[:], bx), V(xin[t * 128:(t + 1) * 128, :], P.buf()))
                s = stt_[:, t, :]
                P.act(V(junk[:], b_junk), V(x_[:], bx), AF.Square, accum=V(stt_[:, t, 0:1], b_st[t]))
                P.ts("dve", V(stt_[:, t, 1:2], b_st[t]), V(stt_[:, t, 0:1], b_st[t]), 1.0 / 1024, ALU.mult, 1e-6, ALU.add)
                P.act(V(stt_[:, t, 2:3], b_st[t]), V(stt_[:, t, 1:2], b_st[t]), AF.Ln)
                P.act(V(stt_[:, t, 3:4], b_st[t]), V(stt_[:, t, 2:3], b_st[t]), AF.Exp, scale=-0.5)
                y = yb[t % 2]; by = b_yb[t % 2]
                P.scale("dve" if t % 2 == 0 else "act", V(y[:], by), V(x_[:], bx), V(stt_[:, t, 3:4], b_st[t]))
                yield
                pt = ps_t[t % 2]; bp = b_pst[t % 2]
                for j in range(8):
                    P.transpose(V(pt[:, j, :], bp), V(y[:, j * 128:(j + 1) * 128], by), V(C.idb[:], C.b_idb))
                o0 = tok_off(t)
                P.copy("act" if t % 2 == 0 else "dve", V(hT[:, :, o0:o0 + 128], b_hT[t]), V(pt[:], bp))
                yield

        win = W["w_in"].rearrange("(kc p) n -> p kc n", p=128)

        def load_w(ci_):
            segs_ = W_CHUNKS[ci_][2]
            c0 = 0
            wi = (ci_ + 2) % 3
            for (w0, n) in segs_:
                P.dma(V(wbuf[wi][:, :, c0:c0 + n], b_wbuf[wi]), V(win[:, :, w0:w0 + n], P.buf()))
                c0 += n
        load_w(0)
        gens = [[gen_mod(), 1], [gen_norm(), 3]]
        while gens:
            for gd in list(gens):
                try:
                    for _ in range(gd[1]):
                        next(gd[0])
                except StopIteration:
                    gens.remove(gd)
        win = W["w_in"].rearrange("(kc p) n -> p kc n", p=128)
        wx = [alloc(P, st, f"a_wx{i}", [128, 8, 512], BF16) for i in range(2)]; b_wx = [P.buf() for _ in range(2)]
        wc = [alloc(P, st, f"a_wc{i}", [128, 8, 512], BF16) for i in range(2)]; b_wc = [P.buf() for _ in range(2)]
        browf = [alloc(P, st, f"a_browf{i}", [2, 512], F32) for i in range(2)]; b_browf = [P.buf() for _ in range(2)]
        bcol = [alloc(P, st, f"a_bcol{i}", [128, 4, 2], F32) for i in range(2)]; b_bcol = [P.buf() for _ in range(2)]
        bbc = [alloc(P, st, f"a_bbc{i}", [128, 512], F32) for i in range(2)]; b_bbc = [P.buf() for _ in range(2)]
        ps_m = [alloc(P, st, f"a_psm{i}", [128, 512], F32, psum=True) for i in range(3)]; b_psm = [P.buf() for _ in range(3)]
        sgf = [alloc(P, st, f"a_sgf{i}", [128, 512], BF16) for i in range(3)]; b_sgf = [P.buf() for _ in range(3)]
        sgt = [alloc(P, st, f"a_sgt{i}", [128, 512], F32) for i in range(3)]; b_sgt = [P.buf() for _ in range(3)]
        all_hT = b_hT
        nmm = 0
        for ci, (kind, dst, segs) in enumerate(W_CHUNKS):
            wb = wbuf[(ci + 2) % 3]; bw = b_wbuf[(ci + 2) % 3]
            ncol = sum(n for _, n in segs)
            if ci + 1 < len(W_CHUNKS):
                load_w(ci + 1)
            wxx = wx[ci % 2]; bwx = b_wx[ci % 2]; wcc = wc[ci % 2]; bwc = b_wc[ci % 2]
            for kc in range(8):
                P.scale("dve", V(wxx[:, kc, :ncol], bwx), V(wb[:, kc, :ncol], bw), V(Asc[:, kc, 0:1], b_Asc))
                P.scale("act" if kc % 4 else "dve", V(wcc[:, kc, :ncol], bwc), V(wb[:, kc, :ncol], bw), V(Asc[:, kc, 1:2], b_Asc))
            for kc in range(8):
                P.mm(V(ps_b[:, :ncol], b_psb), V(mod[:, kc, :], b_mod), V(wb[:, kc, :ncol], bw), start=(kc == 0), stop=(kc == 7))
            br = browf[ci % 2]; bbr = b_browf[ci % 2]
            P.copy("act", V(br[:, :ncol], bbr), V(ps_b[:, :ncol], b_psb))
            if kind == "feat":
                for fc in range(ncol // 128):
                    P.mm(V(ps_mod[:, fc, :], b_psmod), V(br[:, fc * 128:(fc + 1) * 128], bbr), V(C.idf[0:2, 0:2], C.b_idf))
                P.copy("dve", V(bcol[ci % 2][:], b_bcol[ci % 2]), V(ps_mod[:, 0:4, :], b_psmod))
            else:
                for w_ in range(2):
                    pmb = ps_m[nmm % 3]; bpmb = b_psm[nmm % 3]; nmm += 1
                    selw = C.self_[:, 0:128] if w_ == 0 else C.self_[:, 128:256]
                    P.mm(V(pmb[:, :ncol], bpmb), V(selw, C.b_self), V(br[:, :ncol], bbr))
                    P.copy("act", V(bbc[w_][:, :ncol], b_bbc[w_]), V(pmb[:, :ncol], bpmb))
            if kind == "feat":
                for fc in range(ncol // 128):
                    for (g0, gn, wsel, bw_sel, hbufs) in ([] if (l == 1 and ci == 5) else [(2, 256, wcc, bwc, all_hT[0:2])]) + [
                            (262 + 512 * g, 512, wxx, bwx, all_hT[2 + 4 * g:6 + 4 * g]) for g in range(4)]:
                        pm = ps_m[nmm % 3]; bpm = b_psm[nmm % 3]
                        for kc in range(8):
                            P.mm(V(pm[:, :gn], bpm), V(wsel[:, kc, fc * 128:(fc + 1) * 128], bw_sel), V(hT[:, kc, g0:g0 + gn], hbufs),
                                 start=(kc == 0), stop=(kc == 7))
                        sg = sgf[nmm % 3]; bsg = b_sgf[nmm % 3]
                        wcol = 1 if g0 == 2 else 0
                        P.act(V(sg[:, :gn], bsg), V(pm[:, :gn], bpm), AF.Identity, bias=V(bcol[ci % 2][:, fc, wcol:wcol + 1], b_bcol[ci % 2]))
                        r0 = dst + fc * 128
                        P.dma(V(out["PF"][r0:r0 + 128, g0:g0 + gn], out["b_PF"]), V(sg[:, :gn], bsg), eng="sp")
                        nmm += 1
            else:
                for t in range(NT):
                    if l == 1 and t < 2 and ci in (4, 6, 7):
                        continue
                    o0 = tok_off(t)
                    wsel, bw_sel = (wcc, bwc) if t < 2 else (wxx, bwx)
                    pm = ps_m[nmm % 3]; bpm = b_psm[nmm % 3]
                    for kc in range(8):
                        P.mm(V(pm[:, :ncol], bpm), V(hT[:, kc, o0:o0 + 128], b_hT[t]), V(wsel[:, kc, :ncol], bw_sel), start=(kc == 0), stop=(kc == 7))
                    sg = sgt[nmm % 3]; bsg = b_sgt[nmm % 3]
                    wrow = 1 if t < 2 else 0
                    P.tt("dve", V(sg[:, :ncol], bsg), V(pm[:, :ncol], bpm), V(bbc[wrow][:, :ncol], b_bbc[wrow]), ALU.add)
                    P.dma(V(out["PT"][t * 128:(t + 1) * 128, dst:dst + ncol], out["b_PT"]), V(sg[:, :ncol], bsg), eng="sp")
                    nmm += 1
    P.barrier()


NEG = -1.0e30


def gdn_consts_host():
    i = np.arange(128)
    J, I = np.meshgrid(i, i, indexing="ij")
    d = {}
    d["mC0"] = np.where(I >= J, 0.0, NEG).astype(np.float32)
    d["mS0"] = np.where(I > J, 0.0, NEG).astype(np.float32)
    d["mC1"] = np.where(I <= J, 0.0, NEG).astype(np.float32)
    d["mS1"] = np.where(I < J, 0.0, NEG).astype(np.float32)
    d["triF"] = (J <= I).astype(np.float32)
    d["triB"] = (J >= I).astype(np.float32)
    d["bd32"] = (((J // 32) == (I // 32)) & (J != I)).astype(np.float32)
    d["off64"] = (((J // 64) == (I // 64)) & ((J // 32) != (I // 32))).astype(np.float32)
    d["off128"] = ((J // 64) != (I // 64)).astype(np.float32)
    return {"gmask": np.ascontiguousarray(np.stack([d[k] for k in ("mC0", "mS0", "mC1", "mS1", "triF", "triB", "bd32", "off64", "off128")], 1))}


def stage_B(P, C, l, Wd, io, stop=0):
    nc = P.nc
    with ExitStack() as st:
        gm = alloc(P, st, "b_gm", [128, 9, 128], F32); b_gm = P.buf()
        P.dma(V(gm[:], b_gm), V(Wd["gmask"], P.buf()))
        gmb = alloc(P, st, "b_gmb", [128, 3, 128], BF16); b_gmb = P.buf()
        P.copy("dve", V(gmb[:], b_gmb), V(gm[:, 6:9, :], b_gm))
        qkvc = alloc(P, st, "b_qkvc", [128, NT, 1536], BF16); b_qkvc = [P.buf() for _ in range(NT)]
        pp = [alloc(P, st, f"b_pp{i}", [128, 4, 128], F32, psum=True) for i in range(7)]; b_pp = [P.buf() for _ in range(7)]
        ppi = [0]

        def nps():
            k = ppi[0] % 7; ppi[0] += 1
            return pp[k], b_pp[k]

        if stop in (3, 4):
            return
        NS = NT * 8
        ba = alloc(P, st, "b_ba", [128, NT, 16], F32); b_ba = P.mbuf()
        for t in range(NT):
            P.dma(V(ba[:, t, :], b_ba), V(io["PT"][t * 128:(t + 1) * 128, 512:528], io["b_PT"]))
        alog = alloc(P, st, "b_alog", [128, NT, 8], F32); b_alog = P.buf()
        dtb = alloc(P, st, "b_dtb", [128, NT, 8], F32); b_dtb = P.buf()
        P.dma(V(alog[:], b_alog), V(Wd["alog"], P.buf()))
        P.dma(V(dtb[:], b_dtb), V(Wd["dtb"], P.buf()))
        names = ["beta", "negb", "g", "gc", "ngc2", "egc", "ekd", "cq", "ck1", "ck2", "bck2", "egt", "t1", "t2", "t3"]
        S_ = {}
        for n in names:
            S_[n] = (alloc(P, st, "b_s_" + n, [128, NT, 8], F32), P.buf())
        def sv(n): return V(S_[n][0][:], S_[n][1])
        bv = V(ba[:, :, 0:8], b_ba); av = V(ba[:, :, 8:16], b_ba)
        if stop == 7:
            return
        P.act(sv("t1"), bv, AF.Exp, scale=-1.0)
        P.ts("dve", sv("t2"), sv("t1"), 1.0, ALU.add)
        P.recip(sv("beta"), sv("t2"))
        P.ts("dve", sv("negb"), sv("beta"), -1.0, ALU.mult)
        if stop == 8:
            return
        P.tt("dve", sv("t1"), av, V(dtb[:], b_dtb), ALU.add)
        P.act(sv("t2"), sv("t1"), AF.Exp)
        P.ts("dve", sv("t3"), sv("t2"), 1.0, ALU.add)
        P.act(sv("t1"), sv("t3"), AF.Ln)
        P.act(sv("t2"), V(alog[:], b_alog), AF.Exp)
        P.stt("dve", sv("g"), sv("t1"), -1.0, sv("t2"), ALU.mult, ALU.mult)
        if stop == 5:
            return
        g2 = S_["g"][0][:].rearrange("p t e -> p (t e)")
        pm, bpm = nps(); pm2, bpm2 = nps(); pm3, bpm3 = nps()
        pmf = pm[:].rearrange("p a b -> p (a b)"); pm2f = pm2[:].rearrange("p a b -> p (a b)"); pm3f = pm3[:].rearrange("p a b -> p (a b)")
        P.mm(V(pmf[:, :NS], bpm), V(gm[:, 4, :], b_gm), V(g2, S_["g"][1]))
        P.mm(V(pm2f[:, :NS], bpm2), V(gm[:, 5, :], b_gm), V(g2, S_["g"][1]))
        P.mm(V(pm3f[:, :NS], bpm3), V(C.onesf[:], C.b_onesf), V(g2, S_["g"][1]))
        gc = S_["gc"][0]
        P.copy("dve", V(gc[:, :, 0:4], S_["gc"][1]), V(pmf[:, :NS].rearrange("p (t e) -> p t e", e=8)[:, :, 0:4], bpm))
        P.copy("dve", V(gc[:, :, 4:8], S_["gc"][1]), V(pm2f[:, :NS].rearrange("p (t e) -> p t e", e=8)[:, :, 4:8], bpm2))
        P.copy("dve", sv("t3"), V(pm3f[:, :NS].rearrange("p (t e) -> p t e", e=8), bpm3))
        P.act(sv("egt"), sv("t3"), AF.Exp)
        P.ts("dve", sv("ngc2"), sv("gc"), -1.0, ALU.mult)
        P.tt("dve", sv("t1"), sv("t3"), sv("ngc2"), ALU.add)
        P.act(sv("ekd"), sv("t1"), AF.Exp)
        P.act(sv("egc"), sv("gc"), AF.Exp)
        ssq = alloc(P, st, "b_ssq", [128, NT, 8], F32); b_ssq = P.buf()
        rqk = alloc(P, st, "b_rqk", [128, NT, 8], F32); b_rqk = P.buf()
        sqt = [alloc(P, st, f"b_sqt{i}", [128, 8, 128], BF16) for i in range(1)] * 2; b_sqt = [P.buf()] * 2
        P.memset("dve", V(ssq[:], b_ssq), 0.0)
        with ExitStack() as st2:
            PFs = alloc(P, st2, "b_PFs", [128, 12, TP], BF16); b_PFs = [P.mbuf() for _ in range(3)]
            gcv = alloc(P, st2, "b_gcv", [128, 12, 5], F32); b_gcv = P.buf()
            dg = alloc(P, st2, "b_dg", [128, 60, 128], BF16); b_dg = P.buf()
            P.dma(V(gcv[:], b_gcv), V(Wd["gconv"], P.buf()))
            P.memset("pool", V(PFs[:, :, 0:2], b_PFs), 0.0)
            P.memset("pool", V(PFs[:, :, 258:262], b_PFs), 0.0)
            P.memset("pool", V(PFs[:, :, 2310:2312], b_PFs), 0.0)
            pfv = io["PF"][0:1536, :].rearrange("(fc p) t -> p fc t", p=128)
            for fc in range(12):
                P.dma(V(PFs[:, fc, 2:258], b_PFs[fc // 4]), V(pfv[:, fc, 2:258], io["b_PF"]))
                P.dma(V(PFs[:, fc, 262:2310], b_PFs[fc // 4]), V(pfv[:, fc, 262:2310], io["b_PF"]))
            for fc in range(12):
                for j in range(5):
                    P.scale("act" if (fc + j) % 2 else "dve", V(dg[:, fc * 5 + j, :], b_dg), V(C.idb[:], C.b_idb), V(gcv[:, fc, j:j + 1], b_gcv))
            for g in range(3):
                for t in range(NT if stop != 3 else 0):
                    o0 = tok_off(t)
                    pm, bpm = nps()
                    for f4 in range(4):
                        fc = g * 4 + f4
                        for j in range(5):
                            P.mm(V(pm[:, f4, :], bpm), V(PFs[:, fc, o0 + j - 2:o0 + j - 2 + 128], b_PFs[g]), V(dg[:, fc * 5 + j, :], b_dg),
                                 start=(j == 0), stop=(j == 4))
                    P.act(V(qkvc[:, t, g * 512:(g + 1) * 512], b_qkvc[t]), V(pm[:].rearrange("p a b -> p (a b)"), bpm), AF.Silu)
                    if g == 1:
                        sq = sqt[0]; bsq = b_sqt[0]
                        for h8 in range(8):
                            P.act(V(sq[:, h8, :], bsq), V(qkvc[:, t, h8 * 128:(h8 + 1) * 128], b_qkvc[t]), AF.Square, accum=V(ssq[:, t, h8:h8 + 1], b_ssq))
        P.barrier()
        if "dbg" in io and False:
            for tt_ in range(2):
                stg0 = alloc(P, st, f"b_dstq{tt_}", [128, 512], F32); bs0 = P.buf()
                P.copy("dve", V(stg0[:], bs0), V(qkvc[:, tt_ * 8, 0:512], b_qkvc[tt_ * 8]))
                P.dma(V(io["dbg"][:, 22 + tt_, :], io["b_dbg"]), V(stg0[:], bs0))
        if stop == 6:
            return
        P.ts("dve", V(ssq[:], b_ssq), V(ssq[:], b_ssq), 1e-6, ALU.add)
        P.act(V(rqk[:], b_rqk), V(ssq[:], b_ssq), AF.Ln)
        P.act(V(rqk[:], b_rqk), V(rqk[:], b_rqk), AF.Exp, scale=-0.5)
        P.ts("dve", V(rqk[:, :, 0:4], b_rqk), V(rqk[:, :, 0:4], b_rqk), 128.0 ** -0.5, ALU.mult)
        if "dbg" in io:
            P.dma(V(io["dbg"][:, 20, 0:NS], io["b_dbg"]), V(ssq[:].rearrange("p a b -> p (a b)"), b_ssq))
            P.dma(V(io["dbg"][:, 21, 0:NS], io["b_dbg"]), V(rqk[:].rearrange("p a b -> p (a b)"), b_rqk))
        for d in range(2):
            sl = slice(4 * d, 4 * d + 4)
            P.tt("dve", V(S_["cq"][0][:, :, sl], S_["cq"][1]), V(S_["egc"][0][:, :, sl], S_["egc"][1]), V(rqk[:, :, 0:4], b_rqk), ALU.mult)
            P.tt("dve", V(S_["ck1"][0][:, :, sl], S_["ck1"][1]), V(S_["egc"][0][:, :, sl], S_["egc"][1]), V(rqk[:, :, 4:8], b_rqk), ALU.mult)
            P.tt("dve", V(S_["ck2"][0][:, :, sl], S_["ck2"][1]), V(S_["ekd"][0][:, :, sl], S_["ekd"][1]), V(rqk[:, :, 4:8], b_rqk), ALU.mult)
        P.tt("dve", sv("bck2"), sv("beta"), sv("ck2"), ALU.mult)
        NW = 4
        def wt(name, dt=BF16):
            return [(alloc(P, st, f"b_w_{name}{i}", [128, 4, 128], dt), P.buf()) for i in range(NW)]
        NW = 4
        SLOTS = ["Dk", "Dq", "Dkg", "Dqd", "khT", "qhT", "kgT", "qdT", "E1", "E2", "M", "Mn", "Md", "Mdn", "Mo1", "Mo2", "Ya", "Aqk", "kd"]
        ALIAS = {"Pa": "Dk", "Pb": "Dq", "Pna": "Dkg", "Pnb": "Dqd", "Yn": "khT", "Wm": "qhT", "r": "E1", "vn": "E2", "Yb": "M"}
        Wt = {n: wt(n) for n in SLOTS}
        for a_, b_ in ALIAS.items():
            Wt[a_] = Wt[b_]
        Wf = {n: wt(n, F32) for n in ["Dgc", "NG2", "ost"]}
        Sst2 = [(alloc(P, st, f"b_S{i}", [128, 4, 128], F32), P.buf()) for i in range(2)]
        Sb2 = [(alloc(P, st, f"b_Sb{i}", [128, 4, 128], BF16), P.buf()) for i in range(2)]
        idb4 = V(C.idb[:].unsqueeze(1).to_broadcast([128, 4, 128]), C.b_idb)
        idf4 = V(C.idf[:].unsqueeze(1).to_broadcast([128, 4, 128]), C.b_idf)
        it = [0]
        ev = [0]

        def evac(out, pm_v):
            ev[0] += 1
            P.copy("act" if ev[0] % 3 else "dve", out, pm_v)

        dcount = [0]; dstg = []
        def dump(v):
            if "dbg" not in io or dcount[0] >= 24:
                return
            if dcount[0] == 0:
                dstg.append((alloc(P, st, "b_dstg", [128, 4, 128], F32), P.buf()))
            stg, bs = dstg[0]
            P.copy("dve", V(stg[:], bs), v)
            P.dma(V(io["dbg"][:, dcount[0], :], io["b_dbg"]), V(stg[:].rearrange("p a b -> p (a b)"), bs))
            dcount[0] += 1

        def bc(name, t, d):
            a, b_ = S_[name]
            return V(a[:, t, 4 * d:4 * d + 4].unsqueeze(2).to_broadcast([128, 4, 128]), b_)

        def mask4(k):
            return V(gm[:, k, :].unsqueeze(1).to_broadcast([128, 4, 128]), b_gm)

        def maskb4(k):
            return V(gmb[:, k, :].unsqueeze(1).to_broadcast([128, 4, 128]), b_gmb)

        def mm4(lhs, rhs, lhs_k=None):
            pm, bpm = nps()
            for h in range(4):
                P.mm(V(pm[:, h, :], bpm), V(lhs[0][:, h, :], lhs[1]), V(rhs[0][:, h, :], rhs[1]))
            return V(pm[:], bpm)

        def tr4(src):
            pm, bpm = nps()
            pb = pm[:].rearrange("p a b -> p (a b)").bitcast(BF16)
            for h in range(4):
                P.transpose(V(pb[:, h * 128:(h + 1) * 128], bpm), V(src[0][:, h, :], src[1]), V(C.idb[:], C.b_idb))
            return V(pb[:, 0:512].rearrange("p (h i) -> p h i", i=128), bpm)


        def body(d, n_, t, w):
            X = {n: (Wt[n][w][0], Wt[n][w][1]) for n in Wt}
            Xf = {n: (Wf[n][w][0], Wf[n][w][1]) for n in Wf}
            xv = lambda n: V(X[n][0][:], X[n][1])
            xfv = lambda n: V(Xf[n][0][:], Xf[n][1])
            Sst, b_S = Sst2[d]; Sb, b_Sb = Sb2[d]
            rqb = V(rqk[:, t, 0:4].unsqueeze(2).to_broadcast([128, 4, 128]), b_rqk)
            rkb = V(rqk[:, t, 4:8].unsqueeze(2).to_broadcast([128, 4, 128]), b_rqk)
            P.tt("pool", xv("Dk"), idb4, rkb, ALU.mult)
            P.tt("pool", xv("Dq"), idb4, rqb, ALU.mult)
            yield
            P.tt("pool", xv("Dkg"), idb4, bc("ck1", t, d), ALU.mult)
            P.tt("pool", xv("Dqd"), idb4, bc("cq", t, d), ALU.mult)
            yield
            P.tt("pool", xfv("Dgc"), idf4, bc("gc", t, d), ALU.mult)
            P.tt("pool", xfv("NG2"), mask4(0 + 2 * d), bc("ngc2", t, d), ALU.add)
            yield
            qc = (qkvc[:, t, 0:512].rearrange("p (h d) -> p h d", d=128), b_qkvc[t])
            kc = (qkvc[:, t, 512:1024].rearrange("p (h d) -> p h d", d=128), b_qkvc[t])
            vc = (qkvc[:, t, 1024:1536].rearrange("p (h d) -> p h d", d=128), b_qkvc[t])
            evac(xv("khT"), mm4(kc, X["Dk"]))
            yield
            evac(xv("qhT"), mm4(qc, X["Dq"]))
            yield
            evac(xv("kgT"), mm4(kc, X["Dkg"]))
            yield
            evac(xv("qdT"), mm4(qc, X["Dqd"]))
            yield
            for nm, ng in (("E2", "NG2"),):
                pm, bpm = nps()
                pmf_ = pm[:].rearrange("p a b -> p (a b)")
                P.mm(V(pmf_, bpm), V(C.onesf[:], C.b_onesf), V(Xf["Dgc"][0][:].rearrange("p a b -> p (a b)"), Xf["Dgc"][1]), start=True, stop=False)
                P.mm(V(pmf_, bpm), V(C.idf[:], C.b_idf), V(Xf[ng][0][:].rearrange("p a b -> p (a b)"), Xf[ng][1]), start=False, stop=True)
                P.act(xv(nm), V(pm[:], bpm), AF.Exp)
                yield
            G = mm4(X["khT"], X["khT"])
            P.tt("dve", xv("E1"), G, xv("E2"), ALU.mult)
            P.tt("dve", xv("M"), xv("E1"), bc("negb", t, d), ALU.mult)
            yield
            QK = mm4(X["khT"], X["qhT"])
            P.tt("dve", xv("Aqk"), QK, xv("E2"), ALU.mult)
            yield
            evac(xv("Mn"), tr4(X["M"]))
            P.tt("pool", xv("Md"), xv("M"), maskb4(0), ALU.mult)
            P.tt("pool", xv("Ya"), xv("Md"), idb4, ALU.add)
            yield
            P.tt("pool", xv("Mdn"), xv("Mn"), maskb4(0), ALU.mult)
            P.tt("pool", xv("Mo1"), xv("Mn"), maskb4(1), ALU.mult)
            P.tt("pool", xv("Mo2"), xv("Mn"), maskb4(2), ALU.mult)
            yield
            Pc, Pn, Yc = "Md", "Mdn", "Ya"
            Pnext, Pnnext, Ynext = ["Pa", "Pb"], ["Pna", "Pnb"], ["Yb", "Ya"]
            for k in range(1, 5):
                pk_n = Pnnext[k % 2]
                evac(xv(pk_n), mm4(X[Pc], X[Pn]))
                yield
                if k < 4:
                    pk = Pnext[k % 2]
                    evac(xv(pk), mm4(X[Pn], X[Pc]))
                    yield
                yk = Ynext[(k - 1) % 2]
                P.tt("dve", xv(yk), mm4(X[pk_n], X[Yc]), xv(Yc), ALU.add)
                yield
                Pn = pk_n
                if k < 4:
                    Pc = pk
                Yc = yk
            for mo in ("Mo1", "Mo2"):
                evac(xv("Yn"), tr4(X[Yc]))
                yield
                evac(xv("Wm"), mm4(X[mo], X[Yc]))
                yield
                yk = "Ya" if Yc == "Yb" else "Yb"
                P.tt("dve", xv(yk), mm4(X["Yn"], X["Wm"]), xv(Yc), ALU.add)
                yield
                Yc = yk
            if n_ == 0:
                P.memset("pool", V(Sst[:], b_S), 0.0)
                P.memset("pool", V(Sb[:], b_Sb), 0.0)
            SbT = (Sb, b_Sb)
            P.stt("dve", xv("r"), mm4(X["kgT"], SbT), -1.0, V(vc[0], vc[1]), ALU.mult, ALU.add)
            yield
            vp = mm4(X[Yc], X["r"])
            P.tt("dve", xv("vn"), vp, bc("beta", t, d), ALU.mult)
            P.tt("dve", xv("kd"), vp, bc("bck2", t, d), ALU.mult)
            yield
            pm, bpm = nps()
            for h in range(4):
                P.mm(V(pm[:, h, :], bpm), V(X["qdT"][0][:, h, :], X["qdT"][1]), V(Sb[:, h, :], b_Sb), start=True, stop=False)
                P.mm(V(pm[:, h, :], bpm), V(X["Aqk"][0][:, h, :], X["Aqk"][1]), V(X["vn"][0][:, h, :], X["vn"][1]), start=False, stop=True)
            P.copy("act", xfv("ost"), V(pm[:], bpm))
            P.dma(V(io["OA"][t * 128:(t + 1) * 128, d * 512:(d + 1) * 512], io["b_OA"]),
                  V(Xf["ost"][0][:].rearrange("p h v -> p (h v)"), Xf["ost"][1]), eng="sp")
            yield
            Sp = mm4(kc, X["kd"])
            for h_ in range(4):
                P.stt("dve", V(Sst[:, h_, :], b_S), V(Sst[:, h_, :], b_S), V(S_["egt"][0][:, t, 4 * d + h_:4 * d + h_ + 1], S_["egt"][1]),
                      V(Sp.ap[:, h_, :], Sp.bufs), ALU.mult, ALU.add)
            P.copy("act", V(Sb[:], b_Sb), V(Sst[:], b_S))
            yield

        def stream(d, par):
            order = list(range(NT)) if d == 0 else [1, 0] + list(range(NT - 1, 1, -1))
            for n_, t in enumerate(order if stop != 2 else order[:1]):
                if n_ % 2 == par:
                    yield from body(d, n_, t, 2 * d + par)

        LROUND = 33
        streams = []
        if stop != 1:
            streams = [[stream(0, 0), 0], [stream(1, 0), 0], [stream(0, 1), LROUND // 2 + 1], [stream(1, 1), LROUND // 2 + 1]]
        rnd = 0
        while streams:
            for sd in list(streams):
                if rnd < sd[1]:
                    continue
                try:
                    next(sd[0])
                except StopIteration:
                    streams.remove(sd)
            rnd += 1
    P.barrier()


CSTOP = [0]


def mla_perm():
    idx = []
    for h in range(4):
        base = h * 96
        idx += list(range(base, base + 64)) + list(range(base + 64, base + 96, 2)) + list(range(base + 65, base + 96, 2))
    return np.array(idx)


def stage_C(P, C, l, Wd, io, do_ctx):
    with ExitStack() as st:
        pp = [alloc(P, st, f"c_pp{i}", [128, 512], F32, psum=True) for i in range(7)]; b_pp = [P.buf() for _ in range(7)]
        ppi = [0]

        def nps3():
            k = ppi[0] % 7; ppi[0] += 1
            return pp[k], b_pp[k]
        wq1f = alloc(P, st, "c_wq1f", [128, 384], F32); wq2f = alloc(P, st, "c_wq2f", [64, 384], F32); wkf = alloc(P, st, "c_wkf", [128, 512], F32)
        gq1 = alloc(P, st, "c_gq1", [128, 1], F32); gq2 = alloc(P, st, "c_gq2", [64, 1], F32); gk = alloc(P, st, "c_gk", [128, 1], F32)
        wq1 = alloc(P, st, "c_wq1", [128, 384], BF16); wq2 = alloc(P, st, "c_wq2", [128, 384], BF16); wk = alloc(P, st, "c_wk", [128, 512], BF16)
        bw = P.mbuf(); bwb = P.buf()
        P.dma(V(wq1f[:], bw), V(Wd["wqb"][0:128, :], P.buf())); P.dma(V(wq2f[:], bw), V(Wd["wqb"][128:192, :], P.buf()))
        P.dma(V(wkf[:], bw), V(Wd["wkvb"], P.buf()))
        P.dma(V(gq1[:], bw), V(Wd["gq"][0:128, :], P.buf())); P.dma(V(gq2[:], bw), V(Wd["gq"][128:192, :], P.buf()))
        P.dma(V(gk[:], bw), V(Wd["gkv"], P.buf()))
        P.ts("dve", V(wq1[:], bwb), V(wq1f[:], bw), V(gq1[:], bw), ALU.mult)
        P.memset("dve", V(wq2[:], bwb), 0.0)
        P.ts("dve", V(wq2[0:64, :], bwb), V(wq2f[:], bw), V(gq2[:], bw), ALU.mult)
        P.ts("dve", V(wk[:], bwb), V(wkf[:], bw), V(gk[:], bw), ALU.mult)
        rp = alloc(P, st, "c_rope", [128, 16, 48], F32); b_rp = P.buf()
        P.dma(V(rp[:], b_rp), V(Wd["rope"], P.buf()))
        kT = alloc(P, st, "c_kT", [96, 4, T], BF16); b_kT = [P.buf() for _ in range(NT)]
        qT = alloc(P, st, "c_qT", [96, 4, T], BF16); b_qT = [P.buf() for _ in range(NT)]
        Va = alloc(P, st, "c_Va", [128, NT, 4, 66], BF16); b_Va = [P.buf() for _ in range(NT)]
        P.memset("pool", V(Va[:], b_Va), 1.0)
        qa = [alloc(P, st, f"c_qa{i}", [128, 352], F32) for i in range(4)]; b_qa = [P.buf() for _ in range(4)]
        stt_ = alloc(P, st, "c_st", [128, NT, 8], F32); b_stl = [P.buf() for _ in range(NT)]
        P.memset("dve", V(stt_[:], b_stl), 0.0)
        junk = alloc(P, st, "c_junk", [128, 192], BF16); b_junk = P.buf()
        qn = [alloc(P, st, f"c_qn{i}", [128, 320], BF16) for i in range(4)]; b_qn = [P.buf() for _ in range(4)]
        qnT = [alloc(P, st, f"c_qnT{i}", [128, 3, 128], BF16) for i in range(4)]; b_qnT = [P.buf() for _ in range(4)]
        for i_ in range(4):
            P.memset("pool", V(qnT[i_][:], b_qnT[i_]), 0.0)
        qf = [alloc(P, st, f"c_qf{i}", [128, 4, 96], F32) for i in range(4)]; b_qf = [P.buf() for _ in range(4)]
        kpe = [alloc(P, st, f"c_kpe{i}", [128, 32], F32) for i in range(4)]; b_kpe = [P.buf() for _ in range(4)]
        tmp = [alloc(P, st, f"c_tmp{i}", [128, 4, 4, 16], F32) for i in range(4)]; b_tmp = [P.buf() for _ in range(4)]
        qtok = [alloc(P, st, f"c_qtok{i}", [128, 4, 96], BF16) for i in range(4)]; b_qtok = [P.buf() for _ in range(4)]
        ktok = [alloc(P, st, f"c_ktok{i}", [128, 4, 96], BF16) for i in range(4)]; b_ktok = [P.buf() for _ in range(4)]
        def c1_body(t):
                w = t % 4
                P.dma(V(qa[w][:], b_qa[w]), V(io["PT"][t * 128:(t + 1) * 128, 528:880], io["b_PT"]))
                P.act(V(junk[:, 0:192], b_junk), V(qa[w][:, 0:192], b_qa[w]), AF.Square, accum=V(stt_[:, t, 0:1], b_stl[t]))
                P.act(V(junk[:, 0:128], b_junk), V(qa[w][:, 192:320], b_qa[w]), AF.Square, accum=V(stt_[:, t, 1:2], b_stl[t]))
                P.ts("dve", V(stt_[:, t, 2:3], b_stl[t]), V(stt_[:, t, 0:1], b_stl[t]), 1.0 / 192, ALU.mult, 1e-6, ALU.add)
                P.ts("dve", V(stt_[:, t, 3:4], b_stl[t]), V(stt_[:, t, 1:2], b_stl[t]), 1.0 / 128, ALU.mult, 1e-6, ALU.add)
                P.act(V(stt_[:, t, 4:6], b_stl[t]), V(stt_[:, t, 2:4], b_stl[t]), AF.Ln)
                P.act(V(stt_[:, t, 6:8], b_stl[t]), V(stt_[:, t, 4:6], b_stl[t]), AF.Exp, scale=-0.5)
                P.ts("dve", V(qn[w][:, 0:192], b_qn[w]), V(qa[w][:, 0:192], b_qa[w]), V(stt_[:, t, 6:7], b_stl[t]), ALU.mult)
                P.ts("dve", V(qn[w][:, 192:320], b_qn[w]), V(qa[w][:, 192:320], b_qa[w]), V(stt_[:, t, 7:8], b_stl[t]), ALU.mult)
                yield
                pm, bpm = nps3()
                pb = pm[:].bitcast(BF16)
                P.transpose(V(pb[:, 0:128], bpm), V(qn[w][:, 0:128], b_qn[w]), V(C.idb[:], C.b_idb))
                P.transpose(V(pb[0:64, 128:256], bpm), V(qn[w][:, 128:192], b_qn[w]), V(C.idb[:], C.b_idb))
                P.transpose(V(pb[:, 256:384], bpm), V(qn[w][:, 192:320], b_qn[w]), V(C.idb[:], C.b_idb))
                P.copy("act", V(qnT[w][:, 0, :], b_qnT[w]), V(pb[:, 0:128], bpm))
                P.copy("act", V(qnT[w][0:64, 1, :], b_qnT[w]), V(pb[0:64, 128:256], bpm))
                P.copy("act", V(qnT[w][:, 2, :], b_qnT[w]), V(pb[:, 256:384], bpm))
                if CSTOP[0] == 1:
                    return
                yield
                pq, bpq = nps3()
                P.mm(V(pq[:, 0:384], bpq), V(qnT[w][:, 0, :], b_qnT[w]), V(wq1[:], bwb), start=True, stop=False)
                P.mm(V(pq[:, 0:384], bpq), V(qnT[w][:, 1, :], b_qnT[w]), V(wq2[:], bwb), start=False, stop=True)
                if CSTOP[0] == 5:
                    return
                pk, bpk = nps3()
                P.mm(V(pk[:], bpk), V(qnT[w][:, 2, :], b_qnT[w]), V(wk[:], bwb))
                yield
                pq4 = pq[:, 0:384].rearrange("p (h d) -> p h d", d=96)
                pk4 = pk[:].rearrange("p (h d) -> p h d", d=128)
                if CSTOP[0] == 6:
                    return
                P.copy("dve", V(Va[:, t, :, 0:64], b_Va[t]), V(pk4[:, :, 64:128], bpk))
                if CSTOP[0] == 7:
                    return
                P.copy("dve", V(ktok[w][:, :, 0:64], b_ktok[w]), V(pk4[:, :, 0:64], bpk))
                if CSTOP[0] == 2:
                    return
                if t < 2:
                    P.copy("act", V(qtok[w][:], b_qtok[w]), V(pq4, bpq))
                    P.copy("dve", V(ktok[w][:, :, 64:80], b_ktok[w]), V(qa[w][:, 320:352:2].unsqueeze(1).to_broadcast([128, 4, 16]), b_qa[w]))
                    P.copy("dve", V(ktok[w][:, :, 80:96], b_ktok[w]), V(qa[w][:, 321:352:2].unsqueeze(1).to_broadcast([128, 4, 16]), b_qa[w]))
                else:
                    P.copy("act", V(qf[w][:], b_qf[w]), V(pq4, bpq))
                    P.copy("dve", V(qtok[w][:, :, 0:64], b_qtok[w]), V(qf[w][:, :, 0:64], b_qf[w]))
                    cosb = V(rp[:, t - 2, 0:16].unsqueeze(1).to_broadcast([128, 4, 16]), b_rp)
                    sinb = V(rp[:, t - 2, 16:32].unsqueeze(1).to_broadcast([128, 4, 16]), b_rp)
                    nsinb = V(rp[:, t - 2, 32:48].unsqueeze(1).to_broadcast([128, 4, 16]), b_rp)
                    x0 = V(qf[w][:, :, 64:80], b_qf[w]); x1 = V(qf[w][:, :, 80:96], b_qf[w])
                    tm = tmp[w]; btm = b_tmp[w]
                    P.tt("dve", V(tm[:, 0], btm), x0, cosb, ALU.mult)
                    P.tt("dve", V(tm[:, 1], btm), x1, nsinb, ALU.mult)
                    P.tt("dve", V(tm[:, 2], btm), x0, sinb, ALU.mult)
                    P.tt("dve", V(tm[:, 3], btm), x1, cosb, ALU.mult)
                    P.tt("dve", V(qtok[w][:, :, 64:80], b_qtok[w]), V(tm[:, 0], btm), V(tm[:, 1], btm), ALU.add)
                    P.tt("dve", V(qtok[w][:, :, 80:96], b_qtok[w]), V(tm[:, 2], btm), V(tm[:, 3], btm), ALU.add)
                    k0 = V(qa[w][:, 320:352:2], b_qa[w]); k1 = V(qa[w][:, 321:352:2], b_qa[w])
                    c1 = V(rp[:, t - 2, 0:16], b_rp); s1 = V(rp[:, t - 2, 16:32], b_rp); n1 = V(rp[:, t - 2, 32:48], b_rp)
                    kp = kpe[w]; bkp = b_kpe[w]
                    P.tt("dve", V(kp[:, 0:16], bkp), k0, c1, ALU.mult)
                    P.stt("dve", V(kp[:, 0:16], bkp), k1, 1.0, V(kp[:, 0:16], bkp), ALU.mult, ALU.add) if False else None
                    P.tt("dve", V(tm[:, 0, 0, :], btm), k1, n1, ALU.mult)
                    P.tt("dve", V(kp[:, 0:16], bkp), V(kp[:, 0:16], bkp), V(tm[:, 0, 0, :], btm), ALU.add)
                    P.tt("dve", V(kp[:, 16:32], bkp), k0, s1, ALU.mult)
                    P.tt("dve", V(tm[:, 1, 0, :], btm), k1, c1, ALU.mult)
                    P.tt("dve", V(kp[:, 16:32], bkp), V(kp[:, 16:32], bkp), V(tm[:, 1, 0, :], btm), ALU.add)
                    P.copy("dve", V(ktok[w][:, :, 64:96], b_ktok[w]), V(kp[:].unsqueeze(1).to_broadcast([128, 4, 32]), bkp))
                if CSTOP[0] == 3:
                    return
                yield
                for (src, bsrc, dstT, bdst) in ((ktok[w], b_ktok[w], kT, b_kT[t]), (qtok[w], b_qtok[w], qT, b_qT[t])):
                    pt_, bpt = nps3()
                    ptb = pt_[:].bitcast(BF16)
                    for h in range(4):
                        P.transpose(V(ptb[0:96, h * 128:(h + 1) * 128], bpt), V(src[:, h, :], bsrc), V(C.idb[:], C.b_idb))
                    P.copy("act" if dstT is kT else "dve", V(dstT[:, :, t * 128:(t + 1) * 128], bdst),
                           V(ptb[0:96, 0:512].rearrange("p (h i) -> p h i", i=128), bpt))

        def c1_stream(sidx):
            for t in range(NT):
                if t % 4 == sidx:
                    yield from c1_body(t)
                    yield
        c1s = [[c1_stream(i), 2 * i] for i in range(4)]
        rnd = 0
        while c1s:
            for sd in list(c1s):
                if rnd < sd[1]:
                    continue
                try:
                    next(sd[0])
                except StopIteration:
                    c1s.remove(sd)
            rnd += 1
        if CSTOP[0] in (1, 2, 3, 4, 5, 6, 7):
            return
        pT = [alloc(P, st, f"c_pT{i}", [128, 512], BF16) for i in range(4)]; b_pT = [P.buf() for _ in range(4)]
        SCB = (3, 4, 1, 2)
        ob = [alloc(P, st, f"c_ob{i}", [128, 4, 256], F32) for i in range(2)]; b_ob = [P.buf(), P.buf()]
        rec = alloc(P, st, "c_rec", [128, 64], F32); b_rec = P.buf()
        scale = 96.0 ** -0.5
        groups = [(2 + 4 * g, 4, list(range(NT))) for g in range(4)]
        if do_ctx:
            groups.append((0, 2, [0, 1]))
        oTs = [alloc(P, st, f"c_oTs{i}", [65, 512], F32) for i in range(2)]; b_oTs = [P.buf(), P.buf()]
        rec4 = [alloc(P, st, f"c_rec4{i}", [128, 4], F32) for i in range(2)]; b_rec4 = [P.buf(), P.buf()]
        its = []
        for gi, (t0, nq, kts) in enumerate(groups):
            for h in range(4):
                for ki, kt in enumerate(kts):
                    its.append((gi, t0, nq, h, ki, kt, len(kts)))

        def score(i):
            gi, t0, nq, h, ki, kt, nk = its[i]
            ps, bps = pp[SCB[i % 4]], b_pp[SCB[i % 4]]
            P.mm(V(ps[:, 0:nq * 128], bps), V(kT[:, h, kt * 128:(kt + 1) * 128], b_kT[kt]),
                 V(qT[:, h, t0 * 128:(t0 + nq) * 128], b_qT[t0:t0 + nq]))

        score(0)
        if len(its) > 1:
            score(1)
        for i, (gi, t0, nq, h, ki, kt, nk) in enumerate(its):
            nqc = nq * 128
            gh = gi * 4 + h
            obw = ob[gi % 2]; bobw = b_ob[gi % 2]
            acc, bacc = pp[5 + gh % 2], b_pp[5 + gh % 2]
            ps, bps = pp[SCB[i % 4]], b_pp[SCB[i % 4]]
            pt_ = pT[i % 4]; bpt = b_pT[i % 4]
            P.act(V(pt_[:, 0:nqc], bpt), V(ps[:, 0:nqc], bps), AF.Exp, scale=scale)
            if i + 2 < len(its):
                score(i + 2)
            P.mm(V(acc[0:65, 0:nqc], bacc), V(Va[:, kt, h, 0:65], b_Va[kt]), V(pt_[:, 0:nqc], bpt),
                 start=(ki == 0), stop=(ki == nk - 1))
            if ki == nk - 1:
                ot = oTs[gh % 2]; bot = b_oTs[gh % 2]
                P.copy("dve", V(ot[:, 0:nqc], bot), V(acc[0:65, 0:nqc], bacc))
                ptr_, bptr = pp[0], b_pp[0]
                p4 = ptr_[:].rearrange("p (a b) -> p a b", b=128)
                for qi in range(nq):
                    P.transpose(V(p4[:, qi, 0:65], bptr), V(ot[:, qi * 128:(qi + 1) * 128], bot), V(C.idf[0:65, 0:65], C.b_idf))
                r4 = rec4[gh % 2]; br4 = b_rec4[gh % 2]
                P.recip(V(r4[:, 0:nq].unsqueeze(2), br4), V(p4[:, 0:nq, 64:65], bptr))
                P.tt("dve", V(obw[:, 0:nq, h * 64:(h + 1) * 64], bobw), V(p4[:, 0:nq, 0:64], bptr),
                     V(r4[:, 0:nq].unsqueeze(2).to_broadcast([128, nq, 64]), br4), ALU.mult)
                if h == 3:
                    for qi in range(nq):
                        tt_ = t0 + qi
                        P.dma(V(io["OB"][tt_ * 128:(tt_ + 1) * 128, :], io["b_OB"]), V(obw[:, qi, :], bobw), eng="sp")
    P.barrier()


def stage_D(P, C, l, Wd, io, last):
    with ExitStack() as st:
        mod = alloc(P, st, "d_mod", [128, 24, 2], F32); b_mod = P.buf()
        P.dma(V(mod[:], b_mod), V(io["modo"], io["b_modo"]))
        pg = [alloc(P, st, f"d_pg{i}", [128, 512], F32, psum=True) for i in range(2)]; b_pg = [P.buf(), P.buf()]
        pcv = alloc(P, st, "d_pcv", [128, 256], F32, psum=True); b_pcv = P.buf()
        ptr = alloc(P, st, "d_ptr", [128, 8, 128], BF16, psum=True); b_ptr = P.buf()
        pw = [alloc(P, st, f"d_pw{i}", [128, 512], F32, psum=True) for i in range(2)]; b_pw = [P.buf(), P.buf()]
        Dg = [alloc(P, st, f"d_Dg{i}", [128, 128], F32) for i in range(2)]; b_Dg = [P.buf(), P.buf()]
        Wo = [alloc(P, st, f"d_Wo{i}", [128, 8, 1024], BF16) for i in range(2)]; b_Wo = [P.buf(), P.buf()]
        wf = [alloc(P, st, f"d_wf{i}", [128, 1024], F32) for i in range(4)]; b_wf = [P.buf() for _ in range(4)]
        hc = alloc(P, st, "d_hc", [128, 4, TP], BF16); b_hc = P.mbuf()
        prod = alloc(P, st, "d_prod", [128, 2, TP], BF16); b_prod = P.buf()
        ccv = alloc(P, st, "d_ccv", [128, 2, 3], F32); b_ccv = P.buf()
        dgc = alloc(P, st, "d_dgc", [128, 6, 128], BF16); b_dgc = P.buf()
        gng = alloc(P, st, "d_gng", [128, 4, 128], F32); b_gng = P.buf()
        P.dma(V(ccv[:], b_ccv), V(Wd["cconv"], P.buf()))
        P.dma(V(gng[:], b_gng), V(Wd["gng"], P.buf()))
        P.memset("pool", V(hc[:, :, 0:2], b_hc), 0.0)
        P.memset("pool", V(hc[:, :, 258:262], b_hc), 0.0)
        P.memset("pool", V(hc[:, :, 2310:2312], b_hc), 0.0)
        pfv = io["PF"][1536:2048, :].rearrange("(fc p) t -> p fc t", p=128)
        for fc in range(4):
            P.dma(V(hc[:, fc, 2:258], b_hc), V(pfv[:, fc, 2:258], io["b_PF"]))
            P.dma(V(hc[:, fc, 262:2310], b_hc), V(pfv[:, fc, 262:2310], io["b_PF"]))
        wov = Wd["w_out"].rearrange("(kc p) n -> p kc n", p=128)
        n = 0
        for w in ((0,) if last else (0, 1)):
            for j in range(8):
                dgt = Dg[j % 2]; bd = b_Dg[j % 2]
                P.scale("act", V(dgt[:], bd), V(C.idf[:], C.b_idf), V(mod[:, 16 + j, w:w + 1], b_mod))
                P.mm(V(pg[j // 4][:, (j % 4) * 128:(j % 4 + 1) * 128], b_pg[j // 4]), V(C.onesf[:], C.b_onesf), V(dgt[:], bd))
            for kc in range(8):
                wb = wf[n % 4]; bw = b_wf[n % 4]; n += 1
                P.dma(V(wb[:], bw), V(wov[:, kc, :], P.buf()))
                for hf in range(2):
                    P.tt("dve", V(Wo[w][:, kc, hf * 512:(hf + 1) * 512], b_Wo[w]), V(wb[:, hf * 512:(hf + 1) * 512], bw), V(pg[hf][:], b_pg[hf]), ALU.mult)
        for fc in range(2):
            P.tt("dve", V(prod[:, fc, :], b_prod), V(hc[:, fc, :], b_hc), V(hc[:, 2 + fc, :], b_hc), ALU.mult)
            for j in range(3):
                P.ts("dve", V(dgc[:, fc * 3 + j, :], b_dgc), V(C.idb[:], C.b_idb), V(ccv[:, fc, j:j + 1], b_ccv), ALU.mult)
        if last:
            fng = alloc(P, st, "d_fng", [128, 1024], F32); b_fng = P.buf()
            P.dma(V(fng[:], b_fng), V(Wd["fng"], P.buf()))
        NB = 3
        def mk(name, shape, dt):
            return [(alloc(P, st, f"d_{name}{i}", shape, dt), P.buf()) for i in range(NB)]
        B_ = {"oa": mk("oa", [128, 4, 128], F32), "oa2": mk("oa2", [128, 1024], F32), "ob": mk("ob", [128, 256], F32), "za": mk("za", [128, 512], F32), "zbg": mk("zbg", [128, 512], F32),
              "zc": mk("zc", [128, 256], F32), "x": mk("x", [128, 1024], F32), "sza": mk("sza", [128, 512], F32), "szb": mk("szb", [128, 256], F32),
              "szc": mk("szc", [128, 256], F32), "mix": mk("mix", [128, 1024], BF16), "mixT": mk("mixT", [128, 8, 128], BF16),
              "a1": mk("a1", [128, 4, 128], F32), "a2": mk("a2", [128, 4, 128], F32), "tc": mk("tc", [128, 256], F32), "xn": mk("xn", [128, 1024], F32),
              "y": mk("y", [128, 1024], F32)}
        stt_ = alloc(P, st, "d_st", [128, NT, 16], F32); b_st = P.buf()
        junk = alloc(P, st, "d_junk", [128, 1024], BF16); b_junk = P.buf()
        fin = alloc(P, st, "d_fin", [128, NT, 4], F32); b_fin = P.buf()
        PT = io["PT"]
        b_stt = [P.buf() for _ in range(NT)]; b_fint = [P.buf() for _ in range(NT)]
        P.memset("dve", V(stt_[:], b_stt), 0.0)
        P.memset("dve", V(fin[:], b_fint), 0.0)
        ptr2 = alloc(P, st, "d_ptr2", [128, 8, 128], BF16, psum=True); b_ptr2 = P.buf()
        ptr3 = pg[0][:].bitcast(BF16).rearrange("p (a b) -> p a b", b=128)
        PS = [(ptr, b_ptr, [pw[0], pw[0]], [b_pw[0], b_pw[0]]), (ptr2, b_ptr2, [pw[1], pw[1]], [b_pw[1], b_pw[1]]),
              (ptr3, b_pg[0], [pg[1], pg[1]], [b_pg[1], b_pg[1]])]

        def body(it, t):
            w = it % NB
            ptr_, b_ptr_, pw_, b_pw_ = PS[w]
            X = {k: v[w] for k, v in B_.items()}
            xv = lambda k: V(X[k][0][:], X[k][1])
            bst = b_stt[t]; bfin = b_fint[t]
            r = slice(t * 128, (t + 1) * 128)
            P.dma(xv("oa2"), V(io["OA"][r, :], io["b_OA"]))
            P.dma(xv("ob"), V(io["OB"][r, :], io["b_OB"]))
            P.dma(xv("za"), V(PT[r, 0:512], io["b_PT"]))
            P.dma(xv("zbg"), V(PT[r, 880:1392], io["b_PT"]))
            P.dma(xv("zc"), V(PT[r, 1392:1648], io["b_PT"]))
            P.dma(xv("x"), V(io["xin"][r, :], io["b_xin"]))
            yield
            P.tt("dve", V(X["oa"][0][:].rearrange("p h v -> p (h v)"), X["oa"][1]), V(X["oa2"][0][:, 0:512], X["oa2"][1]), V(X["oa2"][0][:, 512:1024], X["oa2"][1]), ALU.add)
            P.act(xv("sza"), xv("za"), AF.Silu)
            P.act(xv("szb"), V(X["zbg"][0][:, 0:256], X["zbg"][1]), AF.Silu)
            P.act(xv("szc"), xv("zc"), AF.Silu)
            yield
            for h in range(4):
                P.act(V(junk[:, 0:128], b_junk), V(X["oa"][0][:, h, :], X["oa"][1]), AF.Square, accum=V(stt_[:, t, h:h + 1], bst))
            P.ts("dve", V(stt_[:, t, 4:8], bst), V(stt_[:, t, 0:4], bst), 1.0 / 128, ALU.mult, 1e-6, ALU.add)
            P.act(V(stt_[:, t, 8:12], bst), V(stt_[:, t, 4:8], bst), AF.Ln)
            P.act(V(stt_[:, t, 12:16], bst), V(stt_[:, t, 8:12], bst), AF.Exp, scale=-0.5)
            yield
            rb = V(stt_[:, t, 12:16].unsqueeze(2).to_broadcast([128, 4, 128]), bst)
            P.tt("dve", xv("a1"), xv("oa"), rb, ALU.mult)
            P.tt("dve", xv("a2"), xv("a1"), V(gng[:], b_gng), ALU.mult)
            P.tt("dve", V(X["mix"][0][:, 0:512], X["mix"][1]), V(X["a2"][0][:].rearrange("p h v -> p (h v)"), X["a2"][1]), xv("sza"), ALU.mult)
            yield
            P.tt("dve", V(X["mix"][0][:, 512:768], X["mix"][1]), xv("ob"), xv("szb"), ALU.mult)
            o0 = tok_off(t)
            for fc in range(2):
                for j in range(3):
                    P.mm(V(pcv[:, fc * 128:(fc + 1) * 128], b_pcv), V(prod[:, fc, o0 + j - 1:o0 + j - 1 + 128], b_prod), V(dgc[:, fc * 3 + j, :], b_dgc),
                         start=(j == 0), stop=(j == 2))
            P.tt("dve", xv("tc"), V(X["zbg"][0][:, 256:512], X["zbg"][1]), xv("szc"), ALU.mult)
            P.tt("dve", V(X["mix"][0][:, 768:1024], X["mix"][1]), V(pcv[:], b_pcv), xv("tc"), ALU.mult)
            yield
            for j in range(8):
                P.transpose(V(ptr_[:, j, :], b_ptr_), V(X["mix"][0][:, j * 128:(j + 1) * 128], X["mix"][1]), V(C.idb[:], C.b_idb))
            P.copy("act", xv("mixT"), V(ptr_[:], b_ptr_))
            yield
            wsel = 1 if t < 2 else 0
            for hf in range(2):
                for kc in range(8):
                    P.mm(V(pw_[hf][:], b_pw_[hf]), V(X["mixT"][0][:, kc, :], X["mixT"][1]), V(Wo[wsel][:, kc, hf * 512:(hf + 1) * 512], b_Wo[wsel]),
                         start=(kc == 0), stop=(kc == 7))
                P.tt("dve", V(X["xn"][0][:, hf * 512:(hf + 1) * 512], X["xn"][1]), V(pw_[hf][:], b_pw_[hf]), V(X["x"][0][:, hf * 512:(hf + 1) * 512], X["x"][1]), ALU.add)
                yield
            if not last:
                P.dma(V(io["xout"][r, :], io["b_xout"]), xv("xn"), eng="sp")
            else:
                P.act(V(junk[:], b_junk), xv("xn"), AF.Square, accum=V(fin[:, t, 0:1], bfin))
                P.ts("dve", V(fin[:, t, 1:2], bfin), V(fin[:, t, 0:1], bfin), 1.0 / 1024, ALU.mult, 1e-6, ALU.add)
                P.act(V(fin[:, t, 2:3], bfin), V(fin[:, t, 1:2], bfin), AF.Ln)
                P.act(V(fin[:, t, 3:4], bfin), V(fin[:, t, 2:3], bfin), AF.Exp, scale=-0.5)
                yield
                P.stt("dve", xv("y"), xv("xn"), V(fin[:, t, 3:4], bfin), V(fng[:], b_fng), ALU.mult, ALU.mult)
                ro = slice((t - 2) * 128, (t - 1) * 128)
                P.dma(V(io["out"][ro, :], io["b_out"]), xv("y"), eng="sp")
            yield

        tiles = list(range(2 if last else 0, NT))

        def stream(sidx):
            for it, t in enumerate(tiles):
                if it % NB == sidx:
                    yield from body(it, t)

        streams = [[stream(i), 3 * i] for i in range(NB)]
        rnd = 0
        while streams:
            for sd in list(streams):
                if rnd < sd[1]:
                    continue
                try:
                    next(sd[0])
                except StopIteration:
                    streams.remove(sd)
            rnd += 1
    P.barrier()


def prep_consts():
    sel = np.zeros((2, 256), np.float32)
    sel[0, 0:128] = 1.0; sel[1, 128:256] = 1.0
    return {"ident": np.eye(128, dtype=np.float32), "sel": sel}


def fm(v, n):
    return np.ascontiguousarray(v.reshape(n, 128).T)


def prep_layer_inputs(inp, b, l):
    d = {}
    cc = np.stack([fm(inp["c"][b], 8), fm(inp["c_ctx"], 8)], -1)
    d["cc"] = np.ascontiguousarray(cc)
    ba = fm(inp["b_ada"][l], 24)
    d[f"bada{l}"] = np.ascontiguousarray(np.stack([ba, ba], -1))
    g = fm(inp["norm_g"][l], 8)
    d[f"ng{l}"] = np.ascontiguousarray(np.stack([g, g], -1))
    d[f"w_ada{l}"] = inp["w_ada"][l]
    d[f"w_in{l}"] = inp["w_in"][l]
    return d


def prep_layer_inputs_b(inp, l):
    d = {}
    gc = inp["gdn_conv"][l]
    d[f"gconv{l}"] = np.ascontiguousarray(gc.T.reshape(12, 128, 5).transpose(1, 0, 2))
    d[f"alog{l}"] = np.ascontiguousarray(np.broadcast_to(inp["gdn_a_log"][l].reshape(1, 1, 8), (128, NT, 8))).astype(np.float32)
    d[f"dtb{l}"] = np.ascontiguousarray(np.broadcast_to(inp["gdn_dt_bias"][l].reshape(1, 1, 8), (128, NT, 8))).astype(np.float32)
    return d


def rope_host():
    rows = 2048 // 64
    r = np.repeat(np.arange(rows), 64); col = np.tile(np.arange(64), rows)
    inv = (10000.0 ** (-np.arange(8, dtype=np.float32) / 8)).astype(np.float32)
    ang = np.concatenate([r[:, None] * inv, col[:, None] * inv], -1).astype(np.float32)
    cs, sn = np.cos(ang), np.sin(ang)
    tab = np.concatenate([cs, sn, -sn], -1).astype(np.float32)
    return np.ascontiguousarray(tab.reshape(16, 128, 48).transpose(1, 0, 2))


def prep_layer_inputs_cd(inp, l):
    d = {}
    d[f"wqb{l}"] = np.ascontiguousarray(inp["mla_w_qb"][l][:, mla_perm()])
    d[f"gq{l}"] = np.ascontiguousarray(inp["mla_q_norm_g"][l].reshape(192, 1))
    d[f"wkvb{l}"] = inp["mla_w_kvb"][l]
    d[f"gkv{l}"] = np.ascontiguousarray(inp["mla_kv_norm_g"][l].reshape(128, 1))
    d[f"w_out{l}"] = inp["w_out"][l]
    d[f"gng{l}"] = np.ascontiguousarray(np.broadcast_to(inp["gdn_norm_g"][l].reshape(1, 1, 128), (128, 4, 128))).astype(np.float32)
    d[f"cconv{l}"] = np.ascontiguousarray(inp["conv_w"][l].T.reshape(2, 128, 3).transpose(1, 0, 2))
    return d


KSTOP = [99]
DBG = [False]


def build_program():
    nc = bass.Bass("TRN2", target_bir_lowering=False)
    P = Prog(nc)

    def din(name, shape, dt=F32):
        return nc.dram_tensor(name, list(shape), dt, kind="ExternalInput").ap()

    def dint(name, shape, dt=F32):
        return nc.dram_tensor(name, list(shape), dt, kind="ExternalOutput" if DBG[0] else "Internal").ap()
    cin = {"ident": din("ident", [128, 128]), "sel": din("sel", [2, 256])}
    gmask = din("gmask", [128, 9, 128]); rope = din("rope", [128, 16, 48]); fng = din("fng", [128, 1024])
    xin0 = din("xin0", [T, 1024]); cc = din("cc", [128, 8, 2])
    out = nc.dram_tensor("out", [2048, 1024], F32, kind="ExternalOutput").ap()
    with ExitStack() as st:
        P.init_bar(st)
        C = load_consts(P, st, cin)
        xin = xin0; b_xin = P.buf()
        for l in range(2):
            WA = {"cc": cc, "w_ada": din(f"w_ada{l}", [1024, 3072]), "bada": din(f"bada{l}", [128, 24, 2]), "ng": din(f"ng{l}", [128, 8, 2]),
                  "w_in": din(f"w_in{l}", [1024, 3696])}
            io = {"PF": dint(f"PF{l}", [2048, TP], BF16), "b_PF": P.mbuf(), "PT": dint(f"PT{l}", [T, PT_W]), "b_PT": P.mbuf(),
                  "modo": dint(f"modo{l}", [128, 24, 2]), "b_modo": P.buf(), "OA": dint(f"OA{l}", [T, 1024]), "b_OA": P.mbuf(),
                  "OB": dint(f"OB{l}", [T, 256]), "b_OB": P.mbuf(), "xin": xin, "b_xin": b_xin,
                  "xout": dint(f"xmid{l}", [T, 1024]), "b_xout": P.mbuf(), "out": out, "b_out": P.mbuf()}
            if KSTOP[0] >= 4 * l + 1:
                stage_A(P, C, l, xin, WA, io)
            if KSTOP[0] >= 4 * l + 2:
              stage_B(P, C, l, {"gconv": din(f"gconv{l}", [128, 12, 5]), "alog": din(f"alog{l}", [128, NT, 8]), "dtb": din(f"dtb{l}", [128, NT, 8]),
                              "gmask": gmask}, io)
            if KSTOP[0] >= 4 * l + 3:
              stage_C(P, C, l, {"wqb": din(f"wqb{l}", [192, 384]), "gq": din(f"gq{l}", [192, 1]), "wkvb": din(f"wkvb{l}", [128, 512]),
                              "gkv": din(f"gkv{l}", [128, 1]), "rope": rope}, io, do_ctx=(l == 0))
            if KSTOP[0] >= 4 * l + 4:
              stage_D(P, C, l, {"w_out": din(f"w_out{l}", [1024, 1024]), "gng": din(f"gng{l}", [128, 4, 128]), "cconv": din(f"cconv{l}", [128, 2, 3]),
                              "fng": fng}, io, last=(l == 1))
            xin = io["xout"]; b_xin = io["b_xout"]
        P.emit()
    return nc


def make_in_maps(inp, cores):
    shared = {**prep_consts(), **gdn_consts_host(), "rope": rope_host(),
              "fng": np.ascontiguousarray(np.broadcast_to(inp["final_norm_g"].reshape(1, 1024), (128, 1024))).astype(np.float32)}
    for l in range(2):
        shared.update(prep_layer_inputs_b(inp, l)); shared.update(prep_layer_inputs_cd(inp, l))
    maps = []
    for b in cores:
        m = dict(shared)
        for l in range(2):
            m.update(prep_layer_inputs(inp, b, l))
        m["xin0"] = np.ascontiguousarray(np.concatenate([inp["ctx"][b], inp["x"][b]], 0))
        maps.append(m)
    return maps


def kernel(**inputs):
    inp = {k: np.asarray(v, dtype=np.float32) for k, v in inputs.items()}
    nc = build_program()
    maps = make_in_maps(inp, list(range(8)))
    res = run_bass_kernel_spmd(nc, maps, core_ids=list(range(8)))
    return np.stack([np.asarray(r["out"], dtype=np.float32) for r in res.results], 0)
```
